# Optimizing a Trainium2 kernel written in Bass

```python
import jax, jax.numpy as jnp
from jax import lax
import numpy as np

D_MODEL = 2048
BATCH = 16
SEQ = 256
DEPTH = 1
DEC_BATCH = 2
DEC_SEQ = 2048
PAST_LEN = 512

GRID_W = 64
MIX_WIDTH = D_MODEL
POOL_WIDTH = MIX_WIDTH // 2
POOL_WINDOWS = (2, 4, 8, 16)
POOL_GROUPS = len(POOL_WINDOWS)
POOL_GROUP_DIM = POOL_WIDTH // POOL_GROUPS
N_HEADS = 8
QK_NOPE_DIM = 128
QK_ROPE_DIM = 64
QK_DIM = QK_NOPE_DIM + QK_ROPE_DIM
V_DIM = 128
ATTN_WIDTH = N_HEADS * V_DIM
Q_LORA = 512
KV_LORA = 256
IN_WIDTH = POOL_WIDTH + Q_LORA + KV_LORA + QK_ROPE_DIM
D_FF = 5632
CONV_W = 3
N_MOD = 6
ROPE_THETA = 10000.0
EPS = 1e-6
Q_BLOCK = 128
ATTN_SCALE = QK_DIM ** -0.5

kernel_name = "hybrid_pool_mla_prefix_diffusion_step"


def rmsnorm(x, g):
    xf = x.astype(jnp.float32)
    xf = xf * lax.rsqrt(jnp.mean(xf * xf, axis=-1, keepdims=True) + EPS)
    return (xf * g.astype(jnp.float32)).astype(x.dtype)


def modulation(cond, w_mod, b_mod):
    m = jax.nn.silu(cond) @ w_mod + b_mod
    m = m.reshape(m.shape[0], 1, N_MOD, D_MODEL)
    return (m[:, :, 0], m[:, :, 1], m[:, :, 2], m[:, :, 3], m[:, :, 4], m[:, :, 5])


def axial_rope_tables(n_tokens, dtype):
    rows = n_tokens // GRID_W
    row = jnp.repeat(jnp.arange(rows), GRID_W).astype(jnp.float32)
    col = jnp.tile(jnp.arange(GRID_W), rows).astype(jnp.float32)
    n_freq = QK_ROPE_DIM // 4
    inv = ROPE_THETA ** (-jnp.arange(n_freq, dtype=jnp.float32) / n_freq)
    ar = row[:, None] * inv
    ac = col[:, None] * inv
    ang = jnp.concatenate([ar, ar, ac, ac], axis=-1)
    return jnp.cos(ang).astype(dtype), jnp.sin(ang).astype(dtype)


def rope_heads(x, cos, sin):
    nope, pe = x[..., :QK_NOPE_DIM], x[..., QK_NOPE_DIM:]
    a1, a2, b1, b2 = jnp.split(pe, 4, axis=-1)
    rot = jnp.concatenate([-a2, a1, -b2, b1], axis=-1)
    pe = pe * cos[:, None, :] + rot * sin[:, None, :]
    return jnp.concatenate([nope, pe], axis=-1)


def multi_scale_pool(u, w_pool, pool_scale):
    B, T, _ = u.shape
    cs = jnp.pad(jnp.cumsum(u.astype(jnp.float32), axis=1), ((0, 0), (1, 0), (0, 0)))
    t = jnp.arange(T)
    outs = []
    for g, w in enumerate(POOL_WINDOWS):
        lo = jnp.clip(t - w // 2, 0, T - 1)
        hi = jnp.clip(t - w // 2 + w - 1, 0, T - 1)
        sl = slice(g * POOL_GROUP_DIM, (g + 1) * POOL_GROUP_DIM)
        cg = cs[:, :, sl]
        s = jnp.take(cg, hi + 1, axis=1) - jnp.take(cg, lo, axis=1)
        cnt = (hi - lo + 1).astype(jnp.float32)[None, :, None]
        d = (s / cnt - u[:, :, sl].astype(jnp.float32)).astype(u.dtype)
        outs.append(d @ w_pool[g])
    return jnp.concatenate(outs, axis=-1) * pool_scale


def mla_keys_values(ckv_n, kpe, w_ukv, k_norm_g):
    B, S, _ = ckv_n.shape
    kv = (ckv_n @ w_ukv).reshape(B, S, N_HEADS, QK_NOPE_DIM + V_DIM)
    k_nope, v = kv[..., :QK_NOPE_DIM], kv[..., QK_NOPE_DIM:]
    k_pe = jnp.broadcast_to(kpe[:, :, None, :], (B, S, N_HEADS, QK_ROPE_DIM))
    k = rmsnorm(jnp.concatenate([k_nope, k_pe], axis=-1), k_norm_g)
    return k, v


def block_attention(q, k, v):
    B, T, H, _ = q.shape
    nb = T // Q_BLOCK
    qb = q.reshape(B, nb, Q_BLOCK, H, QK_DIM).transpose(1, 0, 2, 3, 4)

    def one_block(qi):
        s = jnp.einsum('bqhd,bshd->bhqs', qi, k).astype(jnp.float32) * ATTN_SCALE
        p = jax.nn.softmax(s, axis=-1).astype(v.dtype)
        return jnp.einsum('bhqs,bshd->bqhd', p, v)

    o = lax.map(one_block, qb)
    return o.transpose(1, 0, 2, 3, 4).reshape(B, T, H * V_DIM)


def conv_ffn(h, w_up, conv_w, conv_b, w_down):
    up = h @ w_up
    p = jnp.pad(up, ((0, 0), (1, 1), (0, 0)))
    z = p[:, :-2] * conv_w[0] + p[:, 1:-1] * conv_w[1] + p[:, 2:] * conv_w[2] + conv_b
    a, b = z[..., :D_FF], z[..., D_FF:]
    return (jax.nn.silu(a) * b) @ w_down


def trunk_layer(x, mod, ctx_ckv, ctx_kpe, rope, lp):
    shift1, scale1, gate1, shift2, scale2, gate2 = mod
    B, T, _ = x.shape
    h = rmsnorm(x, lp['norm1_g']) * (1 + scale1) + shift1
    proj = h @ lp['w_in']
    o1 = POOL_WIDTH
    o2 = o1 + Q_LORA
    o3 = o2 + KV_LORA
    u, cq, ckv, kpe = proj[..., :o1], proj[..., o1:o2], proj[..., o2:o3], proj[..., o3:]
    pool_out = multi_scale_pool(u, lp['w_pool'], lp['pool_scale'])
    ckv_n = rmsnorm(ckv, lp['kv_a_g'])
    q = (rmsnorm(cq, lp['q_a_g']) @ lp['w_uq']).reshape(B, T, N_HEADS, QK_DIM)
    q = rmsnorm(q, lp['q_norm_g'])
    k, v = mla_keys_values(ckv_n, kpe, lp['w_ukv'], lp['k_norm_g'])
    if rope is not None:
        cos, sin = rope
        q = rope_heads(q, cos, sin)
        k = rope_heads(k, cos, sin)
    if ctx_ckv is not None:
        ck, cv = mla_keys_values(ctx_ckv, ctx_kpe, lp['w_ukv'], lp['k_norm_g'])
        k = jnp.concatenate([ck, k], axis=1)
        v = jnp.concatenate([cv, v], axis=1)
    att = block_attention(q, k, v)
    mixed = jnp.concatenate([pool_out, att], axis=-1) @ lp['w_out']
    x = x + gate1 * mixed
    h2 = rmsnorm(x, lp['norm2_g']) * (1 + scale2) + shift2
    x = x + gate2 * conv_ffn(h2, lp['w_up'], lp['conv_w'], lp['conv_b'], lp['w_down'])
    return x, ckv_n, kpe


def setup_inputs(seed: int = 0) -> dict:
    key = jax.random.key(seed)
    ks = jax.random.split(key, 32)
    f = jnp.float32

    def nrm(k, shape, scale):
        return jax.random.normal(k, shape, f) * scale

    L = DEPTH
    return {
        'x_prompt': nrm(ks[0], (BATCH, SEQ, D_MODEL), 1.0),
        'x_sample': nrm(ks[1], (DEC_BATCH, DEC_SEQ, D_MODEL), 1.0),
        'cache_ckv': nrm(ks[2], (DEC_BATCH, DEPTH, PAST_LEN, KV_LORA), 1.0),
        'cache_kpe': nrm(ks[3], (DEC_BATCH, DEPTH, PAST_LEN, QK_ROPE_DIM), 1.0),
        'c': nrm(ks[4], (DEC_BATCH, D_MODEL), 1.0),
        'c_ctx': nrm(ks[5], (D_MODEL,), 1.0),
        'norm1_g': 1.0 + nrm(ks[6], (L, D_MODEL), 0.05),
        'norm2_g': 1.0 + nrm(ks[7], (L, D_MODEL), 0.05),
        'w_mod': nrm(ks[8], (L, D_MODEL, N_MOD * D_MODEL), 0.5 * D_MODEL ** -0.5),
        'b_mod': nrm(ks[9], (L, N_MOD * D_MODEL), 0.02),
        'w_in': nrm(ks[10], (L, D_MODEL, IN_WIDTH), D_MODEL ** -0.5),
        'w_pool': nrm(ks[11], (L, POOL_GROUPS, POOL_GROUP_DIM, POOL_GROUP_DIM), POOL_GROUP_DIM ** -0.5),
        'pool_scale': 1.0 + nrm(ks[12], (L, POOL_WIDTH), 0.1),
        'q_a_g': 1.0 + nrm(ks[13], (L, Q_LORA), 0.05),
        'w_uq': nrm(ks[14], (L, Q_LORA, N_HEADS * QK_DIM), Q_LORA ** -0.5),
        'kv_a_g': 1.0 + nrm(ks[15], (L, KV_LORA), 0.05),
        'w_ukv': nrm(ks[16], (L, KV_LORA, N_HEADS * (QK_NOPE_DIM + V_DIM)), KV_LORA ** -0.5),
        'q_norm_g': 1.0 + nrm(ks[17], (L, QK_DIM), 0.05),
        'k_norm_g': 1.0 + nrm(ks[18], (L, QK_DIM), 0.05),
        'w_out': nrm(ks[19], (L, MIX_WIDTH, D_MODEL), MIX_WIDTH ** -0.5),
        'w_up': nrm(ks[20], (L, D_MODEL, 2 * D_FF), D_MODEL ** -0.5),
        'conv_w': nrm(ks[21], (L, CONV_W, 2 * D_FF), 0.3) + jnp.array([0.0, 1.0, 0.0], f)[None, :, None],
        'conv_b': nrm(ks[22], (L, 2 * D_FF), 0.02),
        'w_down': nrm(ks[23], (L, D_FF, D_MODEL), D_FF ** -0.5),
    }


def reference(x_prompt, x_sample, cache_ckv, cache_kpe, c, c_ctx, norm1_g, norm2_g,
              w_mod, b_mod, w_in, w_pool, pool_scale, q_a_g, w_uq, kv_a_g, w_ukv,
              q_norm_g, k_norm_g, w_out, w_up, conv_w, conv_b, w_down):
    rope = axial_rope_tables(x_sample.shape[1], x_sample.dtype)
    xp = x_prompt
    xs = x_sample
    ckv_list = []
    kpe_list = []
    for l in range(DEPTH):
        lp = {
            'norm1_g': norm1_g[l], 'norm2_g': norm2_g[l], 'w_in': w_in[l],
            'w_pool': w_pool[l], 'pool_scale': pool_scale[l], 'q_a_g': q_a_g[l],
            'w_uq': w_uq[l], 'kv_a_g': kv_a_g[l], 'w_ukv': w_ukv[l],
            'q_norm_g': q_norm_g[l], 'k_norm_g': k_norm_g[l], 'w_out': w_out[l],
            'w_up': w_up[l], 'conv_w': conv_w[l], 'conv_b': conv_b[l], 'w_down': w_down[l],
        }
        mod_ctx = modulation(c_ctx[None, :], w_mod[l], b_mod[l])
        xp, ckv_n, kpe = trunk_layer(xp, mod_ctx, None, None, None, lp)
        ckv_list.append(ckv_n)
        kpe_list.append(kpe)
        mod_lat = modulation(c, w_mod[l], b_mod[l])
        xs, _, _ = trunk_layer(xs, mod_lat, cache_ckv[:, l], cache_kpe[:, l], rope, lp)
    new_ckv = jnp.stack(ckv_list, axis=1)
    new_kpe = jnp.stack(kpe_list, axis=1)
    return (xp, xs, new_ckv, new_kpe)
```

```python
import numpy as np
import ml_dtypes
import concourse.bass as bass
import concourse.mybir as mybir
from concourse.bass_utils import run_bass_kernel_spmd
from contextlib import ExitStack

F32 = mybir.dt.float32
BF16 = mybir.dt.bfloat16
AF = mybir.ActivationFunctionType
ALU = mybir.AluOpType

STREAMS = ['pe', 'act', 'dve', 'pool', 'sp']

D = 2048
NKC = 16
DFF = 5632
NJ = 44
EPS = 1e-6
ATTN_SCALE = 192 ** -0.5
NP_ = 512
NS = 514
NH = 17
NALL = NP_ + NS
NKEY_S = 2560
NK_TOT = 512 + NKEY_S
PERM = np.concatenate([np.arange(16, 32), np.arange(0, 16), np.arange(48, 64), np.arange(32, 48)])
SGN = np.concatenate([-np.ones(16), np.ones(16), -np.ones(16), np.ones(16)]).astype(np.float32)
POOL_W = (2, 4, 8, 16)

VC_G1 = 0
VC_G2 = 16
VC_PSC = 32
VC_QAG = 40
VC_CW = 44
VC_CB = VC_CW + 264
VC_QGN = VC_CB + 88
VC_KGN = VC_QGN + 1
VC_QGP = VC_KGN + 1
VC_QGPP = VC_QGP + 1
VC_ML = VC_QGPP + 1
VC_MR = VC_ML + 1
VC_EPS = VC_MR + 1
VC_ONE = VC_EPS + 1
NV = VC_ONE + 1
RW_KVG = 0
RW_KGP = 256
RW_RCP = 320
RW_RCS = RW_RCP + 4 * 256
NR = RW_RCS + 4 * NS


class V:
    __slots__ = ('buf', 'key', 'ap')

    def __init__(self, buf, key, ap):
        self.buf = buf
        self.key = key
        self.ap = ap

    def __getitem__(self, idx):
        return V(self.buf, self.key, self.ap[idx])

    def bc(self, shape):
        return V(self.buf, self.key, self.ap.to_broadcast(list(shape)))

    def re(self, s, **kw):
        return V(self.buf, self.key, self.ap.rearrange(s, **kw))


class Buf:
    def __init__(self, ap, name):
        self.ap = ap
        self.name = name
        self.wr = {}
        self.rd = {}

    def k(self, key=None):
        return V(self, key, self.ap)

    def __getitem__(self, idx):
        return V(self, None, self.ap[idx])


class Op:
    __slots__ = ('stream', 'fn', 'deps', 'dma', 'sem', 'val', 'needed', 'idx', 'nm')


class Prog:
    def __init__(self, nc, n_dma_sems=(40, 40)):
        self.nc = nc
        self.ops = []
        self.es = ExitStack()
        self.csem = {}
        for s in ['pe', 'act', 'dve', 'pool']:
            self.csem[s] = self.es.enter_context(nc.semaphore('c_' + s))
        self.dsem = {'sp': [], 'pool': []}
        for i in range(n_dma_sems[0]):
            self.dsem['sp'].append(self.es.enter_context(nc.semaphore('d_sp%d' % i)))
        for i in range(n_dma_sems[1]):
            self.dsem['pool'].append(self.es.enter_context(nc.semaphore('d_pl%d' % i)))
        self.dcnt = {'sp': 0, 'pool': 0}
        self.dlast = {}
        self.duse = {}
        self.last = {}
        self.dma_since = []

    def sbuf(self, name, shape, dtype):
        t = self.es.enter_context(self.nc.sbuf_tensor(name, list(shape), dtype))
        return Buf(t[:], name)

    def psum(self, name, shape, dtype):
        t = self.es.enter_context(self.nc.psum_tensor(name, list(shape), dtype))
        b = Buf(t[:], name)
        b.excl = True
        return b

    def _deps_read(self, v, deps):
        b = v.buf
        if v.key is None:
            for op in b.wr.values():
                deps.add(op)
        else:
            for k in (None, v.key):
                op = b.wr.get(k)
                if op is not None:
                    deps.add(op)

    def _deps_write(self, v, deps):
        b = v.buf
        if v.key is None:
            for op in b.wr.values():
                deps.add(op)
            for d in b.rd.values():
                for op in d.values():
                    deps.add(op)
        else:
            for k in (None, v.key):
                op = b.wr.get(k)
                if op is not None:
                    deps.add(op)
                d = b.rd.get(k)
                if d:
                    for op in d.values():
                        deps.add(op)

    def add(self, stream, fn, reads=(), writes=(), dma=False, nm='', extra_deps=()):
        op = Op()
        op.stream = stream
        op.fn = fn
        op.dma = dma
        op.needed = False
        op.idx = len(self.ops)
        op.sem = None
        op.val = None
        op.nm = nm
        deps = set(extra_deps)
        for v in reads:
            self._deps_read(v, deps)
            if getattr(v.buf, 'excl', False):
                for d in v.buf.rd.values():
                    for st, o in d.items():
                        if st != stream:
                            deps.add(o)
        for v in writes:
            self._deps_write(v, deps)
        if dma:
            q = stream
            i = self.dcnt[q] % len(self.dsem[q])
            self.dcnt[q] += 1
            prev = self.dlast.get((q, i))
            if prev is not None:
                deps.add(prev)
            self.dlast[(q, i)] = op
            n = self.duse.get((q, i), 0) + 1
            self.duse[(q, i)] = n
            op.sem = self.dsem[q][i]
            op.val = 16 * n
            op.needed = True
            self.dma_since.append(op)
        if stream == 'pe' and not dma:
            deps = {d for d in deps if d.dma or d.stream != 'pe'}
        for d in deps:
            d.needed = True
        op.deps = deps
        for v in reads:
            v.buf.rd.setdefault(v.key, {})[stream if not dma else ('dma', op.idx)] = op
        for v in writes:
            b = v.buf
            if v.key is None:
                b.wr = {None: op}
                b.rd = {}
            else:
                b.wr[v.key] = op
                b.rd[v.key] = {}
        self.ops.append(op)
        if not dma:
            self.last[stream] = op
        return op

    def barrier(self):
        lasts = [o for o in self.last.values()]
        dm = list(self.dma_since)
        self.dma_since = []
        for s in STREAMS:
            deps = list(lasts) + dm
            self.add(s, None, extra_deps=deps, nm='barrier')

    def emit(self):
        nc = self.nc
        cnt = {s: 0 for s in self.csem}
        for op in self.ops:
            if not op.dma and op.needed and op.fn is not None:
                cnt[op.stream] += 1
                op.sem = self.csem[op.stream]
                op.val = cnt[op.stream]
        per = {s: [o for o in self.ops if o.stream == s] for s in STREAMS}
        final_dma = []
        for (q, i), n in self.duse.items():
            final_dma.append((self.dsem[q][i], 16 * n))

        def run(stream, eng):
            seen = {}
            for op in per[stream]:
                waits = {}
                for d in op.deps:
                    if d.sem is None:
                        continue
                    key = id(d.sem)
                    if seen.get(key, 0) >= d.val:
                        continue
                    if key not in waits or waits[key][1] < d.val:
                        waits[key] = (d.sem, d.val)
                for key, (sem, val) in waits.items():
                    eng.wait_ge(sem, val)
                    seen[key] = val
                if op.fn is None:
                    continue
                ins = op.fn(eng)
                if op.dma:
                    ins.then_inc(op.sem, 16)
                elif op.needed:
                    ins.then_inc(op.sem, 1)
            if stream == 'sp':
                for sem, val in final_dma:
                    if seen.get(id(sem), 0) < val:
                        eng.wait_ge(sem, val)

        with nc.Block() as block:
            @block.sync
            def _(e):
                run('sp', e)

            @block.gpsimd
            def _(e):
                run('pool', e)

            @block.tensor
            def _(e):
                run('pe', e)

            @block.scalar
            def _(e):
                run('act', e)

            @block.vector
            def _(e):
                run('dve', e)
        self.es.close()

    def dma(self, out, in_, q='sp', nm='dma', **kw):
        return self.add(q, lambda e: e.dma_start(out=out.ap, in_=in_.ap, **kw),
                        reads=[in_], writes=[out], dma=True, nm=nm)

    def mm(self, out, lhsT, rhs, start=True, stop=True, nm='mm'):
        return self.add('pe', lambda e: e.matmul(out.ap, lhsT.ap, rhs.ap, start=start, stop=stop),
                        reads=[lhsT, rhs], writes=[out], nm=nm)

    def transpose(self, out, in_, ident, nm='tr'):
        return self.add('pe', lambda e: e.transpose(out.ap, in_.ap, ident.ap),
                        reads=[in_, ident], writes=[out], nm=nm)

    def act(self, out, in_, func, bias=None, scale=None, accum_out=None, nm='act'):
        reads = [in_]
        kw = {}
        if bias is not None:
            if isinstance(bias, V):
                reads.append(bias)
                kw['bias'] = bias.ap
            else:
                kw['bias'] = bias
        if scale is not None:
            if isinstance(scale, V):
                reads.append(scale)
                kw['scale'] = scale.ap
            else:
                kw['scale'] = scale
        writes = [out]
        if accum_out is not None:
            writes.append(accum_out)
            kw['accum_out'] = accum_out.ap
        return self.add('act', lambda e: e.activation(out.ap, in_.ap, func, **kw),
                        reads=reads, writes=writes, nm=nm)

    def tt(self, out, in0, in1, op, eng='dve', nm='tt'):
        return self.add(eng, lambda e: e.tensor_tensor(out.ap, in0.ap, in1.ap, op),
                        reads=[in0, in1], writes=[out], nm=nm)

    def ts(self, out, in0, s1, op0, s2=None, op1=None, eng='dve', nm='ts'):
        reads = [in0]
        a1 = s1
        if isinstance(s1, V):
            reads.append(s1)
            a1 = s1.ap
        a2 = s2
        if isinstance(s2, V):
            reads.append(s2)
            a2 = s2.ap
        if op1 is None:
            return self.add(eng, lambda e: e.tensor_scalar(out.ap, in0.ap, a1, None, op0),
                            reads=reads, writes=[out], nm=nm)
        return self.add(eng, lambda e: e.tensor_scalar(out.ap, in0.ap, a1, a2, op0, op1),
                        reads=reads, writes=[out], nm=nm)

    def stt(self, out, in0, scalar, in1, op0, op1, nm='stt'):
        reads = [in0, in1]
        sc = scalar
        if isinstance(scalar, V):
            reads.append(scalar)
            sc = scalar.ap
        return self.add('dve', lambda e: e.scalar_tensor_tensor(out.ap, in0.ap, sc, in1.ap, op0, op1),
                        reads=reads, writes=[out], nm=nm)

    def copy(self, out, in_, eng='dve', nm='cp'):
        if eng == 'act':
            return self.add('act', lambda e: e.copy(out.ap, in_.ap), reads=[in_], writes=[out], nm=nm)
        return self.add(eng, lambda e: e.tensor_copy(out.ap, in_.ap), reads=[in_], writes=[out], nm=nm)

    def memset(self, out, val, eng='dve', nm='ms'):
        return self.add(eng, lambda e: e.memset(out.ap, val), reads=[], writes=[out], nm=nm)


class Arena:
    def __init__(self, P, name, nbytes):
        self.P = P
        self.nb = nbytes
        self.t = P.es.enter_context(P.nc.sbuf_tensor(name, [128, nbytes // 2], BF16))
        self.cur = 0
        self.hi = 0

    def reset(self, to=0):
        self.cur = to

    def alloc(self, name, shape, dtype):
        esz = 4 if dtype == F32 else 2
        n = 1
        for s in shape[1:]:
            n *= s
        nbytes = (n * esz + 31) // 32 * 32
        off = self.cur
        assert off + nbytes <= self.nb, "arena overflow %s: %d + %d > %d" % (name, off, nbytes, self.nb)
        self.cur += nbytes
        self.hi = max(self.hi, self.cur)
        ap = self.t[0:shape[0], off // 2:(off + n * esz) // 2]
        if dtype == F32:
            ap = ap.bitcast(F32)
        if len(shape) == 3:
            ap = ap.rearrange("p (a b) -> p a b", b=shape[2])
        elif len(shape) == 4:
            ap = ap.rearrange("p (a b c) -> p a b c", b=shape[2], c=shape[3])
        return Buf(ap, name)


def build(stop_after=None):
    nc = bass.Bass("TRN2", target_bir_lowering=False)

    def din(name, shape, dt=F32):
        return nc.dram_tensor(name, list(shape), dt, kind="ExternalInput")

    def dout(name, shape):
        return nc.dram_tensor(name, list(shape), F32, kind="ExternalOutput")

    xp_d = din("xp", [512, D])
    xo_d = din("xo", [512, D])
    xs_d = din("xs", [2048, D])
    xh_d = din("xh", [NH, D])
    cckv_d = din("cckv", [512, 256])
    ckpe_d = din("ckpe", [512, 64])
    cT_d = din("cT", [128, 32])
    wmod_d = din("w_mod", [D, 6 * D])
    bmod_d = din("b_mod2", [2, 6 * D])
    winuq_d = din("w_in_uq", [D, 1536])
    winkv_d = din("w_in_kv", [D, 320])
    wpool_d = din("w_pool", [1024, 256])
    wuq_d = din("w_uq_x", [512, 2048])
    wukv_d = din("w_ukv_x", [256, 2048])
    wout_d = din("w_out", [D, D])
    wup_d = din("w_up_x", [D, 2 * DFF])
    wdn_d = din("w_down", [DFF, D])
    vecs_d = din("vecs", [128, NV])
    rows_d = din("rows", [1, NR])
    ropek_d = din("ropek", [2048, 128])
    ropeq_d = din("ropeq", [64, 2 * NS])
    ident_d = din("ident", [128, 128], BF16)
    identf_d = din("identf", [128, 128])
    yp_d = dout("yp", [512, D])
    ys_d = dout("ys", [512, D])
    nckv_d = dout("nckv", [512, 256])
    nkpe_d = dout("nkpe", [512, 64])

    P = Prog(nc, n_dma_sems=(44, 44))

    def DB(t, name):
        return Buf(t[:], name)

    XP, XO, XS, XH = DB(xp_d, 'xp'), DB(xo_d, 'xo'), DB(xs_d, 'xs'), DB(xh_d, 'xh')
    CCKV, CKPE = DB(cckv_d, 'cckv'), DB(ckpe_d, 'ckpe')
    WMOD, BMOD = DB(wmod_d, 'wmod'), DB(bmod_d, 'bmod')
    WINUQ, WINKV, WPOOL = DB(winuq_d, 'winuq'), DB(winkv_d, 'winkv'), DB(wpool_d, 'wpool')
    WUQ, WUKV, WOUT, WUP, WDN = DB(wuq_d, 'wuq'), DB(wukv_d, 'wukv'), DB(wout_d, 'wout'), DB(wup_d, 'wup'), DB(wdn_d, 'wdn')
    ROPEK, ROPEQ = DB(ropek_d, 'ropek'), DB(ropeq_d, 'ropeq')
    YP, YS, NCKV, NKPE = DB(yp_d, 'yp'), DB(ys_d, 'ys'), DB(nckv_d, 'nckv'), DB(nkpe_d, 'nkpe')
    ROWS = DB(rows_d, 'rows')

    AR = Arena(P, "arena", 206 * 1024)
    PSB = [P.psum("psb%d" % i, [128, 512], F32) for i in range(8)]

    def psv(i, key=None):
        return V(PSB[i], key, PSB[i].ap)

    def psv16(i, key=None):
        return V(PSB[i], key, PSB[i].ap.bitcast(BF16))

    vecs = AR.alloc("vecs", [128, NV], F32)
    ident = AR.alloc("ident", [128, 128], BF16)
    identf = AR.alloc("identf", [128, 128], F32)
    onesb = AR.alloc("onesb", [128, 128], BF16)
    onesf = AR.alloc("onesf", [128, 128], F32)
    cTf = AR.alloc("cTf", [128, 32], F32)
    cTb = AR.alloc("cTb", [128, 16, 2], BF16)
    MODF = AR.alloc("MODF", [128, 96, 2], F32)
    SS = AR.alloc("SS", [128, 2, 16, 2], F32)
    kvg_b = AR.alloc("kvg_b", [128, 256], F32)
    kgp_b = AR.alloc("kgp_b", [128, 64], F32)
    stat = [AR.alloc("stat%d" % i, [128, 8], F32) for i in range(4)]
    mark_mix = AR.cur
    mixT = AR.alloc("mixT", [128, 16, NALL], BF16)
    mark_cqn = AR.cur
    cqn = AR.alloc("cqn", [128, 4, NALL], BF16)
    mark_a2 = AR.cur
    CKVT = AR.alloc("CKVT", [128, 2, NK_TOT], BF16)
    KPT = AR.alloc("KPT", [64, NK_TOT], F32)
    KPSQ = AR.alloc("KPSQ", [128, NK_TOT], BF16)
    mark_arena = AR.cur

    def vcol(c, n=1, parts=128):
        return vecs.k()[0:parts, c:c + n]

    eps_v = vcol(VC_EPS)

    P.dma(vecs.k(), V(DB(vecs_d, 'vecs_d'), None, vecs_d[:]))
    P.dma(ident.k(), V(DB(ident_d, 'ident_d'), None, ident_d[:]))
    P.dma(identf.k(), V(DB(identf_d, 'identf_d'), None, identf_d[:]))
    P.dma(cTf.k(), V(DB(cT_d, 'cT_d'), None, cT_d[:]))
    P.dma(kvg_b.k(), V(ROWS, None, rows_d[:, RW_KVG:RW_KVG + 256].partition_broadcast(128)))
    P.dma(kgp_b.k(), V(ROWS, None, rows_d[:, RW_KGP:RW_KGP + 64].partition_broadcast(128)))
    P.memset(onesb.k(), 1.0)
    P.memset(KPSQ.k()[64:128, :], 0.0)
    P.memset(onesf.k(), 1.0, eng='pool')
    P.act(cTb.k().re("p a b -> p (a b)"), cTf.k(), AF.Silu)

    def mod_group(col0, width, wblk, b2, mrow, psA, psB_, skip_wdma=False):
        nblk = width // 128
        if not skip_wdma:
            P.dma(wblk.k()[:, :, 0:width], V(WMOD, None, wmod_d.rearrange("(kc p) n -> p kc n", p=128)[:, :, col0:col0 + width]),
                  q='pool', nm='wmod')
        P.dma(b2.k()[:, 0:width], V(BMOD, None, bmod_d[:, col0:col0 + width]))
        for kc in range(NKC):
            P.mm(psv(psA)[0:2, 0:width], cTb.k()[:, kc, :], wblk.k()[:, kc, 0:width], start=(kc == 0), stop=(kc == NKC - 1))
        P.tt(mrow.k()[:, 0:width], psv(psA)[0:2, 0:width], b2.k()[:, 0:width], ALU.add)
        for j in range(nblk):
            P.mm(psv(psB_)[:, 2 * j:2 * j + 2], mrow.k()[0:2, j * 128:(j + 1) * 128], identf.k()[0:2, 0:2])
        b0 = col0 // 128
        P.copy(MODF.k()[:, b0:b0 + nblk, :].re("p a b -> p (a b)"), psv(psB_)[:, 0:2 * nblk], eng='dve')

    def mod_finish(which):
        sc0 = 16 if which == 0 else 64
        gcol = VC_G1 if which == 0 else VC_G2
        for c in range(2):
            P.stt(SS.k()[:, which, :, c], MODF.k()[:, sc0:sc0 + 16, c], 1.0, vcol(gcol, 16), ALU.add, ALU.mult)

    AR.reset(mark_arena)
    m_wblk = [AR.alloc("m_wblk%d" % i, [128, 16, 512], BF16) for i in range(3)]
    m_b2 = [AR.alloc("m_b2_%d" % i, [2, 512], F32) for i in range(2)]
    m_row = [AR.alloc("m_row%d" % i, [2, 512], F32) for i in range(2)]
    mark_a1 = AR.cur
    for cg in range(8):
        mod_group(cg * 512, 512, m_wblk[cg % 3], m_b2[cg % 2], m_row[cg % 2], 6, 7)
    mod_finish(0)

    ctr = {'n': 0}

    def norm_stats(x_v, ntok, xsb_list):
        i = ctr['n']
        ctr['n'] += 1
        xsb = xsb_list[i % len(xsb_list)]
        st = stat[i % 4]
        P.act(xsb.k()[0:ntok, :], x_v, AF.Square, accum_out=st.k()[0:ntok, 0:1])
        P.act(st.k()[0:ntok, 1:2], st.k()[0:ntok, 0:1], AF.Sqrt, bias=eps_v[0:ntok], scale=1.0 / D)
        P.add('dve', lambda e: e.reciprocal(st.ap[0:ntok, 2:3], st.ap[0:ntok, 1:2]),
              reads=[st.k()], writes=[st.k()], nm='rc')
        P.ts(xsb.k()[0:ntok, :], x_v, st.k()[0:ntok, 2:3], ALU.mult)
        return i

    def norm_T(x_v, ntok, which, c, dst_fn, xsb_list, psT, plain_dst=None):
        i = norm_stats(x_v, ntok, xsb_list)
        norm_trans(i, ntok, which, c, dst_fn, xsb_list, psT, plain_dst)

    def norm_trans(i, ntok, which, c, dst_fn, xsb_list, psT, plain_dst=None):
        xsb = xsb_list[i % len(xsb_list)]
        boff = 0 if which == 0 else 48
        for half in range(2):
            bank = psT[(2 * i + half) % len(psT)]
            for cc in range(8):
                kc = half * 8 + cc
                P.transpose(psv16(bank)[:, cc * 128:cc * 128 + ntok], xsb.k()[0:ntok, kc * 128:(kc + 1) * 128],
                            ident.k()[0:ntok, 0:ntok])
            if plain_dst is not None:
                src = psv16(bank)[:, 0:1024].re("p (a b) -> p a b", b=128)
                P.copy(plain_dst(half), src, eng=('act' if half == 0 else 'dve'))
                continue
            for cc in range(8):
                kc = half * 8 + cc
                if half == 0:
                    P.act(dst_fn(kc), psv16(bank)[:, cc * 128:cc * 128 + ntok], AF.Identity,
                          scale=SS.k()[:, which, kc, c:c + 1], bias=MODF.k()[:, boff + kc, c:c + 1])
                else:
                    P.ts(dst_fn(kc), psv16(bank)[:, cc * 128:cc * 128 + ntok], SS.k()[:, which, kc, c:c + 1], ALU.mult,
                         MODF.k()[:, boff + kc, c:c + 1], ALU.add)

    def phase_A1():
        AR.reset(mark_arena)
        w_kv = AR.alloc("w_kv", [128, 16, 320], BF16)
        a1_xt = [AR.alloc("a1_xt%d" % i, [128, D], F32) for i in range(4)]
        a1_xs = [AR.alloc("a1_xs%d" % i, [128, D], BF16) for i in range(3)]
        a1_hT = [AR.alloc("a1_hT%d" % i, [128, 16, 128], BF16) for i in range(3)]
        a1_kv = [AR.alloc("a1_kv%d" % i, [128, 320], F32) for i in range(2)]
        a1_ck = [AR.alloc("a1_ck%d" % i, [128, 256], F32) for i in range(4)]
        a1_ckb = [AR.alloc("a1_ckb%d" % i, [128, 256], BF16) for i in range(2)]
        a1_kp = [AR.alloc("a1_kp%d" % i, [128, 128], F32) for i in range(4)]
        a1_st = [AR.alloc("a1_st%d" % i, [128, 8], F32) for i in range(2)]
        a1_kg = [AR.alloc("a1_kg%d" % i, [128, 128], F32) for i in range(2)]
        a1_kr = [AR.alloc("a1_kr%d" % i, [128, 128], F32) for i in range(2)]
        for i in range(4):
            P.memset(a1_kp[i].k(), 0.0)
        for i in range(2):
            P.memset(a1_kg[i].k(), 0.0)
            P.memset(a1_kr[i].k(), 0.0)
        a1_tm = [AR.alloc("a1_tm%d" % i, [128, 64], F32) for i in range(2)]
        a1_rp = [AR.alloc("a1_rp%d" % i, [128, 128], F32) for i in range(2)]
        a1_jk = AR.alloc("a1_jk", [128, 256], BF16)
        P.dma(w_kv.k(), V(WINKV, None, winkv_d.rearrange("(kc p) n -> p kc n", p=128)), q='pool', nm='wkv')
        w_kvs = [AR.alloc("w_kvs%d" % i, [128, 16, 320], BF16) for i in range(2)]
        B1b = AR.alloc("B1b", [128, 16, 2], BF16)
        brow = AR.alloc("brow", [128, 2, 320], BF16)
        onerow = AR.alloc("onerow", [128, 128], BF16)
        P.memset(brow.k(), 0.0)
        P.memset(onerow.k(), 0.0)
        P.memset(onerow.k()[0:1, :], 1.0)
        P.copy(B1b.k(), MODF.k()[:, 0:16, :], eng='dve')
        for c_ in range(2):
            for kc in range(NKC):
                P.ts(w_kvs[c_].k()[:, kc, :], w_kv.k()[:, kc, :], SS.k()[:, 0, kc, c_:c_ + 1], ALU.mult)
            for kc in range(NKC):
                P.mm(psv(5)[0:1, 0:320], B1b.k()[:, kc, c_:c_ + 1], w_kv.k()[:, kc, :], start=(kc == 0), stop=(kc == NKC - 1))
            P.copy(brow.k()[0:1, c_, :], psv(5)[0:1, 0:320], eng='act')

        class _V64:
            def __init__(self, b):
                self.b = b

            def k(self):
                return self.b.k()[:, 0:64]

        a1_idx = {}

        def kv_stats(it, kind, src_v):
            if kind == 'x':
                xt = a1_xt[it % len(a1_xt)]
                P.dma(xt.k(), src_v, nm='ldx')
                a1_idx[it] = norm_stats(xt.k(), 128, a1_xs)
            else:
                P.dma(a1_ck[it % 4].k(), src_v[0], nm='ldc')
                P.dma(_V64(a1_kp[it % 4]).k(), src_v[1], nm='ldc')

        def kv_trans(it, kind, c):
            if kind == 'x':
                hT = a1_hT[it % 3]
                norm_trans(a1_idx[it], 128, 0, c, None, a1_xs, [0, 1, 2, 3],
                           plain_dst=lambda half: hT.k(half)[:, half * 8:(half + 1) * 8, :])

        def kv_back(it, kind, col0, rope_row0=None, out_row0=None, c=0):
            ck = a1_ck[it % 4]
            ckb = a1_ckb[it % 2]
            kp_full = a1_kp[it % 4]
            kg_full = a1_kg[it % 2]
            kp = _V64(kp_full)
            kg = _V64(kg_full)
            st = a1_st[it % 2]
            if kind == 'x':
                hT = a1_hT[it % 3]
                pk = 4 + (it % 2)
                for kc in range(NKC):
                    P.mm(psv(pk)[:, 0:320], hT.k()[:, kc, :], w_kvs[c].k()[:, kc, :], start=(kc == 0), stop=False)
                P.mm(psv(pk)[:, 0:320], onerow.k(), brow.k()[:, c, :], start=False, stop=True)
                kvb = a1_kv[it % 2]
                P.copy(kvb.k(), psv(pk)[:, 0:320], eng='dve')
                P.act(a1_jk.k(), kvb.k()[:, 0:256], AF.Square, accum_out=st.k()[:, 4:5])
                P.act(st.k()[:, 5:6], st.k()[:, 4:5], AF.Sqrt, bias=eps_v, scale=1.0 / 256)
                P.add('dve', lambda e: e.reciprocal(st.ap[:, 6:7], st.ap[:, 5:6]), reads=[st.k()], writes=[st.k()], nm='rc')
                P.stt(ck.k(), kvb.k()[:, 0:256], st.k()[:, 6:7], kvg_b.k(), ALU.mult, ALU.mult)
                P.copy(kp.k(), kvb.k()[:, 256:320], eng='pool')
                if out_row0 is not None:
                    P.dma(V(NCKV, out_row0, nckv_d[out_row0:out_row0 + 128, :]), ck.k(), nm='st_ckv')
                    P.dma(V(NKPE, out_row0, nkpe_d[out_row0:out_row0 + 128, :]), kp.k(), nm='st_kpe')
            P.copy(ckb.k(), ck.k(), eng='pool')
            P.tt(kg.k(), kp.k(), kgp_b.k(), ALU.mult, eng='pool')
            if rope_row0 is not None:
                rp = a1_rp[it % 2]
                P.dma(rp.k(), V(ROPEK, None, ropek_d[rope_row0:rope_row0 + 128, :]), nm='ldrope')
                kr_full = a1_kr[it % 2]
                kr = _V64(kr_full)
                tm = a1_tm[it % 2]
                kg4 = kg.k().re("p (a b c) -> p a b c", a=2, b=2, c=16)
                tm4 = tm.k().re("p (a b c) -> p a b c", a=2, b=2, c=16)
                sn4 = rp.k()[:, 64:128].re("p (a b c) -> p a b c", a=2, b=2, c=16)
                for b_ in range(2):
                    P.tt(tm4[:, :, b_, :], kg4[:, :, 1 - b_, :], sn4[:, :, b_, :], ALU.mult, eng='pool')
                P.tt(kr.k(), kg.k(), rp.k()[:, 0:64], ALU.mult, eng='pool')
                P.tt(kr.k(), kr.k(), tm.k(), ALU.add, eng='pool')

        def kv_back2(it, col0, rope_row0):
            ckb = a1_ckb[it % 2]
            kp_full = a1_kp[it % 4]
            kfin = a1_kr[it % 2] if rope_row0 is not None else a1_kg[it % 2]
            pt = 6
            for kc in range(2):
                P.transpose(psv16(pt)[:, kc * 128:(kc + 1) * 128], ckb.k()[:, kc * 128:(kc + 1) * 128], ident.k())
            P.copy(V(CKVT, col0, CKVT.ap[:, :, col0:col0 + 128]), psv16(pt)[:, 0:256].re("p (a b) -> p a b", b=128), eng='dve')
            pt2 = 7
            P.transpose(psv(pt2)[:, 0:128], kp_full.k(), identf.k())
            P.transpose(psv(pt2)[:, 128:256], kfin.k(), identf.k())
            P.act(V(KPSQ, col0, KPSQ.ap[0:64, col0:col0 + 128]), psv(pt2)[0:64, 0:128], AF.Square)
            P.copy(V(KPT, col0, KPT.ap[:, col0:col0 + 128]), psv(pt2)[0:64, 128:256], eng='dve')

        tiles = []
        for t in range(4):
            tiles.append(('x', V(XP, None, xp_d[t * 128:(t + 1) * 128, :]), 0, t * 128, None, t * 128))
        for t in range(4):
            tiles.append(('ctx', (V(CCKV, None, cckv_d[t * 128:(t + 1) * 128, :]), V(CKPE, None, ckpe_d[t * 128:(t + 1) * 128, :])),
                          1, 512 + t * 128, None, None))
        for t in range(16):
            tiles.append(('x', V(XS, None, xs_d[t * 128:(t + 1) * 128, :]), 1, 1024 + t * 128, t * 128, None))
        m1_wblk = [AR.alloc("m1_wblk%d" % i, [128, 16, 256], BF16) for i in range(2)]
        m1_b2 = [AR.alloc("m1_b2", [2, 256], F32)] * 2
        m1_row = [AR.alloc("m1_row", [2, 256], F32)] * 2

        def mk_mod1(gi):
            col0 = 4096 + gi * 256

            def fd():
                P.dma(m1_wblk[gi % 2].k(), V(WMOD, None, wmod_d.rearrange("(kc p) n -> p kc n", p=128)[:, :, col0:col0 + 256]),
                      q='pool', nm='wmod')

            def fc():
                mod_group(col0, 256, m1_wblk[gi % 2], m1_b2[gi % 2], m1_row[gi % 2], 6, 7, skip_wdma=True)
            return fd, fc

        mods1 = [mk_mod1(gi) for gi in range(16)]
        mq = [mods1[0][0], mods1[1][0]]
        for gi in range(16):
            mq.append(mods1[gi][1])
            if gi + 2 < 16:
                mq.append(mods1[gi + 2][0])

        nt = len(tiles)

        def stage(i):
            if 0 <= i + 3 < nt:
                kv_stats(i + 3, tiles[i + 3][0], tiles[i + 3][1])
            if 0 <= i + 2 < nt:
                kv_trans(i + 2, tiles[i + 2][0], tiles[i + 2][2])
            if 0 <= i + 1 < nt:
                kind, src, c, col0, rrow, orow = tiles[i + 1]
                kv_back(i + 1, kind, col0, rrow, orow, c)
            if 0 <= i < nt:
                kind, src, c, col0, rrow, orow = tiles[i]
                kv_back2(i, col0, rrow)

        for i in range(-3, nt):
            stage(i)
            for _ in range(2):
                if mq:
                    mq.pop(0)()
        while mq:
            mq.pop(0)()

    P.barrier()
    if stop_after == 'P0':
        P.emit()
        return nc

    AR.reset(mark_a2)
    w_in = AR.alloc("w_in", [128, 16, 1536], BF16)
    w_pool = AR.alloc("w_pool", [128, 8, 256], BF16)
    a2_xt = [AR.alloc("a2_xt%d" % i, [128, D], F32) for i in range(2)]
    a2_xs = [AR.alloc("a2_xs%d" % i, [128, D], BF16) for i in range(2)]
    hTP = AR.alloc("hTP", [128, 16, 512], BF16)
    hTS = AR.alloc("hTS", [128, 16, NS], BF16)
    hTH = AR.alloc("hTH", [128, 16, NH], BF16)
    Ub = [AR.alloc("Ub%d" % i, [128, 2 * 544], F32) for i in range(2)]
    cq = AR.alloc("cq", [128, 4, NS], F32)
    sqb = [AR.alloc("sqb%d" % i, [128, NS], BF16) for i in range(2)]
    rq = AR.alloc("rq", [128, NS], F32)
    Ta = AR.alloc("Ta", [128, 2 * 544], F32)
    Tb = AR.alloc("Tb", [128, 2 * 544], F32)
    rcb = [AR.alloc("rcb%d" % i, [128, NS], F32) for i in range(2)]
    dT = [AR.alloc("dT%d" % i, [128, 2 * NS], BF16) for i in range(2)]
    for blk in range(3):
        P.dma(w_in.k(blk)[:, :, blk * 512:(blk + 1) * 512],
              V(WINUQ, None, winuq_d.rearrange("(kc p) n -> p kc n", p=128)[:, :, blk * 512:(blk + 1) * 512]),
              q='pool', nm='w_in')
    P.dma(w_pool.k(), V(WPOOL, None, wpool_d.rearrange("(a p) n -> p a n", p=128)), q='pool', nm='w_pool')
    psrot = {'i': 0}

    def nextps(banks=(4, 5, 6, 7)):
        b = banks[psrot['i'] % len(banks)]
        psrot['i'] += 1
        return b

    xctr = {'i': 0}

    def a2_group(isP):
        cond = 0 if isP else 1
        n = 512 if isP else NS
        splits = [(0, 512)] if isP else [(0, 257), (257, 257)]
        coff = 0 if isP else NP_
        L = 272 if isP else 530
        R = 4 if isP else 2
        hT = hTP if isP else hTS
        def proj(g):
            ub = Ub[g % 2]
            P.memset(ub.k(), 0.0, eng='pool')
            for ocl in range(2):
                oc = 2 * g + ocl
                for (c0, cn) in splits:
                    pb = nextps()
                    for kc in range(NKC):
                        P.mm(psv(pb)[:, 0:cn], w_in.k(oc // 4)[:, kc, oc * 128:(oc + 1) * 128], hT.k()[:, kc, c0:c0 + cn],
                             start=(kc == 0), stop=(kc == NKC - 1))
                    if isP:
                        dst = ub.k().re("p (a b c) -> p a b c", a=2, b=2, c=272)[:, ocl, :, 8:264]
                        P.copy(dst, psv(pb)[:, 0:512].re("p (b c) -> p b c", c=256), eng='act')
                    else:
                        dst = ub.k().re("p (a c) -> p a c", a=2)[:, ocl, 8 + c0:8 + c0 + cn]
                        P.copy(dst, psv(pb)[:, 0:cn], eng='act')
                if not isP:
                    pb = nextps()
                    for kc in range(NKC):
                        P.mm(psv(pb)[:, 0:NH], w_in.k(oc // 4)[:, kc, oc * 128:(oc + 1) * 128], hTH.k()[:, kc, :],
                             start=(kc == 0), stop=(kc == NKC - 1))
                    u2 = ub.k().re("p (a c) -> p a c", a=2)
                    P.copy(u2[:, ocl, 0:8], psv(pb)[:, 0:8], eng='dve')
                    P.copy(u2[:, ocl, 522:529], psv(pb)[:, 10:17], eng='dve')
        def pool_(g):
            ub = Ub[g % 2]
            rc = rcb[g % 2]
            if isP:
                P.dma(rc.k()[:, 0:256], V(ROWS, None, rows_d[:, RW_RCP + g * 256:RW_RCP + (g + 1) * 256].partition_broadcast(128)), nm='ldrc')
            else:
                P.dma(rc.k(), V(ROWS, None, rows_d[:, RW_RCS + g * NS:RW_RCS + (g + 1) * NS].partition_broadcast(128)), nm='ldrc')
            X = ub.k().re("p (r l) -> p r l", l=272)[:, 0:R, 0:L] if isP else ub.k().re("p (r l) -> p r l", l=544)[:, 0:R, 0:L]
            TA = Ta.k().re("p (r l) -> p r l", l=272)[:, 0:R, 0:L] if isP else Ta.k().re("p (r l) -> p r l", l=544)[:, 0:R, 0:L]
            TB = Tb.k().re("p (r l) -> p r l", l=272)[:, 0:R, 0:L] if isP else Tb.k().re("p (r l) -> p r l", l=544)[:, 0:R, 0:L]
            P.tt(TA[:, :, 1:L], X[:, :, 0:L - 1], X[:, :, 1:L], ALU.add, eng='dve')
            sfin = TA
            if g >= 1:
                P.tt(TB[:, :, 2:L - 1], TA[:, :, 1:L - 2], TA[:, :, 3:L], ALU.add, eng='dve')
                sfin = TB
            if g >= 2:
                P.tt(TA[:, :, 4:L - 3], TB[:, :, 2:L - 5], TB[:, :, 6:L - 1], ALU.add, eng='dve')
                sfin = TA
            if g >= 3:
                P.tt(TB[:, :, 8:L - 7], TA[:, :, 4:L - 11], TA[:, :, 12:L - 3], ALU.add, eng='dve')
                sfin = TB
            nn = 256 if isP else NS
            rcv = V(rc, None, rc.ap[:, 0:nn].unsqueeze(1).to_broadcast([128, R, nn]))
            P.tt(sfin[:, :, 8:8 + nn], sfin[:, :, 8:8 + nn], rcv, ALU.mult)
            dt_ = dT[g % 2]
            dv = dt_.k()[:, 0:1024].re("p (r l) -> p r l", l=256) if isP else dt_.k().re("p (r l) -> p r l", l=NS)
            P.tt(dv, sfin[:, :, 8:8 + nn], X[:, :, 8:8 + nn], ALU.subtract)
        def pmm(g):
            dt_ = dT[g % 2]
            dflat = dt_.k()[:, 0:1024].re("p (a l) -> p a l", a=2) if isP else dt_.k().re("p (a l) -> p a l", a=2)
            for oc2 in range(2):
                for (c0, cn) in splits:
                    pb = nextps()
                    for kc2 in range(2):
                        P.mm(psv(pb)[:, 0:cn], w_pool.k()[:, g * 2 + kc2, oc2 * 128:(oc2 + 1) * 128], dflat[:, kc2, c0:c0 + cn],
                             start=(kc2 == 0), stop=(kc2 == 1))
                    ch = 2 * g + oc2
                    P.act(mixT.k(ch)[:, ch, coff + c0:coff + c0 + cn], psv(pb)[:, 0:cn], AF.Identity,
                          scale=vcol(VC_PSC + ch))

        def cqproj():
            for c4 in range(4):
                oc = 8 + c4
                for (c0, cn) in splits:
                    pb = nextps()
                    for kc in range(NKC):
                        P.mm(psv(pb)[:, 0:cn], w_in.k(oc // 4)[:, kc, oc * 128:(oc + 1) * 128], hT.k()[:, kc, c0:c0 + cn],
                             start=(kc == 0), stop=(kc == NKC - 1))
                    P.copy(cq.k()[:, c4, c0:c0 + cn], psv(pb)[:, 0:cn], eng='act')

        proj(0)
        proj(1)
        cqproj()
        for g in range(4):
            pool_(g)
            pmm(g)
            if g + 2 < 4:
                proj(g + 2)
        for (c0, cn) in splits:
            pb = nextps()
            for c4 in range(4):
                sq = sqb[c4 % 2]
                P.act(sq.k()[:, 0:cn], cq.k()[:, c4, c0:c0 + cn], AF.Square)
                P.mm(psv(pb)[:, 0:cn], onesb.k(), sq.k()[:, 0:cn], start=(c4 == 0), stop=(c4 == 3))
            P.act(rq.k()[:, c0:c0 + cn], psv(pb)[:, 0:cn], AF.Ln, bias=eps_v, scale=1.0 / 512)
            P.act(rq.k()[:, c0:c0 + cn], rq.k()[:, c0:c0 + cn], AF.Exp, scale=-0.5)
            for c4 in range(4):
                P.stt(cqn.k()[:, c4, coff + c0:coff + c0 + cn], cq.k()[:, c4, c0:c0 + cn], vcol(VC_QAG + c4),
                      rq.k()[:, c0:c0 + cn], ALU.mult, ALU.mult)

    ntiles = []
    for t in range(4):
        ntiles.append((V(XP, None, xp_d[t * 128:(t + 1) * 128, :]), 128, 0,
                       (lambda kc, c0=t * 128: hTP.k((c0, kc // 8))[:, kc, c0:c0 + 128]), None))

    def post_H():
        P.ts(hTH.k()[:, :, 0:9], hTH.k()[:, :, 0:9], vcol(VC_ML), ALU.mult)
        P.ts(hTH.k()[:, :, 9:17], hTH.k()[:, :, 9:17], vcol(VC_MR), ALU.mult)
        P.copy(hTS.k('h0')[:, :, 0:1], hTH.k()[:, :, 8:9], eng='dve')
        P.copy(hTS.k('h1')[:, :, 513:514], hTH.k()[:, :, 9:10], eng='dve')

    ntiles.append((V(XH, None, xh_d[:, :]), NH, 1, (lambda kc: hTH.k(kc // 8)[:, kc, :]), post_H))
    for t in range(4):
        ntiles.append((V(XO, None, xo_d[t * 128:(t + 1) * 128, :]), 128, 1,
                       (lambda kc, c0=1 + t * 128: hTS.k((c0, kc // 8))[:, kc, c0:c0 + 128]), None))

    def n_stats(j):
        src, ntok, cnd, dfn, post = ntiles[j]
        xt = a2_xt[j % 2]
        P.dma(xt.k()[0:ntok, :], src, nm='ldx')
        return norm_stats(xt.k()[0:ntok, :], ntok, a2_xs)

    idx = {0: n_stats(0)}
    for j in range(len(ntiles)):
        if j + 1 < len(ntiles):
            idx[j + 1] = n_stats(j + 1)
        src, ntok, cnd, dfn, post = ntiles[j]
        norm_trans(idx[j], ntok, 0, cnd, dfn, a2_xs, [0, 1, 2, 3])
        if post is not None:
            post()
    a2_group(True)
    a2_group(False)
    P.barrier()
    if stop_after == 'A2':
        P.emit()
        return nc
    phase_A1()
    P.barrier()
    if stop_after == 'A':
        P.emit()
        return nc

    AR.reset(mark_arena)
    w_uq = AR.alloc("w_uq", [128, 4, 2048], BF16)
    w_ukv = AR.alloc("w_ukv", [128, 2, 2048], BF16)
    Vb = AR.alloc("Vb", [128, 20, 1024], BF16)
    Kn = [AR.alloc("Kn%d" % i, [128, NKEY_S], BF16) for i in range(2)]
    Kp = [AR.alloc("Kp%d" % i, [128, NKEY_S], BF16) for i in range(2)]
    Qn = [AR.alloc("Qn%d" % i, [128, NS], BF16) for i in range(2)]
    Qp = [AR.alloc("Qp%d" % i, [128, NS], BF16) for i in range(2)]
    PT = [AR.alloc("PT%d" % i, [128, 260], BF16) for i in range(4)]
    pacc = [AR.alloc("pacc%d" % i, [128, 260], F32) for i in range(2)]
    paccB = [AR.alloc("paccB%d" % i, [128, 260], F32) for i in range(2)]
    sqn = [AR.alloc("sqn%d" % i, [128, 512], BF16) for i in range(2)]
    sqp = [AR.alloc("sqp%d" % i, [128, 260], BF16) for i in range(2)]
    Rk = [AR.alloc("Rk%d" % i, [128, 512], F32) for i in range(2)]
    Rqb = [AR.alloc("Rq%d" % i, [128, 260], F32) for i in range(2)]
    t1b = [AR.alloc("t1b%d" % i, [64, 260], F32) for i in range(1)] * 2
    t2b = [AR.alloc("t2b%d" % i, [64, 260], F32) for i in range(1)] * 2
    rden = [AR.alloc("rden%d" % i, [128, 260], F32) for i in range(2)]
    ropq = AR.alloc("ropq", [64, 2 * NS], F32)
    m3_wblk = [AR.alloc("m3_wblk%d" % i, [128, 16, 256], BF16) for i in range(2)]
    m3_b2 = [AR.alloc("m3_b2_%d" % i, [2, 256], F32) for i in range(1)] * 2
    m3_row = [AR.alloc("m3_row%d" % i, [2, 256], F32) for i in range(1)] * 2

    P.dma(w_uq.k(), V(WUQ, None, wuq_d.rearrange("(kc p) n -> p kc n", p=128)), q='pool', nm='w_uq')
    P.dma(w_ukv.k(), V(WUKV, None, wukv_d.rearrange("(kc p) n -> p kc n", p=128)), q='pool', nm='w_ukv')
    P.dma(ropq.k(), V(ROPEQ, None, ropeq_d[:, :]), nm='ropeq')
    for i_ in range(2):
        P.memset(Kp[i_].k()[64:128, :], 0.0)
        P.memset(Qp[i_].k()[64:128, :], 0.0, eng='pool')
        P.memset(sqp[i_].k()[64:128, :], 0.0)
    GENB = (0, 1, 2, 3)
    gctr = {'k': 0, 'q': 0, 'a': 0, 'pt': 0, 'v': 0}

    def build_V(kcol0, nkt):
        for kt in range(nkt):
            for half in range(2):
                pb = nextps(GENB)
                for kc in range(2):
                    P.mm(psv(pb), CKVT.k()[:, kc, kcol0 + kt * 128:kcol0 + (kt + 1) * 128],
                         w_ukv.k()[:, kc, 1024 + half * 512:1024 + (half + 1) * 512], start=(kc == 0), stop=(kc == 1))
                gctr['v'] += 1
                P.copy(Vb.k()[:, kt, half * 512:(half + 1) * 512], psv(pb), eng=('act' if gctr['v'] % 2 else 'dve'))

    def gen_chunks(h, kcol0, nkeys, qoff, splits, rope):
        hb = h % 2
        chunks = []

        def kchunk(k0, kn, j):
            pb = j % 2
            sq = sqn[j % 2]
            R = Rk[j % 2]

            def fa():
                for kc in range(2):
                    P.mm(psv(pb)[:, 0:kn], w_ukv.k()[:, kc, h * 128:(h + 1) * 128], CKVT.k()[:, kc, kcol0 + k0:kcol0 + k0 + kn],
                         start=(kc == 0), stop=(kc == 1))
                P.act(sq.k()[:, 0:kn], psv(pb)[:, 0:kn], AF.Square)

            def fb():
                pb2 = 2
                P.mm(psv(pb2)[:, 0:kn], onesb.k(), sq.k()[:, 0:kn], start=True, stop=False)
                P.mm(psv(pb2)[:, 0:kn], onesb.k(), KPSQ.k()[:, kcol0 + k0:kcol0 + k0 + kn], start=False, stop=True)
                P.act(R.k()[:, 0:kn], psv(pb2)[:, 0:kn], AF.Ln, bias=eps_v, scale=1.0 / 192)
                P.act(R.k()[:, 0:kn], R.k()[:, 0:kn], AF.Exp, scale=-0.5)
                P.stt(Kn[hb].k()[:, k0:k0 + kn], psv(pb)[:, 0:kn], vcol(VC_KGN), R.k()[:, 0:kn], ALU.mult, ALU.mult)
                P.tt(Kp[hb].k()[0:64, k0:k0 + kn], KPT.k()[0:64, kcol0 + k0:kcol0 + k0 + kn], R.k()[0:64, 0:kn], ALU.mult)
            return fa, fb

        def qchunk(c0, cn):
            def f():
                i = gctr['q']
                gctr['q'] += 1
                qa = qoff + c0
                pbn = nextps(GENB)
                for kc in range(4):
                    P.mm(psv(pbn)[:, 0:cn], w_uq.k()[:, kc, h * 256:h * 256 + 128], cqn.k()[:, kc, qa:qa + cn],
                         start=(kc == 0), stop=(kc == 3))
                pbp = nextps(GENB)
                for kc in range(4):
                    P.mm(psv(pbp)[:, 0:cn], w_uq.k()[:, kc, h * 256 + 128:h * 256 + 256], cqn.k()[:, kc, qa:qa + cn],
                         start=(kc == 0), stop=(kc == 3))
                if rope:
                    pbr = nextps(GENB)
                    for kc in range(4):
                        P.mm(psv(pbr)[0:64, 0:cn], w_uq.k()[:, kc, h * 256 + 192:h * 256 + 256], cqn.k()[:, kc, qa:qa + cn],
                             start=(kc == 0), stop=(kc == 3))
                s1 = sqn[i % 2]
                s2 = sqp[i % 2]
                P.act(s1.k()[:, 0:cn], psv(pbn)[:, 0:cn], AF.Square)
                P.act(s2.k()[0:64, 0:cn], psv(pbp)[0:64, 0:cn], AF.Square)
                pss = nextps(GENB)
                P.mm(psv(pss)[:, 0:cn], onesb.k(), s1.k()[:, 0:cn], start=True, stop=False)
                P.mm(psv(pss)[:, 0:cn], onesb.k(), s2.k()[:, 0:cn], start=False, stop=True)
                R = Rqb[i % 2]
                P.act(R.k()[:, 0:cn], psv(pss)[:, 0:cn], AF.Ln, bias=eps_v, scale=1.0 / 192)
                P.act(R.k()[:, 0:cn], R.k()[:, 0:cn], AF.Exp, scale=-0.5)
                P.stt(Qn[hb].k()[:, c0:c0 + cn], psv(pbn)[:, 0:cn], vcol(VC_QGN), R.k()[:, 0:cn], ALU.mult, ALU.mult)
                if not rope:
                    P.stt(Qp[hb].k()[0:64, c0:c0 + cn], psv(pbp)[0:64, 0:cn], vcol(VC_QGP, 1, 64), R.k()[0:64, 0:cn], ALU.mult, ALU.mult)
                else:
                    t1 = t1b[i % 2]
                    t2 = t2b[i % 2]
                    P.stt(t1.k()[:, 0:cn], psv(pbp)[0:64, 0:cn], vcol(VC_QGP, 1, 64), ropq.k()[:, c0:c0 + cn], ALU.mult, ALU.mult)
                    P.stt(t2.k()[:, 0:cn], psv(pbr)[0:64, 0:cn], vcol(VC_QGPP, 1, 64), ropq.k()[:, NS + c0:NS + c0 + cn], ALU.mult, ALU.mult)
                    P.tt(t1.k()[:, 0:cn], t1.k()[:, 0:cn], t2.k()[:, 0:cn], ALU.add, eng='pool')
                    P.tt(Qp[hb].k()[0:64, c0:c0 + cn], t1.k()[:, 0:cn], R.k()[0:64, 0:cn], ALU.mult)
            return f

        for (c0, cn) in splits:
            chunks.append(qchunk(c0, cn))
        k0 = 0
        j = 0
        while k0 < nkeys:
            kn = min(512, nkeys - k0)
            fa, fb = kchunk(k0, kn, j)
            chunks.append(fa)
            chunks.append(fb)
            k0 += kn
            j += 1
        return chunks

    def attend(h, nkt, splits, mcol0, pending):
        hb = h % 2
        it_ = 0
        for (c0, cn) in splits:
            ai = gctr['a']
            gctr['a'] += 1
            pob = 6 + (ai % 2)
            pa = pacc[ai % 2]
            pb_ = paccB[ai % 2]
            def score(kt):
                sb = nextps((4, 5))
                P.mm(psv(sb)[:, 0:cn], Kn[hb].k()[:, kt * 128:(kt + 1) * 128], Qn[hb].k()[:, c0:c0 + cn], start=True, stop=False)
                P.mm(psv(sb)[:, 0:cn], Kp[hb].k()[:, kt * 128:(kt + 1) * 128], Qp[hb].k()[:, c0:c0 + cn], start=False, stop=True)
                pt = PT[gctr['pt'] % 4]
                gctr['pt'] += 1
                P.act(pt.k()[:, 0:cn], psv(sb)[:, 0:cn], AF.Exp, scale=ATTN_SCALE)
                return pt

            pts = {0: score(0)}
            for kt in range(nkt):
                if kt + 1 < nkt:
                    pts[kt + 1] = score(kt + 1)
                pt = pts.pop(kt)
                P.mm(psv(pob)[:, 0:cn], Vb.k()[:, kt, h * 128:(h + 1) * 128], pt.k()[:, 0:cn], start=(kt == 0), stop=(kt == nkt - 1))
                acc, eng_ = (pa, 'pool') if kt % 2 == 0 else (pb_, 'dve')
                if kt < 2:
                    P.copy(acc.k()[:, 0:cn], pt.k()[:, 0:cn], eng=eng_)
                else:
                    P.tt(acc.k()[:, 0:cn], acc.k()[:, 0:cn], pt.k()[:, 0:cn], ALU.add, eng=eng_)
                it_ += 1
                if pending and it_ % 2 == 0:
                    pending.pop(0)()
            pdb = nextps((2, 3))
            P.mm(psv(pdb)[:, 0:cn], onesf.k(), pa.k()[:, 0:cn], start=True, stop=False)
            P.mm(psv(pdb)[:, 0:cn], onesf.k(), pb_.k()[:, 0:cn], start=False, stop=True)
            rd = rden[ai % 2]
            P.act(rd.k()[:, 0:cn], psv(pdb)[:, 0:cn], AF.Ln)
            P.act(rd.k()[:, 0:cn], rd.k()[:, 0:cn], AF.Exp, scale=-1.0)
            P.tt(mixT.k(8 + h)[:, 8 + h, mcol0 + c0:mcol0 + c0 + cn], psv(pob)[:, 0:cn], rd.k()[:, 0:cn], ALU.mult)
        while pending:
            pending.pop(0)()

    build_V(0, 2)

    def build_V_tiles(kcol0, t0):
        for kt in range(2):
            for half in range(2):
                pb = nextps(GENB)
                for kc in range(2):
                    P.mm(psv(pb), CKVT.k()[:, kc, kcol0 + kt * 128:kcol0 + (kt + 1) * 128],
                         w_ukv.k()[:, kc, 1024 + half * 512:1024 + (half + 1) * 512], start=(kc == 0), stop=(kc == 1))
                P.copy(Vb.k()[:, t0 + kt, half * 512:(half + 1) * 512], psv(pb), eng=('act' if half else 'dve'))

    build_V_tiles(256, 2)

    def prompt_stream(s_, kcol0, qoff):
        B = (4 * s_, 4 * s_ + 1, 4 * s_ + 2, 4 * s_ + 3)
        st_ = []
        for h in range(8):
            def Q1(h=h):
                for kc in range(4):
                    P.mm(psv(B[0])[:, 0:256], w_uq.k()[:, kc, h * 256:h * 256 + 128], cqn.k()[:, kc, qoff:qoff + 256],
                         start=(kc == 0), stop=(kc == 3))
                for kc in range(4):
                    P.mm(psv(B[1])[:, 0:256], w_uq.k()[:, kc, h * 256 + 128:h * 256 + 256], cqn.k()[:, kc, qoff:qoff + 256],
                         start=(kc == 0), stop=(kc == 3))
                P.act(sqn[s_].k()[:, 0:256], psv(B[0])[:, 0:256], AF.Square)
                P.act(sqp[s_].k()[0:64, 0:256], psv(B[1])[0:64, 0:256], AF.Square)

            def Q2(h=h):
                P.mm(psv(B[2])[:, 0:256], onesb.k(), sqn[s_].k()[:, 0:256], start=True, stop=False)
                P.mm(psv(B[2])[:, 0:256], onesb.k(), sqp[s_].k()[:, 0:256], start=False, stop=True)
                R = Rqb[s_]
                P.act(R.k()[:, 0:256], psv(B[2])[:, 0:256], AF.Ln, bias=eps_v, scale=1.0 / 192)
                P.act(R.k()[:, 0:256], R.k()[:, 0:256], AF.Exp, scale=-0.5)
                P.stt(Qn[s_].k()[:, 0:256], psv(B[0])[:, 0:256], vcol(VC_QGN), R.k()[:, 0:256], ALU.mult, ALU.mult)
                P.stt(Qp[s_].k()[0:64, 0:256], psv(B[1])[0:64, 0:256], vcol(VC_QGP, 1, 64), R.k()[0:64, 0:256], ALU.mult, ALU.mult)

            def K1(h=h):
                for kc in range(2):
                    P.mm(psv(B[3])[:, 0:256], w_ukv.k()[:, kc, h * 128:(h + 1) * 128], CKVT.k()[:, kc, kcol0:kcol0 + 256],
                         start=(kc == 0), stop=(kc == 1))
                P.act(sqn[s_].k()[:, 256:512], psv(B[3])[:, 0:256], AF.Square)

            def K2(h=h):
                P.mm(psv(B[2])[:, 256:512], onesb.k(), sqn[s_].k()[:, 256:512], start=True, stop=False)
                P.mm(psv(B[2])[:, 256:512], onesb.k(), KPSQ.k()[:, kcol0:kcol0 + 256], start=False, stop=True)
                R = Rk[s_]
                P.act(R.k()[:, 0:256], psv(B[2])[:, 256:512], AF.Ln, bias=eps_v, scale=1.0 / 192)
                P.act(R.k()[:, 0:256], R.k()[:, 0:256], AF.Exp, scale=-0.5)
                P.stt(Kn[s_].k()[:, 0:256], psv(B[3])[:, 0:256], vcol(VC_KGN), R.k()[:, 0:256], ALU.mult, ALU.mult)
                P.tt(Kp[s_].k()[0:64, 0:256], KPT.k()[0:64, kcol0:kcol0 + 256], R.k()[0:64, 0:256], ALU.mult)

            def A1_(h=h):
                for kt in range(2):
                    P.mm(psv(B[0])[:, kt * 256:(kt + 1) * 256], Kn[s_].k()[:, kt * 128:(kt + 1) * 128], Qn[s_].k()[:, 0:256],
                         start=True, stop=False)
                    P.mm(psv(B[0])[:, kt * 256:(kt + 1) * 256], Kp[s_].k()[:, kt * 128:(kt + 1) * 128], Qp[s_].k()[:, 0:256],
                         start=False, stop=True)
                for kt in range(2):
                    P.act(PT[2 * s_ + kt].k()[:, 0:256], psv(B[0])[:, kt * 256:(kt + 1) * 256], AF.Exp, scale=ATTN_SCALE)

            def A2_(h=h):
                for kt in range(2):
                    P.mm(psv(B[1])[:, 0:256], Vb.k()[:, 2 * s_ + kt, h * 128:(h + 1) * 128], PT[2 * s_ + kt].k()[:, 0:256],
                         start=(kt == 0), stop=(kt == 1))
                P.tt(pacc[s_].k()[:, 0:256], PT[2 * s_].k()[:, 0:256], PT[2 * s_ + 1].k()[:, 0:256], ALU.add, eng='pool')
                P.mm(psv(B[1])[:, 256:512], onesf.k(), pacc[s_].k()[:, 0:256])
                rd = rden[s_]
                P.act(rd.k()[:, 0:256], psv(B[1])[:, 256:512], AF.Ln)
                P.act(rd.k()[:, 0:256], rd.k()[:, 0:256], AF.Exp, scale=-1.0)
                P.tt(mixT.k(8 + h)[:, 8 + h, qoff:qoff + 256], psv(B[1])[:, 0:256], rd.k()[:, 0:256], ALU.mult)

            st_ += [Q1, Q2, K1, K2, A1_, A2_]
        return st_

    ps0 = prompt_stream(0, 0, 0)
    ps1 = prompt_stream(1, 256, 256)
    for a_, b_ in zip(ps0, ps1):
        a_()
        b_()

    keysets = [
        (512, NKEY_S, 512, [(0, 257), (257, 257)], True),
    ]
    mod_chunks = []

    def mk_mod(gi):
        col0 = 4096 + gi * 256

        def fd():
            P.dma(m3_wblk[gi % 2].k(), V(WMOD, None, wmod_d.rearrange("(kc p) n -> p kc n", p=128)[:, :, col0:col0 + 256]),
                  q='pool', nm='wmod')

        def fc():
            mod_group(col0, 256, m3_wblk[gi % 2], m3_b2[gi % 2], m3_row[gi % 2], 3, 2, skip_wdma=True)
        return fd, fc

    mods = [mk_mod(gi) for gi in range(16, 32)]
    mod_chunks.append(mods[0][0])
    mod_chunks.append(mods[1][0])
    for gi in range(16):
        mod_chunks.append(mods[gi][1])
        if gi + 2 < 16:
            mod_chunks.append(mods[gi + 2][0])
    for (kcol0, nkeys, qoff, splits, rope) in keysets:
        nkt = nkeys // 128
        build_V(kcol0, nkt)
        for f in gen_chunks(0, kcol0, nkeys, qoff, splits, rope):
            f()
        for h in range(8):
            pending = gen_chunks(h + 1, kcol0, nkeys, qoff, splits, rope) if h < 7 else []
            if rope:
                for _ in range(4):
                    if mod_chunks:
                        pending.append(mod_chunks.pop(0))
            attend(h, nkt, splits, qoff, pending)
    while mod_chunks:
        mod_chunks.pop(0)()
    mod_finish(1)
    P.barrier()
    if stop_after == 'ATT':
        P.emit()
        return nc

    def build_gate(GB, kind, Dt, conds=(0, 1)):
        for ci, c in enumerate(conds):
            for kc in range(NKC):
                d = Dt[kc % 2]
                P.ts(d.k(), identf.k(), MODF.k()[:, kind * 16 + kc, c:c + 1], ALU.mult)
                if kc % 4 == 0:
                    pb = nextps(GENB)
                P.mm(psv(pb)[:, (kc % 4) * 128:(kc % 4 + 1) * 128], onesf.k(), d.k())
                if kc % 4 == 3:
                    P.copy(GB.k()[:, ci, (kc // 4) * 512:(kc // 4 + 1) * 512], psv(pb), eng='act')

    AR.reset(mark_cqn)
    h2T = AR.alloc("h2T", [128, 16, NALL], BF16)
    assert AR.cur <= mark_arena
    AR.reset(mark_arena)
    w_out = AR.alloc("w_out", [128, 16, 2048], BF16)
    GB1 = AR.alloc("GB1", [128, 1, 2048], F32)
    wo_xt = [AR.alloc("wo_xt%d" % i, [128, D], F32) for i in range(2)]
    wo_x1 = [AR.alloc("wo_x1%d" % i, [128, D], F32) for i in range(3)]
    wo_xs = [AR.alloc("wo_xs%d" % i, [128, D], BF16) for i in range(3)]
    Dt = [AR.alloc("Dt%d" % i, [128, 128], F32) for i in range(2)]
    mixH = AR.alloc("mixH", [128, 16, 2], BF16)
    hh = AR.alloc("hh", [128, 16, 2], BF16)
    for blk in range(4):
        P.dma(w_out.k(blk)[:, :, blk * 512:(blk + 1) * 512],
              V(WOUT, None, wout_d.rearrange("(kc p) n -> p kc n", p=128)[:, :, blk * 512:(blk + 1) * 512]),
              q='pool', nm='w_out')
    build_gate(GB1, 2, Dt, conds=(0,))
    P.copy(mixH.k()[:, :, 0:1], mixT.k()[:, :, 512:513], eng='dve')
    P.copy(mixH.k()[:, :, 1:2], mixT.k()[:, :, 1025:1026], eng='dve')
    wtiles = []
    for t in range(4):
        wtiles.append(('P', t))
    for t in range(4):
        wtiles.append(('S', t))
    wtiles.append(('H', 0))
    def wo_front(wi):
        kind, t = wtiles[wi]
        cnd = 0 if kind == 'P' else 1
        ntok = 2 if kind == 'H' else 128
        xt = wo_xt[wi % 2]
        x1 = wo_x1[wi % 3]
        mc0 = 0
        if wi == 4:
            build_gate(GB1, 2, Dt, conds=(1,))
        if kind == 'P':
            P.dma(xt.k(), V(XP, None, xp_d[t * 128:(t + 1) * 128, :]), nm='ldx')
            mc0 = t * 128
        elif kind == 'S':
            P.dma(xt.k(), V(XO, None, xo_d[t * 128:(t + 1) * 128, :]), nm='ldx')
            mc0 = 512 + 1 + t * 128
        else:
            P.dma(xt.k()[0:2, :], V(XH, None, xh_d[8:10, :]), nm='ldx')
        for cg in range(4):
            pb = nextps((4, 5, 6, 7))
            for kc in range(NKC):
                lhsT = mixH.k()[:, kc, :] if kind == 'H' else mixT.k()[:, kc, mc0:mc0 + 128]
                P.mm(psv(pb)[0:ntok, :], lhsT, w_out.k(cg)[:, kc, cg * 512:(cg + 1) * 512], start=(kc == 0), stop=(kc == NKC - 1))
            P.tt(x1.k()[0:ntok, cg * 512:(cg + 1) * 512], psv(pb)[0:ntok, :], GB1.k()[0:ntok, 0, cg * 512:(cg + 1) * 512], ALU.mult)
            P.tt(x1.k()[0:ntok, cg * 512:(cg + 1) * 512], x1.k()[0:ntok, cg * 512:(cg + 1) * 512],
                 xt.k()[0:ntok, cg * 512:(cg + 1) * 512], ALU.add)
        if kind == 'P':
            P.dma(V(YP, t, yp_d[t * 128:(t + 1) * 128, :]), x1.k(), nm='st_x1')
        elif kind == 'S':
            P.dma(V(YS, t, ys_d[t * 128:(t + 1) * 128, :]), x1.k(), nm='st_x1')

    wo_idx = {}

    def wo_stats(wi):
        kind, t = wtiles[wi]
        ntok = 2 if kind == 'H' else 128
        x1 = wo_x1[wi % 3]
        wo_idx[wi] = norm_stats(x1.k()[0:ntok, :], ntok, wo_xs)

    def wo_back(wi):
        kind, t = wtiles[wi]
        cnd = 0 if kind == 'P' else 1
        i_ = wo_idx[wi]
        if kind == 'P':
            norm_trans(i_, 128, 1, cnd, lambda kc, c0=t * 128: h2T.k((c0, kc // 8))[:, kc, c0:c0 + 128], wo_xs, [0, 1, 2, 3])
        elif kind == 'S':
            mc0 = 512 + 1 + t * 128
            norm_trans(i_, 128, 1, cnd, lambda kc, c0=mc0: h2T.k((c0, kc // 8))[:, kc, c0:c0 + 128], wo_xs, [0, 1, 2, 3])
        else:
            norm_trans(i_, 2, 1, cnd, lambda kc: hh.k(kc // 8)[:, kc, :], wo_xs, [0, 1, 2, 3])
            P.ts(h2T.k()[:, :, 512:513], hh.k()[:, :, 0:1], vcol(VC_ML), ALU.mult)
            P.ts(h2T.k()[:, :, 1025:1026], hh.k()[:, :, 1:2], vcol(VC_MR), ALU.mult)

    for j in range(2):
        wo_front(j)
        wo_stats(j)
    for wi in range(len(wtiles)):
        if wi + 2 < len(wtiles):
            wo_front(wi + 2)
            wo_stats(wi + 2)
        wo_back(wi)
    P.barrier()
    if stop_after == 'WOUT':
        P.emit()
        return nc

    AR.reset(mark_mix)
    wub = [AR.alloc("wub%d" % i, [128, 16, 512], BF16) for i in range(2)]
    assert AR.cur <= mark_cqn
    AR.reset(mark_arena)
    gT = AR.alloc("gT", [128, NJ, NALL], BF16)
    mark_fdn = AR.cur
    upb = [AR.alloc("upb%d" % i, [128, NALL], F32) for i in range(2)]
    zb = [AR.alloc("zb%d" % i, [128, NALL], F32) for i in range(2)]
    sab = [AR.alloc("sab%d" % i, [128, NALL], F32) for i in range(2)]
    usplits = [(0, 512), (512, 257), (769, 257)]
    cix = 0
    def ld_wup(jj):
        P.dma(wub[jj % 2].k(), V(WUP, None, wup_d.rearrange("(kc p) n -> p kc n", p=128)[:, :, jj * 512:(jj + 1) * 512]),
              q='pool', nm='w_up')

    ld_wup(0)
    for jj in range(22):
        wb = wub[jj % 2]
        if jj + 1 < 22:
            ld_wup(jj + 1)
        for q4 in range(4):
            ci = 4 * jj + q4
            up = upb[cix % 2]
            z = zb[cix % 2]
            cix += 1
            for (c0, cn) in usplits:
                pb = nextps((0, 1, 2, 3, 4, 5, 6, 7))
                for kc in range(NKC):
                    P.mm(psv(pb)[:, 0:cn], wb.k()[:, kc, q4 * 128:(q4 + 1) * 128], h2T.k()[:, kc, c0:c0 + cn],
                         start=(kc == 0), stop=(kc == NKC - 1))
                P.copy(up.k()[:, c0:c0 + cn], psv(pb)[:, 0:cn], eng='act')
            w0 = vcol(VC_CW + ci)
            w1 = vcol(VC_CW + 88 + ci)
            w2 = vcol(VC_CW + 176 + ci)
            bb = vcol(VC_CB + ci)
            P.ts(z.k(), up.k(), w1, ALU.mult, bb, ALU.add, eng='pool')
            zP = z.k()[:, 0:512].re("p (s l) -> p s l", l=256)
            uP = up.k()[:, 0:512].re("p (s l) -> p s l", l=256)
            zS = z.k()[:, 512:NALL]
            uS = up.k()[:, 512:NALL]
            P.stt(zP[:, :, 1:256], uP[:, :, 0:255], w0, zP[:, :, 1:256], ALU.mult, ALU.add)
            P.stt(zS[:, 1:NS], uS[:, 0:NS - 1], w0, zS[:, 1:NS], ALU.mult, ALU.add)
            P.stt(zP[:, :, 0:255], uP[:, :, 1:256], w2, zP[:, :, 0:255], ALU.mult, ALU.add)
            P.stt(zS[:, 0:NS - 1], uS[:, 1:NS], w2, zS[:, 0:NS - 1], ALU.mult, ALU.add)
            if q4 % 2 == 0:
                sa = sab[(ci // 2) % 2]
                P.act(sa.k(), z.k(), AF.Silu)
            else:
                j = 2 * jj + q4 // 2
                sa = sab[(ci // 2) % 2]
                P.tt(gT.k(j)[:, j, :], sa.k(), z.k(), ALU.mult)
    P.barrier()
    if stop_after == 'FUP':
        P.emit()
        return nc

    AR.reset(mark_mix)
    wdb0 = AR.alloc("wdb0", [128, NJ, 256], BF16)
    assert AR.cur <= mark_cqn
    AR.reset(mark_cqn)
    GB2 = AR.alloc("GB2", [128, 2, 2048], F32)
    Dt2 = [AR.alloc("Dt2_%d" % i, [128, 128], F32) for i in range(2)]
    x1s = [AR.alloc("x1s%d" % i, [128, 256], F32) for i in range(4)]
    ot = [AR.alloc("ot%d" % i, [128, 256], F32) for i in range(4)]
    assert AR.cur <= mark_arena
    AR.reset(mark_fdn)
    wdb1 = AR.alloc("wdb1", [128, NJ, 256], BF16)
    wdb = [wdb0, wdb1]
    build_gate(GB2, 5, Dt2)
    oi = 0
    def ld_wdn(cg):
        P.dma(wdb[cg % 2].k(), V(WDN, None, wdn_d.rearrange("(kc p) n -> p kc n", p=128)[:, :, cg * 256:(cg + 1) * 256]),
              q='pool', nm='w_dn')

    ld_wdn(0)
    for cg in range(8):
        wd = wdb[cg % 2]
        if cg + 1 < 8:
            ld_wdn(cg + 1)
        for ti in range(8):
            isP = ti < 4
            t = ti % 4
            cnd = 0 if isP else 1
            gc0 = t * 128 if isP else 512 + 1 + t * 128
            YB, y_d = (YP, yp_d) if isP else (YS, ys_d)
            xs_ = x1s[oi % 4]
            o = ot[oi % 4]
            oi += 1
            P.dma(xs_.k(), V(YB, (t, cg), y_d[t * 128:(t + 1) * 128, cg * 256:(cg + 1) * 256]), nm='ld_x1')
            pb = nextps((0, 1, 2, 3, 4, 5, 6, 7))
            for kc in range(NJ):
                P.mm(psv(pb)[:, 0:256], gT.k()[:, kc, gc0:gc0 + 128], wd.k()[:, kc, :], start=(kc == 0), stop=(kc == NJ - 1))
            P.tt(o.k(), psv(pb)[:, 0:256], GB2.k()[:, cnd, cg * 256:(cg + 1) * 256], ALU.mult)
            P.tt(o.k(), o.k(), xs_.k(), ALU.add, eng='pool')
            P.dma(V(YB, (t, cg), y_d[t * 128:(t + 1) * 128, cg * 256:(cg + 1) * 256]), o.k(), nm='st_y')
    P.emit()
    return nc


def _rope_tables(pos):
    pos = np.asarray(pos)
    row = (pos // 64).astype(np.float32)
    col = (pos % 64).astype(np.float32)
    inv = (10000.0 ** (-np.arange(16, dtype=np.float32) / 16)).astype(np.float32)
    ar = row[:, None] * inv
    ac = col[:, None] * inv
    ang = np.concatenate([ar, ar, ac, ac], axis=-1).astype(np.float32)
    return np.cos(ang).astype(np.float32), (np.sin(ang).astype(np.float32) * SGN[None, :]).astype(np.float32)


def _rc_table(pos, T):
    out = np.zeros((4, len(pos)), np.float32)
    for g, w in enumerate(POOL_W):
        lo = np.clip(pos - w // 2, 0, T - 1)
        hi = np.clip(pos - w // 2 + w - 1, 0, T - 1)
        out[g] = 1.0 / np.maximum(hi - lo + 1, 1).astype(np.float32)
    return out


def prep_inputs(inp):
    f = np.float32
    g = lambda k: np.asarray(inp[k], dtype=f)
    x_prompt, x_sample = g('x_prompt'), g('x_sample')
    cache_ckv, cache_kpe, c, c_ctx = g('cache_ckv'), g('cache_kpe'), g('c'), g('c_ctx')
    w_in = g('w_in')[0]
    w_uq = g('w_uq')[0]
    w_ukv = g('w_ukv')[0]
    w_up = g('w_up')[0]
    conv_w, conv_b = g('conv_w')[0], g('conv_b')[0]
    qg, kg = g('q_norm_g')[0], g('k_norm_g')[0]
    shared = {}
    shared['w_mod'] = np.ascontiguousarray(g('w_mod')[0])
    shared['b_mod2'] = np.ascontiguousarray(np.broadcast_to(g('b_mod')[0][None, :], (2, 6 * D)))
    shared['w_in_uq'] = np.ascontiguousarray(w_in[:, :1536])
    shared['w_in_kv'] = np.ascontiguousarray(w_in[:, 1536:1856])
    shared['w_pool'] = np.ascontiguousarray(g('w_pool')[0].reshape(1024, 256))
    cols = []
    for h in range(8):
        b0 = h * 192
        cols += list(range(b0, b0 + 128)) + list(range(b0 + 128, b0 + 192)) + list(b0 + 128 + PERM)
    shared['w_uq_x'] = np.ascontiguousarray(w_uq[:, cols])
    kc_, vc_ = [], []
    for h in range(8):
        kc_ += list(range(h * 256, h * 256 + 128))
        vc_ += list(range(h * 256 + 128, h * 256 + 256))
    shared['w_ukv_x'] = np.ascontiguousarray(w_ukv[:, kc_ + vc_])
    shared['w_out'] = np.ascontiguousarray(g('w_out')[0])
    chunk_col = []
    for jj in range(22):
        for j in (2 * jj, 2 * jj + 1):
            chunk_col += [j * 128, DFF + j * 128]
    upcols = np.concatenate([np.arange(c0, c0 + 128) for c0 in chunk_col])
    shared['w_up_x'] = np.ascontiguousarray(w_up[:, upcols])
    shared['w_down'] = np.ascontiguousarray(g('w_down')[0])
    shared['ident'] = np.eye(128, dtype=f).astype(ml_dtypes.bfloat16)
    shared['identf'] = np.eye(128, dtype=f)
    rck, rsk = _rope_tables(np.arange(2048))
    shared['ropek'] = np.ascontiguousarray(np.concatenate([rck, rsk], axis=1))

    def fm(v, nch):
        return np.asarray(v, f).reshape(nch, 128).T

    vecs = np.zeros((128, NV), f)
    vecs[:, VC_G1:VC_G1 + 16] = fm(g('norm1_g')[0], 16)
    vecs[:, VC_G2:VC_G2 + 16] = fm(g('norm2_g')[0], 16)
    vecs[:, VC_PSC:VC_PSC + 8] = fm(g('pool_scale')[0], 8)
    vecs[:, VC_QAG:VC_QAG + 4] = fm(g('q_a_g')[0], 4)
    cwp = conv_w[:, upcols]
    cbp = conv_b[upcols]
    for tap in range(3):
        vecs[:, VC_CW + tap * 88:VC_CW + (tap + 1) * 88] = fm(cwp[tap], 88)
    vecs[:, VC_CB:VC_CB + 88] = fm(cbp, 88)
    vecs[:, VC_QGN] = qg[:128]
    vecs[:, VC_KGN] = kg[:128]
    vecs[:64, VC_QGP] = qg[128:192]
    vecs[:64, VC_QGPP] = qg[128:192][PERM]
    vecs[:, VC_EPS] = EPS
    vecs[:, VC_ONE] = 1.0
    rcP = _rc_table(np.arange(256), 256)
    in_maps = []
    for core in range(8):
        b = core // 4
        qd = core % 4
        s0 = qd * 512
        m = dict(shared)
        m['xp'] = np.ascontiguousarray(x_prompt[2 * core:2 * core + 2].reshape(512, D))
        m['xo'] = np.ascontiguousarray(x_sample[b, s0:s0 + 512])
        m['xs'] = np.ascontiguousarray(x_sample[b])
        hidx = np.concatenate([np.arange(s0 - 9, s0), np.arange(s0 + 512, s0 + 520)])
        m['xh'] = np.ascontiguousarray(x_sample[b, np.clip(hidx, 0, 2047)])
        m['cckv'] = np.ascontiguousarray(cache_ckv[b, 0])
        m['ckpe'] = np.ascontiguousarray(cache_kpe[b, 0])
        cc = np.stack([c_ctx, c[b]], axis=0)
        m['cT'] = np.ascontiguousarray(cc.reshape(2, 16, 128).transpose(2, 1, 0).reshape(128, 32))
        v = vecs.copy()
        v[:, VC_ML] = 0.0 if qd == 0 else 1.0
        v[:, VC_MR] = 0.0 if qd == 3 else 1.0
        m['vecs'] = v
        sxpos = np.arange(s0 - 1, s0 + 513)
        rows = np.zeros((1, NR), f)
        rows[0, RW_KVG:RW_KVG + 256] = g('kv_a_g')[0]
        rows[0, RW_KGP:RW_KGP + 64] = kg[128:192]
        rows[0, RW_RCP:RW_RCP + 1024] = rcP.reshape(-1)
        rows[0, RW_RCS:RW_RCS + 4 * NS] = _rc_table(sxpos, 2048).reshape(-1)
        m['rows'] = rows
        qc, qs = _rope_tables(np.clip(sxpos, 0, 2047))
        m['ropeq'] = np.ascontiguousarray(np.concatenate([qc.T, qs.T], axis=1))
        in_maps.append(m)
    return in_maps


_NC_CACHE = {}


def kernel(**inputs):
    in_maps = prep_inputs(inputs)
    if 'nc' not in _NC_CACHE:
        _NC_CACHE['nc'] = build()
    nc = _NC_CACHE['nc']
    res = run_bass_kernel_spmd(nc, in_maps, core_ids=list(range(8)))
    r = res.results
    yp = np.stack([r[c]['yp'] for c in range(8)], 0).reshape(16, 256, D).astype(np.float32)
    ys = np.stack([r[c]['ys'] for c in range(8)], 0).reshape(2, 2048, D).astype(np.float32)
    nckv = np.stack([r[c]['nckv'] for c in range(8)], 0).reshape(16, 1, 256, 256).astype(np.float32)
    nkpe = np.stack([r[c]['nkpe'] for c in range(8)], 0).reshape(16, 1, 256, 64).astype(np.float32)
    return (yp, ys, nckv, nkpe)
```

```python
import numpy as np
import ml_dtypes
import concourse.bass as bass
import concourse.mybir as mybir
from concourse.bass_utils import run_bass_kernel_spmd
from contextlib import ExitStack

F32 = mybir.dt.float32
BF16 = mybir.dt.bfloat16
AF = mybir.ActivationFunctionType
ALU = mybir.AluOpType

STREAMS = ['pe', 'act', 'dve', 'pool', 'sp']

D = 2048
NKC = 16
DFF = 5632
NJ = 44
EPS = 1e-6
ATTN_SCALE = 192 ** -0.5
NP_ = 512
NS = 514
NH = 17
NALL = NP_ + NS
NKEY_S = 2560
NK_TOT = 512 + NKEY_S
PERM = np.concatenate([np.arange(16, 32), np.arange(0, 16), np.arange(48, 64), np.arange(32, 48)])
SGN = np.concatenate([-np.ones(16), np.ones(16), -np.ones(16), np.ones(16)]).astype(np.float32)
POOL_W = (2, 4, 8, 16)

VC_G1 = 0
VC_G2 = 16
VC_PSC = 32
VC_QAG = 40
VC_CW = 44
VC_CB = VC_CW + 264
VC_QGN = VC_CB + 88
VC_KGN = VC_QGN + 1
VC_QGP = VC_KGN + 1
VC_QGPP = VC_QGP + 1
VC_ML = VC_QGPP + 1
VC_MR = VC_ML + 1
VC_EPS = VC_MR + 1
VC_ONE = VC_EPS + 1
NV = VC_ONE + 1
RW_KVG = 0
RW_KGP = 256
RW_RCP = 320
RW_RCS = RW_RCP + 4 * 256
NR = RW_RCS + 4 * NS


class V:
    __slots__ = ('buf', 'key', 'ap')

    def __init__(self, buf, key, ap):
        self.buf = buf
        self.key = key
        self.ap = ap

    def __getitem__(self, idx):
        return V(self.buf, self.key, self.ap[idx])

    def bc(self, shape):
        return V(self.buf, self.key, self.ap.to_broadcast(list(shape)))

    def re(self, s, **kw):
        return V(self.buf, self.key, self.ap.rearrange(s, **kw))


class Buf:
    def __init__(self, ap, name):
        self.ap = ap
        self.name = name
        self.wr = {}
        self.rd = {}

    def k(self, key=None):
        return V(self, key, self.ap)

    def __getitem__(self, idx):
        return V(self, None, self.ap[idx])


class Op:
    __slots__ = ('stream', 'fn', 'deps', 'dma', 'sem', 'val', 'needed', 'idx', 'nm')


class Prog:
    def __init__(self, nc, n_dma_sems=(40, 40)):
        self.nc = nc
        self.ops = []
        self.es = ExitStack()
        self.csem = {}
        for s in ['pe', 'act', 'dve', 'pool']:
            self.csem[s] = self.es.enter_context(nc.semaphore('c_' + s))
        self.dsem = {'sp': [], 'pool': []}
        for i in range(n_dma_sems[0]):
            self.dsem['sp'].append(self.es.enter_context(nc.semaphore('d_sp%d' % i)))
        for i in range(n_dma_sems[1]):
            self.dsem['pool'].append(self.es.enter_context(nc.semaphore('d_pl%d' % i)))
        self.dcnt = {'sp': 0, 'pool': 0}
        self.dlast = {}
        self.duse = {}
        self.last = {}
        self.dma_since = []

    def sbuf(self, name, shape, dtype):
        t = self.es.enter_context(self.nc.sbuf_tensor(name, list(shape), dtype))
        return Buf(t[:], name)

    def psum(self, name, shape, dtype):
        t = self.es.enter_context(self.nc.psum_tensor(name, list(shape), dtype))
        b = Buf(t[:], name)
        b.excl = True
        return b

    def _deps_read(self, v, deps):
        b = v.buf
        if v.key is None:
            for op in b.wr.values():
                deps.add(op)
        else:
            for k in (None, v.key):
                op = b.wr.get(k)
                if op is not None:
                    deps.add(op)

    def _deps_write(self, v, deps):
        b = v.buf
        if v.key is None:
            for op in b.wr.values():
                deps.add(op)
            for d in b.rd.values():
                for op in d.values():
                    deps.add(op)
        else:
            for k in (None, v.key):
                op = b.wr.get(k)
                if op is not None:
                    deps.add(op)
                d = b.rd.get(k)
                if d:
                    for op in d.values():
                        deps.add(op)

    def add(self, stream, fn, reads=(), writes=(), dma=False, nm='', extra_deps=()):
        op = Op()
        op.stream = stream
        op.fn = fn
        op.dma = dma
        op.needed = False
        op.idx = len(self.ops)
        op.sem = None
        op.val = None
        op.nm = nm
        deps = set(extra_deps)
        for v in reads:
            self._deps_read(v, deps)
            if getattr(v.buf, 'excl', False):
                for d in v.buf.rd.values():
                    for st, o in d.items():
                        if st != stream:
                            deps.add(o)
        for v in writes:
            self._deps_write(v, deps)
        if dma:
            q = stream
            i = self.dcnt[q] % len(self.dsem[q])
            self.dcnt[q] += 1
            prev = self.dlast.get((q, i))
            if prev is not None:
                deps.add(prev)
            self.dlast[(q, i)] = op
            n = self.duse.get((q, i), 0) + 1
            self.duse[(q, i)] = n
            op.sem = self.dsem[q][i]
            op.val = 16 * n
            op.needed = True
            self.dma_since.append(op)
        if stream == 'pe' and not dma:
            deps = {d for d in deps if d.dma or d.stream != 'pe'}
        for d in deps:
            d.needed = True
        op.deps = deps
        for v in reads:
            v.buf.rd.setdefault(v.key, {})[stream if not dma else ('dma', op.idx)] = op
        for v in writes:
            b = v.buf
            if v.key is None:
                b.wr = {None: op}
                b.rd = {}
            else:
                b.wr[v.key] = op
                b.rd[v.key] = {}
        self.ops.append(op)
        if not dma:
            self.last[stream] = op
        return op

    def barrier(self):
        lasts = [o for o in self.last.values()]
        dm = list(self.dma_since)
        self.dma_since = []
        for s in STREAMS:
            deps = list(lasts) + dm
            self.add(s, None, extra_deps=deps, nm='barrier')

    def emit(self):
        nc = self.nc
        cnt = {s: 0 for s in self.csem}
        for op in self.ops:
            if not op.dma and op.needed and op.fn is not None:
                cnt[op.stream] += 1
                op.sem = self.csem[op.stream]
                op.val = cnt[op.stream]
        per = {s: [o for o in self.ops if o.stream == s] for s in STREAMS}
        final_dma = []
        for (q, i), n in self.duse.items():
            final_dma.append((self.dsem[q][i], 16 * n))

        def run(stream, eng):
            seen = {}
            for op in per[stream]:
                waits = {}
                for d in op.deps:
                    if d.sem is None:
                        continue
                    key = id(d.sem)
                    if seen.get(key, 0) >= d.val:
                        continue
                    if key not in waits or waits[key][1] < d.val:
                        waits[key] = (d.sem, d.val)
                for key, (sem, val) in waits.items():
                    eng.wait_ge(sem, val)
                    seen[key] = val
                if op.fn is None:
                    continue
                ins = op.fn(eng)
                if op.dma:
                    ins.then_inc(op.sem, 16)
                elif op.needed:
                    ins.then_inc(op.sem, 1)
            if stream == 'sp':
                for sem, val in final_dma:
                    if seen.get(id(sem), 0) < val:
                        eng.wait_ge(sem, val)

        with nc.Block() as block:
            @block.sync
            def _(e):
                run('sp', e)

            @block.gpsimd
            def _(e):
                run('pool', e)

            @block.tensor
            def _(e):
                run('pe', e)

            @block.scalar
            def _(e):
                run('act', e)

            @block.vector
            def _(e):
                run('dve', e)
        self.es.close()

    def dma(self, out, in_, q='sp', nm='dma', **kw):
        return self.add(q, lambda e: e.dma_start(out=out.ap, in_=in_.ap, **kw),
                        reads=[in_], writes=[out], dma=True, nm=nm)

    def mm(self, out, lhsT, rhs, start=True, stop=True, nm='mm'):
        return self.add('pe', lambda e: e.matmul(out.ap, lhsT.ap, rhs.ap, start=start, stop=stop),
                        reads=[lhsT, rhs], writes=[out], nm=nm)

    def transpose(self, out, in_, ident, nm='tr'):
        return self.add('pe', lambda e: e.transpose(out.ap, in_.ap, ident.ap),
                        reads=[in_, ident], writes=[out], nm=nm)

    def act(self, out, in_, func, bias=None, scale=None, accum_out=None, nm='act'):
        reads = [in_]
        kw = {}
        if bias is not None:
            if isinstance(bias, V):
                reads.append(bias)
                kw['bias'] = bias.ap
            else:
                kw['bias'] = bias
        if scale is not None:
            if isinstance(scale, V):
                reads.append(scale)
                kw['scale'] = scale.ap
            else:
                kw['scale'] = scale
        writes = [out]
        if accum_out is not None:
            writes.append(accum_out)
            kw['accum_out'] = accum_out.ap
        return self.add('act', lambda e: e.activation(out.ap, in_.ap, func, **kw),
                        reads=reads, writes=writes, nm=nm)

    def tt(self, out, in0, in1, op, eng='dve', nm='tt'):
        return self.add(eng, lambda e: e.tensor_tensor(out.ap, in0.ap, in1.ap, op),
                        reads=[in0, in1], writes=[out], nm=nm)

    def ts(self, out, in0, s1, op0, s2=None, op1=None, eng='dve', nm='ts'):
        reads = [in0]
        a1 = s1
        if isinstance(s1, V):
            reads.append(s1)
            a1 = s1.ap
        a2 = s2
        if isinstance(s2, V):
            reads.append(s2)
            a2 = s2.ap
        if op1 is None:
            return self.add(eng, lambda e: e.tensor_scalar(out.ap, in0.ap, a1, None, op0),
                            reads=reads, writes=[out], nm=nm)
        return self.add(eng, lambda e: e.tensor_scalar(out.ap, in0.ap, a1, a2, op0, op1),
                        reads=reads, writes=[out], nm=nm)

    def stt(self, out, in0, scalar, in1, op0, op1, nm='stt'):
        reads = [in0, in1]
        sc = scalar
        if isinstance(scalar, V):
            reads.append(scalar)
            sc = scalar.ap
        return self.add('dve', lambda e: e.scalar_tensor_tensor(out.ap, in0.ap, sc, in1.ap, op0, op1),
                        reads=reads, writes=[out], nm=nm)

    def copy(self, out, in_, eng='dve', nm='cp'):
        if eng == 'act':
            return self.add('act', lambda e: e.copy(out.ap, in_.ap), reads=[in_], writes=[out], nm=nm)
        return self.add(eng, lambda e: e.tensor_copy(out.ap, in_.ap), reads=[in_], writes=[out], nm=nm)

    def memset(self, out, val, eng='dve', nm='ms'):
        return self.add(eng, lambda e: e.memset(out.ap, val), reads=[], writes=[out], nm=nm)


class Arena:
    def __init__(self, P, name, nbytes):
        self.P = P
        self.nb = nbytes
        self.t = P.es.enter_context(P.nc.sbuf_tensor(name, [128, nbytes // 2], BF16))
        self.cur = 0
        self.hi = 0

    def reset(self, to=0):
        self.cur = to

    def alloc(self, name, shape, dtype):
        esz = 4 if dtype == F32 else 2
        n = 1
        for s in shape[1:]:
            n *= s
        nbytes = (n * esz + 31) // 32 * 32
        off = self.cur
        assert off + nbytes <= self.nb, "arena overflow %s: %d + %d > %d" % (name, off, nbytes, self.nb)
        self.cur += nbytes
        self.hi = max(self.hi, self.cur)
        ap = self.t[0:shape[0], off // 2:(off + n * esz) // 2]
        if dtype == F32:
            ap = ap.bitcast(F32)
        if len(shape) == 3:
            ap = ap.rearrange("p (a b) -> p a b", b=shape[2])
        elif len(shape) == 4:
            ap = ap.rearrange("p (a b c) -> p a b c", b=shape[2], c=shape[3])
        return Buf(ap, name)


def build(stop_after=None):
    nc = bass.Bass("TRN2", target_bir_lowering=False)

    def din(name, shape, dt=F32):
        return nc.dram_tensor(name, list(shape), dt, kind="ExternalInput")

    def dout(name, shape):
        return nc.dram_tensor(name, list(shape), F32, kind="ExternalOutput")

    xp_d = din("xp", [512, D])
    xo_d = din("xo", [512, D])
    xs_d = din("xs", [2048, D])
    xh_d = din("xh", [NH, D])
    cckv_d = din("cckv", [512, 256])
    ckpe_d = din("ckpe", [512, 64])
    cT_d = din("cT", [128, 32])
    wmod_d = din("w_mod", [D, 6 * D])
    bmod_d = din("b_mod2", [2, 6 * D])
    winuq_d = din("w_in_uq", [D, 1536])
    winkv_d = din("w_in_kv", [D, 320])
    wpool_d = din("w_pool", [1024, 256])
    wuq_d = din("w_uq_x", [512, 2048])
    wukv_d = din("w_ukv_x", [256, 2048])
    wout_d = din("w_out", [D, D])
    wup_d = din("w_up_x", [D, 2 * DFF])
    wdn_d = din("w_down", [DFF, D])
    vecs_d = din("vecs", [128, NV])
    rows_d = din("rows", [1, NR])
    ropek_d = din("ropek", [2048, 128])
    ropeq_d = din("ropeq", [64, 2 * NS])
    ident_d = din("ident", [128, 128], BF16)
    identf_d = din("identf", [128, 128])
    yp_d = dout("yp", [512, D])
    ys_d = dout("ys", [512, D])
    nckv_d = dout("nckv", [512, 256])
    nkpe_d = dout("nkpe", [512, 64])

    P = Prog(nc, n_dma_sems=(44, 44))

    def DB(t, name):
        return Buf(t[:], name)

    XP, XO, XS, XH = DB(xp_d, 'xp'), DB(xo_d, 'xo'), DB(xs_d, 'xs'), DB(xh_d, 'xh')
    CCKV, CKPE = DB(cckv_d, 'cckv'), DB(ckpe_d, 'ckpe')
    WMOD, BMOD = DB(wmod_d, 'wmod'), DB(bmod_d, 'bmod')
    WINUQ, WINKV, WPOOL = DB(winuq_d, 'winuq'), DB(winkv_d, 'winkv'), DB(wpool_d, 'wpool')
    WUQ, WUKV, WOUT, WUP, WDN = DB(wuq_d, 'wuq'), DB(wukv_d, 'wukv'), DB(wout_d, 'wout'), DB(wup_d, 'wup'), DB(wdn_d, 'wdn')
    ROPEK, ROPEQ = DB(ropek_d, 'ropek'), DB(ropeq_d, 'ropeq')
    YP, YS, NCKV, NKPE = DB(yp_d, 'yp'), DB(ys_d, 'ys'), DB(nckv_d, 'nckv'), DB(nkpe_d, 'nkpe')
    ROWS = DB(rows_d, 'rows')

    AR = Arena(P, "arena", 206 * 1024)
    PSB = [P.psum("psb%d" % i, [128, 512], F32) for i in range(8)]

    def psv(i, key=None):
        return V(PSB[i], key, PSB[i].ap)

    def psv16(i, key=None):
        return V(PSB[i], key, PSB[i].ap.bitcast(BF16))

    vecs = AR.alloc("vecs", [128, NV], F32)
    ident = AR.alloc("ident", [128, 128], BF16)
    identf = AR.alloc("identf", [128, 128], F32)
    onesb = AR.alloc("onesb", [128, 128], BF16)
    onesf = AR.alloc("onesf", [128, 128], F32)
    cTf = AR.alloc("cTf", [128, 32], F32)
    cTb = AR.alloc("cTb", [128, 16, 2], BF16)
    MODF = AR.alloc("MODF", [128, 96, 2], F32)
    SS = AR.alloc("SS", [128, 2, 16, 2], F32)
    kvg_b = AR.alloc("kvg_b", [128, 256], F32)
    kgp_b = AR.alloc("kgp_b", [128, 64], F32)
    stat = [AR.alloc("stat%d" % i, [128, 8], F32) for i in range(4)]
    mark_mix = AR.cur
    mixT = AR.alloc("mixT", [128, 16, NALL], BF16)
    mark_cqn = AR.cur
    cqn = AR.alloc("cqn", [128, 4, NALL], BF16)
    mark_a2 = AR.cur
    CKVT = AR.alloc("CKVT", [128, 2, NK_TOT], BF16)
    KPT = AR.alloc("KPT", [64, NK_TOT], F32)
    KPSQ = AR.alloc("KPSQ", [128, NK_TOT], BF16)
    mark_arena = AR.cur

    def vcol(c, n=1, parts=128):
        return vecs.k()[0:parts, c:c + n]

    eps_v = vcol(VC_EPS)

    P.dma(vecs.k(), V(DB(vecs_d, 'vecs_d'), None, vecs_d[:]))
    P.dma(ident.k(), V(DB(ident_d, 'ident_d'), None, ident_d[:]))
    P.dma(identf.k(), V(DB(identf_d, 'identf_d'), None, identf_d[:]))
    P.dma(cTf.k(), V(DB(cT_d, 'cT_d'), None, cT_d[:]))
    P.dma(kvg_b.k(), V(ROWS, None, rows_d[:, RW_KVG:RW_KVG + 256].partition_broadcast(128)))
    P.dma(kgp_b.k(), V(ROWS, None, rows_d[:, RW_KGP:RW_KGP + 64].partition_broadcast(128)))
    P.memset(onesb.k(), 1.0)
    P.memset(onesf.k(), 1.0, eng='pool')
    P.act(cTb.k().re("p a b -> p (a b)"), cTf.k(), AF.Silu)

    def mod_group(col0, width, wblk, b2, mrow, psA, psB_, skip_wdma=False):
        nblk = width // 128
        if not skip_wdma:
            P.dma(wblk.k()[:, :, 0:width], V(WMOD, None, wmod_d.rearrange("(kc p) n -> p kc n", p=128)[:, :, col0:col0 + width]),
                  q='pool', nm='wmod')
        P.dma(b2.k()[:, 0:width], V(BMOD, None, bmod_d[:, col0:col0 + width]))
        for kc in range(NKC):
            P.mm(psv(psA)[0:2, 0:width], cTb.k()[:, kc, :], wblk.k()[:, kc, 0:width], start=(kc == 0), stop=(kc == NKC - 1))
        P.tt(mrow.k()[:, 0:width], psv(psA)[0:2, 0:width], b2.k()[:, 0:width], ALU.add)
        for j in range(nblk):
            P.mm(psv(psB_)[:, 2 * j:2 * j + 2], mrow.k()[0:2, j * 128:(j + 1) * 128], identf.k()[0:2, 0:2])
        b0 = col0 // 128
        P.copy(MODF.k()[:, b0:b0 + nblk, :].re("p a b -> p (a b)"), psv(psB_)[:, 0:2 * nblk], eng='dve')

    def mod_finish(which):
        sc0 = 16 if which == 0 else 64
        gcol = VC_G1 if which == 0 else VC_G2
        for c in range(2):
            P.stt(SS.k()[:, which, :, c], MODF.k()[:, sc0:sc0 + 16, c], 1.0, vcol(gcol, 16), ALU.add, ALU.mult)

    AR.reset(mark_arena)
    m_wblk = [AR.alloc("m_wblk%d" % i, [128, 16, 512], BF16) for i in range(3)]
    m_b2 = [AR.alloc("m_b2_%d" % i, [2, 512], F32) for i in range(2)]
    m_row = [AR.alloc("m_row%d" % i, [2, 512], F32) for i in range(2)]
    mark_a1 = AR.cur
    for cg in range(8):
        mod_group(cg * 512, 512, m_wblk[cg % 3], m_b2[cg % 2], m_row[cg % 2], 6, 7)
    mod_finish(0)

    ctr = {'n': 0}

    def norm_stats(x_v, ntok, xsb_list):
        i = ctr['n']
        ctr['n'] += 1
        xsb = xsb_list[i % len(xsb_list)]
        st = stat[i % 4]
        P.act(xsb.k()[0:ntok, :], x_v, AF.Square, accum_out=st.k()[0:ntok, 0:1])
        P.act(st.k()[0:ntok, 1:2], st.k()[0:ntok, 0:1], AF.Sqrt, bias=eps_v[0:ntok], scale=1.0 / D)
        P.add('dve', lambda e: e.reciprocal(st.ap[0:ntok, 2:3], st.ap[0:ntok, 1:2]),
              reads=[st.k()], writes=[st.k()], nm='rc')
        P.ts(xsb.k()[0:ntok, :], x_v, st.k()[0:ntok, 2:3], ALU.mult)
        return i

    def norm_T(x_v, ntok, which, c, dst_fn, xsb_list, psT, plain_dst=None):
        i = norm_stats(x_v, ntok, xsb_list)
        norm_trans(i, ntok, which, c, dst_fn, xsb_list, psT, plain_dst)

    def norm_trans(i, ntok, which, c, dst_fn, xsb_list, psT, plain_dst=None):
        xsb = xsb_list[i % len(xsb_list)]
        boff = 0 if which == 0 else 48
        for half in range(2):
            bank = psT[(2 * i + half) % len(psT)]
            for cc in range(8):
                kc = half * 8 + cc
                P.transpose(psv16(bank)[:, cc * 128:cc * 128 + ntok], xsb.k()[0:ntok, kc * 128:(kc + 1) * 128],
                            ident.k()[0:ntok, 0:ntok])
            if plain_dst is not None:
                src = psv16(bank)[:, 0:1024].re("p (a b) -> p a b", b=128)
                P.copy(plain_dst(half), src, eng=('act' if half == 0 else 'dve'))
                continue
            for cc in range(8):
                kc = half * 8 + cc
                if half == 0:
                    P.act(dst_fn(kc), psv16(bank)[:, cc * 128:cc * 128 + ntok], AF.Identity,
                          scale=SS.k()[:, which, kc, c:c + 1], bias=MODF.k()[:, boff + kc, c:c + 1])
                else:
                    P.ts(dst_fn(kc), psv16(bank)[:, cc * 128:cc * 128 + ntok], SS.k()[:, which, kc, c:c + 1], ALU.mult,
                         MODF.k()[:, boff + kc, c:c + 1], ALU.add)

    def phase_A1():
        AR.reset(mark_arena)
        w_kv = AR.alloc("w_kv", [128, 16, 320], BF16)
        P.memset(KPSQ.k()[64:128, :], 0.0)
        a1_xt = [AR.alloc("a1_xt%d" % i, [128, D], F32) for i in range(4)]
        a1_xs = [AR.alloc("a1_xs%d" % i, [128, D], BF16) for i in range(3)]
        a1_hT = [AR.alloc("a1_hT%d" % i, [128, 16, 128], BF16) for i in range(3)]
        a1_kv = [AR.alloc("a1_kv%d" % i, [128, 320], F32) for i in range(2)]
        a1_ck = [AR.alloc("a1_ck%d" % i, [128, 256], F32) for i in range(4)]
        a1_ckb = [AR.alloc("a1_ckb%d" % i, [128, 256], BF16) for i in range(2)]
        a1_kp = [AR.alloc("a1_kp%d" % i, [128, 128], F32) for i in range(4)]
        a1_st = [AR.alloc("a1_st%d" % i, [128, 8], F32) for i in range(2)]
        a1_kg = [AR.alloc("a1_kg%d" % i, [128, 128], F32) for i in range(2)]
        a1_kr = [AR.alloc("a1_kr%d" % i, [128, 128], F32) for i in range(2)]
        for i in range(4):
            P.memset(a1_kp[i].k(), 0.0)
        for i in range(2):
            P.memset(a1_kg[i].k(), 0.0)
            P.memset(a1_kr[i].k(), 0.0)
        a1_tm = [AR.alloc("a1_tm%d" % i, [128, 64], F32) for i in range(2)]
        a1_rp = [AR.alloc("a1_rp%d" % i, [128, 128], F32) for i in range(2)]
        a1_jk = AR.alloc("a1_jk", [128, 256], BF16)
        P.dma(w_kv.k(), V(WINKV, None, winkv_d.rearrange("(kc p) n -> p kc n", p=128)), q='pool', nm='wkv')
        w_kvs = [AR.alloc("w_kvs%d" % i, [128, 16, 320], BF16) for i in range(2)]
        B1b = AR.alloc("B1b", [128, 16, 2], BF16)
        brow = AR.alloc("brow", [128, 2, 320], BF16)
        onerow = AR.alloc("onerow", [128, 128], BF16)
        P.memset(brow.k(), 0.0)
        P.memset(onerow.k(), 0.0)
        P.memset(onerow.k()[0:1, :], 1.0)
        P.copy(B1b.k(), MODF.k()[:, 0:16, :], eng='dve')
        for c_ in range(2):
            for kc in range(NKC):
                P.ts(w_kvs[c_].k()[:, kc, :], w_kv.k()[:, kc, :], SS.k()[:, 0, kc, c_:c_ + 1], ALU.mult)
            for kc in range(NKC):
                P.mm(psv(5)[0:1, 0:320], B1b.k()[:, kc, c_:c_ + 1], w_kv.k()[:, kc, :], start=(kc == 0), stop=(kc == NKC - 1))
            P.copy(brow.k()[0:1, c_, :], psv(5)[0:1, 0:320], eng='act')

        class _V64:
            def __init__(self, b):
                self.b = b

            def k(self):
                return self.b.k()[:, 0:64]

        a1_idx = {}

        def kv_stats(it, kind, src_v):
            if kind == 'x':
                xt = a1_xt[it % len(a1_xt)]
                P.dma(xt.k(), src_v, nm='ldx')
                a1_idx[it] = norm_stats(xt.k(), 128, a1_xs)
            else:
                P.dma(a1_ck[it % 4].k(), src_v[0], nm='ldc')
                P.dma(_V64(a1_kp[it % 4]).k(), src_v[1], nm='ldc')

        def kv_trans(it, kind, c):
            if kind == 'x':
                hT = a1_hT[it % 3]
                norm_trans(a1_idx[it], 128, 0, c, None, a1_xs, [0, 1, 2, 3],
                           plain_dst=lambda half: hT.k(half)[:, half * 8:(half + 1) * 8, :])

        def kv_back(it, kind, col0, rope_row0=None, out_row0=None, c=0):
            ck = a1_ck[it % 4]
            ckb = a1_ckb[it % 2]
            kp_full = a1_kp[it % 4]
            kg_full = a1_kg[it % 2]
            kp = _V64(kp_full)
            kg = _V64(kg_full)
            st = a1_st[it % 2]
            if kind == 'x':
                hT = a1_hT[it % 3]
                pk = 4 + (it % 2)
                for kc in range(NKC):
                    P.mm(psv(pk)[:, 0:320], hT.k()[:, kc, :], w_kvs[c].k()[:, kc, :], start=(kc == 0), stop=False)
                P.mm(psv(pk)[:, 0:320], onerow.k(), brow.k()[:, c, :], start=False, stop=True)
                kvb = a1_kv[it % 2]
                P.copy(kvb.k(), psv(pk)[:, 0:320], eng='dve')
                P.act(a1_jk.k(), kvb.k()[:, 0:256], AF.Square, accum_out=st.k()[:, 4:5])
                P.act(st.k()[:, 5:6], st.k()[:, 4:5], AF.Sqrt, bias=eps_v, scale=1.0 / 256)
                P.add('dve', lambda e: e.reciprocal(st.ap[:, 6:7], st.ap[:, 5:6]), reads=[st.k()], writes=[st.k()], nm='rc')
                P.stt(ck.k(), kvb.k()[:, 0:256], st.k()[:, 6:7], kvg_b.k(), ALU.mult, ALU.mult)
                P.copy(kp.k(), kvb.k()[:, 256:320], eng='pool')
                if out_row0 is not None:
                    P.dma(V(NCKV, out_row0, nckv_d[out_row0:out_row0 + 128, :]), ck.k(), nm='st_ckv')
                    P.dma(V(NKPE, out_row0, nkpe_d[out_row0:out_row0 + 128, :]), kp.k(), nm='st_kpe')
            P.copy(ckb.k(), ck.k(), eng='pool')
            P.tt(kg.k(), kp.k(), kgp_b.k(), ALU.mult, eng='pool')
            if rope_row0 is not None:
                rp = a1_rp[it % 2]
                P.dma(rp.k(), V(ROPEK, None, ropek_d[rope_row0:rope_row0 + 128, :]), nm='ldrope')
                kr_full = a1_kr[it % 2]
                kr = _V64(kr_full)
                tm = a1_tm[it % 2]
                kg4 = kg.k().re("p (a b c) -> p a b c", a=2, b=2, c=16)
                tm4 = tm.k().re("p (a b c) -> p a b c", a=2, b=2, c=16)
                sn4 = rp.k()[:, 64:128].re("p (a b c) -> p a b c", a=2, b=2, c=16)
                for b_ in range(2):
                    P.tt(tm4[:, :, b_, :], kg4[:, :, 1 - b_, :], sn4[:, :, b_, :], ALU.mult, eng='pool')
                P.tt(kr.k(), kg.k(), rp.k()[:, 0:64], ALU.mult, eng='pool')
                P.tt(kr.k(), kr.k(), tm.k(), ALU.add, eng='pool')

        def kv_back2(it, col0, rope_row0):
            ckb = a1_ckb[it % 2]
            kp_full = a1_kp[it % 4]
            kfin = a1_kr[it % 2] if rope_row0 is not None else a1_kg[it % 2]
            pt = 6
            for kc in range(2):
                P.transpose(psv16(pt)[:, kc * 128:(kc + 1) * 128], ckb.k()[:, kc * 128:(kc + 1) * 128], ident.k())
            P.copy(V(CKVT, col0, CKVT.ap[:, :, col0:col0 + 128]), psv16(pt)[:, 0:256].re("p (a b) -> p a b", b=128), eng='dve')
            pt2 = 7
            P.transpose(psv(pt2)[:, 0:128], kp_full.k(), identf.k())
            P.transpose(psv(pt2)[:, 128:256], kfin.k(), identf.k())
            P.act(V(KPSQ, col0, KPSQ.ap[0:64, col0:col0 + 128]), psv(pt2)[0:64, 0:128], AF.Square)
            P.copy(V(KPT, col0, KPT.ap[:, col0:col0 + 128]), psv(pt2)[0:64, 128:256], eng='dve')

        tiles = []
        for t in range(4):
            tiles.append(('x', V(XP, None, xp_d[t * 128:(t + 1) * 128, :]), 0, t * 128, None, t * 128))
        for t in range(4):
            tiles.append(('ctx', (V(CCKV, None, cckv_d[t * 128:(t + 1) * 128, :]), V(CKPE, None, ckpe_d[t * 128:(t + 1) * 128, :])),
                          1, 512 + t * 128, None, None))
        for t in range(16):
            tiles.append(('x', V(XS, None, xs_d[t * 128:(t + 1) * 128, :]), 1, 1024 + t * 128, t * 128, None))
        m1_wblk = [AR.alloc("m1_wblk%d" % i, [128, 16, 256], BF16) for i in range(2)]
        m1_b2 = [AR.alloc("m1_b2", [2, 256], F32)] * 2
        m1_row = [AR.alloc("m1_row", [2, 256], F32)] * 2

        def mk_mod1(gi):
            col0 = 4096 + gi * 256

            def fd():
                P.dma(m1_wblk[gi % 2].k(), V(WMOD, None, wmod_d.rearrange("(kc p) n -> p kc n", p=128)[:, :, col0:col0 + 256]),
                      q='pool', nm='wmod')

            def fc():
                mod_group(col0, 256, m1_wblk[gi % 2], m1_b2[gi % 2], m1_row[gi % 2], 6, 7, skip_wdma=True)
            return fd, fc

        mods1 = [mk_mod1(gi) for gi in range(16)]
        mq = [mods1[0][0], mods1[1][0]]
        for gi in range(16):
            mq.append(mods1[gi][1])
            if gi + 2 < 16:
                mq.append(mods1[gi + 2][0])

        nt = len(tiles)

        def stage(i):
            if 0 <= i + 3 < nt:
                kv_stats(i + 3, tiles[i + 3][0], tiles[i + 3][1])
            if 0 <= i + 2 < nt:
                kv_trans(i + 2, tiles[i + 2][0], tiles[i + 2][2])
            if 0 <= i + 1 < nt:
                kind, src, c, col0, rrow, orow = tiles[i + 1]
                kv_back(i + 1, kind, col0, rrow, orow, c)
            if 0 <= i < nt:
                kind, src, c, col0, rrow, orow = tiles[i]
                kv_back2(i, col0, rrow)

        for i in range(-3, nt):
            stage(i)
            for _ in range(2):
                if mq:
                    mq.pop(0)()
        while mq:
            mq.pop(0)()

    P.barrier()
    if stop_after == 'P0':
        P.emit()
        return nc

    AR.reset(mark_a2)
    w_in = AR.alloc("w_in", [128, 16, 1536], BF16)
    w_pool = AR.alloc("w_pool", [128, 8, 256], BF16)
    a2_xt = [AR.alloc("a2_xt%d" % i, [128, D], F32) for i in range(2)]
    a2_xs = [AR.alloc("a2_xs%d" % i, [128, D], BF16) for i in range(2)]
    hTP = AR.alloc("hTP", [128, 16, 512], BF16)
    hTS = AR.alloc("hTS", [128, 16, NS], BF16)
    hTH = AR.alloc("hTH", [128, 16, NH], BF16)
    Ub = [AR.alloc("Ub%d" % i, [128, 2 * 544], F32) for i in range(2)]
    cq = AR.alloc("cq", [128, 4, NS], F32)
    sqb = [AR.alloc("sqb%d" % i, [128, NS], BF16) for i in range(2)]
    rq = AR.alloc("rq", [128, NS], F32)
    Ta = AR.alloc("Ta", [128, 2 * 544], F32)
    Tb = AR.alloc("Tb", [128, 2 * 544], F32)
    rcb = [AR.alloc("rcb%d" % i, [128, NS], F32) for i in range(2)]
    dT = [AR.alloc("dT%d" % i, [128, 2 * NS], BF16) for i in range(2)]
    for blk in range(3):
        P.dma(w_in.k(blk)[:, :, blk * 512:(blk + 1) * 512],
              V(WINUQ, None, winuq_d.rearrange("(kc p) n -> p kc n", p=128)[:, :, blk * 512:(blk + 1) * 512]),
              q='pool', nm='w_in')
    P.dma(w_pool.k(), V(WPOOL, None, wpool_d.rearrange("(a p) n -> p a n", p=128)), q='pool', nm='w_pool')
    psrot = {'i': 0}

    def nextps(banks=(4, 5, 6, 7)):
        b = banks[psrot['i'] % len(banks)]
        psrot['i'] += 1
        return b

    xctr = {'i': 0}

    def a2_group(isP):
        cond = 0 if isP else 1
        n = 512 if isP else NS
        splits = [(0, 512)] if isP else [(0, 257), (257, 257)]
        coff = 0 if isP else NP_
        L = 272 if isP else 530
        R = 4 if isP else 2
        hT = hTP if isP else hTS
        def proj(g):
            ub = Ub[g % 2]
            P.memset(ub.k(), 0.0, eng='pool')
            for ocl in range(2):
                oc = 2 * g + ocl
                for (c0, cn) in splits:
                    pb = nextps()
                    for kc in range(NKC):
                        P.mm(psv(pb)[:, 0:cn], w_in.k(oc // 4)[:, kc, oc * 128:(oc + 1) * 128], hT.k()[:, kc, c0:c0 + cn],
                             start=(kc == 0), stop=(kc == NKC - 1))
                    if isP:
                        dst = ub.k().re("p (a b c) -> p a b c", a=2, b=2, c=272)[:, ocl, :, 8:264]
                        P.copy(dst, psv(pb)[:, 0:512].re("p (b c) -> p b c", c=256), eng='act')
                    else:
                        dst = ub.k().re("p (a c) -> p a c", a=2)[:, ocl, 8 + c0:8 + c0 + cn]
                        P.copy(dst, psv(pb)[:, 0:cn], eng='act')
                if not isP:
                    pb = nextps()
                    for kc in range(NKC):
                        P.mm(psv(pb)[:, 0:NH], w_in.k(oc // 4)[:, kc, oc * 128:(oc + 1) * 128], hTH.k()[:, kc, :],
                             start=(kc == 0), stop=(kc == NKC - 1))
                    u2 = ub.k().re("p (a c) -> p a c", a=2)
                    P.copy(u2[:, ocl, 0:8], psv(pb)[:, 0:8], eng='dve')
                    P.copy(u2[:, ocl, 522:529], psv(pb)[:, 10:17], eng='dve')
        def pool_(g):
            ub = Ub[g % 2]
            rc = rcb[g % 2]
            if isP:
                P.dma(rc.k()[:, 0:256], V(ROWS, None, rows_d[:, RW_RCP + g * 256:RW_RCP + (g + 1) * 256].partition_broadcast(128)), nm='ldrc')
            else:
                P.dma(rc.k(), V(ROWS, None, rows_d[:, RW_RCS + g * NS:RW_RCS + (g + 1) * NS].partition_broadcast(128)), nm='ldrc')
            X = ub.k().re("p (r l) -> p r l", l=272)[:, 0:R, 0:L] if isP else ub.k().re("p (r l) -> p r l", l=544)[:, 0:R, 0:L]
            TA = Ta.k().re("p (r l) -> p r l", l=272)[:, 0:R, 0:L] if isP else Ta.k().re("p (r l) -> p r l", l=544)[:, 0:R, 0:L]
            TB = Tb.k().re("p (r l) -> p r l", l=272)[:, 0:R, 0:L] if isP else Tb.k().re("p (r l) -> p r l", l=544)[:, 0:R, 0:L]
            P.tt(TA[:, :, 1:L], X[:, :, 0:L - 1], X[:, :, 1:L], ALU.add, eng='dve')
            sfin = TA
            if g >= 1:
                P.tt(TB[:, :, 2:L - 1], TA[:, :, 1:L - 2], TA[:, :, 3:L], ALU.add, eng='dve')
                sfin = TB
            if g >= 2:
                P.tt(TA[:, :, 4:L - 3], TB[:, :, 2:L - 5], TB[:, :, 6:L - 1], ALU.add, eng='dve')
                sfin = TA
            if g >= 3:
                P.tt(TB[:, :, 8:L - 7], TA[:, :, 4:L - 11], TA[:, :, 12:L - 3], ALU.add, eng='dve')
                sfin = TB
            nn = 256 if isP else NS
            rcv = V(rc, None, rc.ap[:, 0:nn].unsqueeze(1).to_broadcast([128, R, nn]))
            P.tt(sfin[:, :, 8:8 + nn], sfin[:, :, 8:8 + nn], rcv, ALU.mult)
            dt_ = dT[g % 2]
            dv = dt_.k()[:, 0:1024].re("p (r l) -> p r l", l=256) if isP else dt_.k().re("p (r l) -> p r l", l=NS)
            P.tt(dv, sfin[:, :, 8:8 + nn], X[:, :, 8:8 + nn], ALU.subtract)
        def pmm(g):
            dt_ = dT[g % 2]
            dflat = dt_.k()[:, 0:1024].re("p (a l) -> p a l", a=2) if isP else dt_.k().re("p (a l) -> p a l", a=2)
            for oc2 in range(2):
                for (c0, cn) in splits:
                    pb = nextps()
                    for kc2 in range(2):
                        P.mm(psv(pb)[:, 0:cn], w_pool.k()[:, g * 2 + kc2, oc2 * 128:(oc2 + 1) * 128], dflat[:, kc2, c0:c0 + cn],
                             start=(kc2 == 0), stop=(kc2 == 1))
                    ch = 2 * g + oc2
                    P.act(mixT.k(ch)[:, ch, coff + c0:coff + c0 + cn], psv(pb)[:, 0:cn], AF.Identity,
                          scale=vcol(VC_PSC + ch))

        def cqproj():
            for c4 in range(4):
                oc = 8 + c4
                for (c0, cn) in splits:
                    pb = nextps()
                    for kc in range(NKC):
                        P.mm(psv(pb)[:, 0:cn], w_in.k(oc // 4)[:, kc, oc * 128:(oc + 1) * 128], hT.k()[:, kc, c0:c0 + cn],
                             start=(kc == 0), stop=(kc == NKC - 1))
                    P.copy(cq.k()[:, c4, c0:c0 + cn], psv(pb)[:, 0:cn], eng='act')

        proj(0)
        proj(1)
        cqproj()
        for g in range(4):
            pool_(g)
            pmm(g)
            if g + 2 < 4:
                proj(g + 2)
        for (c0, cn) in splits:
            pb = nextps()
            for c4 in range(4):
                sq = sqb[c4 % 2]
                P.act(sq.k()[:, 0:cn], cq.k()[:, c4, c0:c0 + cn], AF.Square)
                P.mm(psv(pb)[:, 0:cn], onesb.k(), sq.k()[:, 0:cn], start=(c4 == 0), stop=(c4 == 3))
            P.act(rq.k()[:, c0:c0 + cn], psv(pb)[:, 0:cn], AF.Ln, bias=eps_v, scale=1.0 / 512)
            P.act(rq.k()[:, c0:c0 + cn], rq.k()[:, c0:c0 + cn], AF.Exp, scale=-0.5)
            for c4 in range(4):
                P.stt(cqn.k()[:, c4, coff + c0:coff + c0 + cn], cq.k()[:, c4, c0:c0 + cn], vcol(VC_QAG + c4),
                      rq.k()[:, c0:c0 + cn], ALU.mult, ALU.mult)

    ntiles = []
    for t in range(4):
        ntiles.append((V(XP, None, xp_d[t * 128:(t + 1) * 128, :]), 128, 0,
                       (lambda kc, c0=t * 128: hTP.k((c0, kc // 8))[:, kc, c0:c0 + 128]), None))

    def post_H():
        P.ts(hTH.k()[:, :, 0:9], hTH.k()[:, :, 0:9], vcol(VC_ML), ALU.mult)
        P.ts(hTH.k()[:, :, 9:17], hTH.k()[:, :, 9:17], vcol(VC_MR), ALU.mult)
        P.copy(hTS.k('h0')[:, :, 0:1], hTH.k()[:, :, 8:9], eng='dve')
        P.copy(hTS.k('h1')[:, :, 513:514], hTH.k()[:, :, 9:10], eng='dve')

    ntiles.append((V(XH, None, xh_d[:, :]), NH, 1, (lambda kc: hTH.k(kc // 8)[:, kc, :]), post_H))
    for t in range(4):
        ntiles.append((V(XO, None, xo_d[t * 128:(t + 1) * 128, :]), 128, 1,
                       (lambda kc, c0=1 + t * 128: hTS.k((c0, kc // 8))[:, kc, c0:c0 + 128]), None))

    def n_stats(j):
        src, ntok, cnd, dfn, post = ntiles[j]
        xt = a2_xt[j % 2]
        P.dma(xt.k()[0:ntok, :], src, nm='ldx')
        return norm_stats(xt.k()[0:ntok, :], ntok, a2_xs)

    idx = {0: n_stats(0)}
    for j in range(len(ntiles)):
        if j + 1 < len(ntiles):
            idx[j + 1] = n_stats(j + 1)
        src, ntok, cnd, dfn, post = ntiles[j]
        norm_trans(idx[j], ntok, 0, cnd, dfn, a2_xs, [0, 1, 2, 3])
        if post is not None:
            post()
    a2_group(True)
    a2_group(False)
    P.barrier()
    if stop_after == 'A2':
        P.emit()
        return nc
    phase_A1()
    P.barrier()
    if stop_after == 'A':
        P.emit()
        return nc

    AR.reset(mark_arena)
    w_uq = AR.alloc("w_uq", [128, 4, 2048], BF16)
    w_ukv = AR.alloc("w_ukv", [128, 2, 2048], BF16)
    Vb = AR.alloc("Vb", [128, 20, 1024], BF16)
    Kn = [AR.alloc("Kn%d" % i, [128, NKEY_S], BF16) for i in range(2)]
    Kp = [AR.alloc("Kp%d" % i, [128, NKEY_S], BF16) for i in range(2)]
    Qn = [AR.alloc("Qn%d" % i, [128, NS], BF16) for i in range(2)]
    Qp = [AR.alloc("Qp%d" % i, [128, NS], BF16) for i in range(2)]
    PT = [AR.alloc("PT%d" % i, [128, 260], BF16) for i in range(4)]
    pacc = [AR.alloc("pacc%d" % i, [128, 260], F32) for i in range(2)]
    paccB = [AR.alloc("paccB%d" % i, [128, 260], F32) for i in range(2)]
    sqn = [AR.alloc("sqn%d" % i, [128, 512], BF16) for i in range(2)]
    sqp = [AR.alloc("sqp%d" % i, [128, 260], BF16) for i in range(2)]
    Rk = [AR.alloc("Rk%d" % i, [128, 512], F32) for i in range(2)]
    Rqb = [AR.alloc("Rq%d" % i, [128, 260], F32) for i in range(2)]
    t1b = [AR.alloc("t1b%d" % i, [64, 260], F32) for i in range(1)] * 2
    t2b = [AR.alloc("t2b%d" % i, [64, 260], F32) for i in range(1)] * 2
    rden = [AR.alloc("rden%d" % i, [128, 260], F32) for i in range(2)]
    ropq = AR.alloc("ropq", [64, 2 * NS], F32)
    m3_wblk = [AR.alloc("m3_wblk%d" % i, [128, 16, 256], BF16) for i in range(2)]
    m3_b2 = [AR.alloc("m3_b2_%d" % i, [2, 256], F32) for i in range(1)] * 2
    m3_row = [AR.alloc("m3_row%d" % i, [2, 256], F32) for i in range(1)] * 2

    P.dma(w_uq.k(), V(WUQ, None, wuq_d.rearrange("(kc p) n -> p kc n", p=128)), q='pool', nm='w_uq')
    P.dma(w_ukv.k(), V(WUKV, None, wukv_d.rearrange("(kc p) n -> p kc n", p=128)), q='pool', nm='w_ukv')
    P.dma(ropq.k(), V(ROPEQ, None, ropeq_d[:, :]), nm='ropeq')
    for i_ in range(2):
        P.memset(Kp[i_].k()[64:128, :], 0.0)
        P.memset(Qp[i_].k()[64:128, :], 0.0, eng='pool')
        P.memset(sqp[i_].k()[64:128, :], 0.0)
    GENB = (0, 1, 2, 3)
    gctr = {'k': 0, 'q': 0, 'a': 0, 'pt': 0, 'v': 0}

    def build_V(kcol0, nkt):
        for kt in range(nkt):
            for half in range(2):
                pb = nextps(GENB)
                for kc in range(2):
                    P.mm(psv(pb), CKVT.k()[:, kc, kcol0 + kt * 128:kcol0 + (kt + 1) * 128],
                         w_ukv.k()[:, kc, 1024 + half * 512:1024 + (half + 1) * 512], start=(kc == 0), stop=(kc == 1))
                gctr['v'] += 1
                P.copy(Vb.k()[:, kt, half * 512:(half + 1) * 512], psv(pb), eng=('act' if gctr['v'] % 2 else 'dve'))

    def gen_chunks(h, kcol0, nkeys, qoff, splits, rope):
        hb = h % 2
        chunks = []

        def kchunk(k0, kn, j):
            pb = j % 2
            sq = sqn[j % 2]
            R = Rk[j % 2]

            def fa():
                for kc in range(2):
                    P.mm(psv(pb)[:, 0:kn], w_ukv.k()[:, kc, h * 128:(h + 1) * 128], CKVT.k()[:, kc, kcol0 + k0:kcol0 + k0 + kn],
                         start=(kc == 0), stop=(kc == 1))
                P.act(sq.k()[:, 0:kn], psv(pb)[:, 0:kn], AF.Square)

            def fb():
                pb2 = 2
                P.mm(psv(pb2)[:, 0:kn], onesb.k(), sq.k()[:, 0:kn], start=True, stop=False)
                P.mm(psv(pb2)[:, 0:kn], onesb.k(), KPSQ.k()[:, kcol0 + k0:kcol0 + k0 + kn], start=False, stop=True)
                P.act(R.k()[:, 0:kn], psv(pb2)[:, 0:kn], AF.Ln, bias=eps_v, scale=1.0 / 192)
                P.act(R.k()[:, 0:kn], R.k()[:, 0:kn], AF.Exp, scale=-0.5)
                P.stt(Kn[hb].k()[:, k0:k0 + kn], psv(pb)[:, 0:kn], vcol(VC_KGN), R.k()[:, 0:kn], ALU.mult, ALU.mult)
                P.tt(Kp[hb].k()[0:64, k0:k0 + kn], KPT.k()[0:64, kcol0 + k0:kcol0 + k0 + kn], R.k()[0:64, 0:kn], ALU.mult)
            return fa, fb

        def qchunk(c0, cn):
            def f():
                i = gctr['q']
                gctr['q'] += 1
                qa = qoff + c0
                pbn = nextps(GENB)
                for kc in range(4):
                    P.mm(psv(pbn)[:, 0:cn], w_uq.k()[:, kc, h * 256:h * 256 + 128], cqn.k()[:, kc, qa:qa + cn],
                         start=(kc == 0), stop=(kc == 3))
                pbp = nextps(GENB)
                for kc in range(4):
                    P.mm(psv(pbp)[:, 0:cn], w_uq.k()[:, kc, h * 256 + 128:h * 256 + 256], cqn.k()[:, kc, qa:qa + cn],
                         start=(kc == 0), stop=(kc == 3))
                if rope:
                    pbr = nextps(GENB)
                    for kc in range(4):
                        P.mm(psv(pbr)[0:64, 0:cn], w_uq.k()[:, kc, h * 256 + 192:h * 256 + 256], cqn.k()[:, kc, qa:qa + cn],
                             start=(kc == 0), stop=(kc == 3))
                s1 = sqn[i % 2]
                s2 = sqp[i % 2]
                P.act(s1.k()[:, 0:cn], psv(pbn)[:, 0:cn], AF.Square)
                P.act(s2.k()[0:64, 0:cn], psv(pbp)[0:64, 0:cn], AF.Square)
                pss = nextps(GENB)
                P.mm(psv(pss)[:, 0:cn], onesb.k(), s1.k()[:, 0:cn], start=True, stop=False)
                P.mm(psv(pss)[:, 0:cn], onesb.k(), s2.k()[:, 0:cn], start=False, stop=True)
                R = Rqb[i % 2]
                P.act(R.k()[:, 0:cn], psv(pss)[:, 0:cn], AF.Ln, bias=eps_v, scale=1.0 / 192)
                P.act(R.k()[:, 0:cn], R.k()[:, 0:cn], AF.Exp, scale=-0.5)
                P.stt(Qn[hb].k()[:, c0:c0 + cn], psv(pbn)[:, 0:cn], vcol(VC_QGN), R.k()[:, 0:cn], ALU.mult, ALU.mult)
                if not rope:
                    P.stt(Qp[hb].k()[0:64, c0:c0 + cn], psv(pbp)[0:64, 0:cn], vcol(VC_QGP, 1, 64), R.k()[0:64, 0:cn], ALU.mult, ALU.mult)
                else:
                    t1 = t1b[i % 2]
                    t2 = t2b[i % 2]
                    P.stt(t1.k()[:, 0:cn], psv(pbp)[0:64, 0:cn], vcol(VC_QGP, 1, 64), ropq.k()[:, c0:c0 + cn], ALU.mult, ALU.mult)
                    P.stt(t2.k()[:, 0:cn], psv(pbr)[0:64, 0:cn], vcol(VC_QGPP, 1, 64), ropq.k()[:, NS + c0:NS + c0 + cn], ALU.mult, ALU.mult)
                    P.tt(t1.k()[:, 0:cn], t1.k()[:, 0:cn], t2.k()[:, 0:cn], ALU.add, eng='pool')
                    P.tt(Qp[hb].k()[0:64, c0:c0 + cn], t1.k()[:, 0:cn], R.k()[0:64, 0:cn], ALU.mult)
            return f

        for (c0, cn) in splits:
            chunks.append(qchunk(c0, cn))
        k0 = 0
        j = 0
        while k0 < nkeys:
            kn = min(512, nkeys - k0)
            fa, fb = kchunk(k0, kn, j)
            chunks.append(fa)
            chunks.append(fb)
            k0 += kn
            j += 1
        return chunks

    def attend(h, nkt, splits, mcol0, pending):
        hb = h % 2
        it_ = 0
        for (c0, cn) in splits:
            ai = gctr['a']
            gctr['a'] += 1
            pob = 6 + (ai % 2)
            pa = pacc[ai % 2]
            pb_ = paccB[ai % 2]
            def score(kt):
                sb = nextps((4, 5))
                P.mm(psv(sb)[:, 0:cn], Kn[hb].k()[:, kt * 128:(kt + 1) * 128], Qn[hb].k()[:, c0:c0 + cn], start=True, stop=False)
                P.mm(psv(sb)[:, 0:cn], Kp[hb].k()[:, kt * 128:(kt + 1) * 128], Qp[hb].k()[:, c0:c0 + cn], start=False, stop=True)
                pt = PT[gctr['pt'] % 4]
                gctr['pt'] += 1
                P.act(pt.k()[:, 0:cn], psv(sb)[:, 0:cn], AF.Exp, scale=ATTN_SCALE)
                return pt

            pts = {0: score(0)}
            for kt in range(nkt):
                if kt + 1 < nkt:
                    pts[kt + 1] = score(kt + 1)
                pt = pts.pop(kt)
                P.mm(psv(pob)[:, 0:cn], Vb.k()[:, kt, h * 128:(h + 1) * 128], pt.k()[:, 0:cn], start=(kt == 0), stop=(kt == nkt - 1))
                acc, eng_ = (pa, 'pool') if kt % 2 == 0 else (pb_, 'dve')
                if kt < 2:
                    P.copy(acc.k()[:, 0:cn], pt.k()[:, 0:cn], eng=eng_)
                else:
                    P.tt(acc.k()[:, 0:cn], acc.k()[:, 0:cn], pt.k()[:, 0:cn], ALU.add, eng=eng_)
                it_ += 1
                if pending and it_ % 2 == 0:
                    pending.pop(0)()
            pdb = nextps((2, 3))
            P.mm(psv(pdb)[:, 0:cn], onesf.k(), pa.k()[:, 0:cn], start=True, stop=False)
            P.mm(psv(pdb)[:, 0:cn], onesf.k(), pb_.k()[:, 0:cn], start=False, stop=True)
            rd = rden[ai % 2]
            P.act(rd.k()[:, 0:cn], psv(pdb)[:, 0:cn], AF.Ln)
            P.act(rd.k()[:, 0:cn], rd.k()[:, 0:cn], AF.Exp, scale=-1.0)
            P.tt(mixT.k(8 + h)[:, 8 + h, mcol0 + c0:mcol0 + c0 + cn], psv(pob)[:, 0:cn], rd.k()[:, 0:cn], ALU.mult)
        while pending:
            pending.pop(0)()

    build_V(0, 2)

    def build_V_tiles(kcol0, t0):
        for kt in range(2):
            for half in range(2):
                pb = nextps(GENB)
                for kc in range(2):
                    P.mm(psv(pb), CKVT.k()[:, kc, kcol0 + kt * 128:kcol0 + (kt + 1) * 128],
                         w_ukv.k()[:, kc, 1024 + half * 512:1024 + (half + 1) * 512], start=(kc == 0), stop=(kc == 1))
                P.copy(Vb.k()[:, t0 + kt, half * 512:(half + 1) * 512], psv(pb), eng=('act' if half else 'dve'))

    build_V_tiles(256, 2)

    def prompt_stream(s_, kcol0, qoff):
        B = (4 * s_, 4 * s_ + 1, 4 * s_ + 2, 4 * s_ + 3)
        st_ = []
        for h in range(8):
            def Q1(h=h):
                for kc in range(4):
                    P.mm(psv(B[0])[:, 0:256], w_uq.k()[:, kc, h * 256:h * 256 + 128], cqn.k()[:, kc, qoff:qoff + 256],
                         start=(kc == 0), stop=(kc == 3))
                for kc in range(4):
                    P.mm(psv(B[1])[:, 0:256], w_uq.k()[:, kc, h * 256 + 128:h * 256 + 256], cqn.k()[:, kc, qoff:qoff + 256],
                         start=(kc == 0), stop=(kc == 3))
                P.act(sqn[s_].k()[:, 0:256], psv(B[0])[:, 0:256], AF.Square)
                P.act(sqp[s_].k()[0:64, 0:256], psv(B[1])[0:64, 0:256], AF.Square)

            def Q2(h=h):
                P.mm(psv(B[2])[:, 0:256], onesb.k(), sqn[s_].k()[:, 0:256], start=True, stop=False)
                P.mm(psv(B[2])[:, 0:256], onesb.k(), sqp[s_].k()[:, 0:256], start=False, stop=True)
                R = Rqb[s_]
                P.act(R.k()[:, 0:256], psv(B[2])[:, 0:256], AF.Ln, bias=eps_v, scale=1.0 / 192)
                P.act(R.k()[:, 0:256], R.k()[:, 0:256], AF.Exp, scale=-0.5)
                P.stt(Qn[s_].k()[:, 0:256], psv(B[0])[:, 0:256], vcol(VC_QGN), R.k()[:, 0:256], ALU.mult, ALU.mult)
                P.stt(Qp[s_].k()[0:64, 0:256], psv(B[1])[0:64, 0:256], vcol(VC_QGP, 1, 64), R.k()[0:64, 0:256], ALU.mult, ALU.mult)

            def K1(h=h):
                for kc in range(2):
                    P.mm(psv(B[3])[:, 0:256], w_ukv.k()[:, kc, h * 128:(h + 1) * 128], CKVT.k()[:, kc, kcol0:kcol0 + 256],
                         start=(kc == 0), stop=(kc == 1))
                P.act(sqn[s_].k()[:, 256:512], psv(B[3])[:, 0:256], AF.Square)

            def K2(h=h):
                P.mm(psv(B[2])[:, 256:512], onesb.k(), sqn[s_].k()[:, 256:512], start=True, stop=False)
                P.mm(psv(B[2])[:, 256:512], onesb.k(), KPSQ.k()[:, kcol0:kcol0 + 256], start=False, stop=True)
                R = Rk[s_]
                P.act(R.k()[:, 0:256], psv(B[2])[:, 256:512], AF.Ln, bias=eps_v, scale=1.0 / 192)
                P.act(R.k()[:, 0:256], R.k()[:, 0:256], AF.Exp, scale=-0.5)
                P.stt(Kn[s_].k()[:, 0:256], psv(B[3])[:, 0:256], vcol(VC_KGN), R.k()[:, 0:256], ALU.mult, ALU.mult)
                P.tt(Kp[s_].k()[0:64, 0:256], KPT.k()[0:64, kcol0:kcol0 + 256], R.k()[0:64, 0:256], ALU.mult)

            def A1_(h=h):
                for kt in range(2):
                    P.mm(psv(B[0])[:, kt * 256:(kt + 1) * 256], Kn[s_].k()[:, kt * 128:(kt + 1) * 128], Qn[s_].k()[:, 0:256],
                         start=True, stop=False)
                    P.mm(psv(B[0])[:, kt * 256:(kt + 1) * 256], Kp[s_].k()[:, kt * 128:(kt + 1) * 128], Qp[s_].k()[:, 0:256],
                         start=False, stop=True)
                for kt in range(2):
                    P.act(PT[2 * s_ + kt].k()[:, 0:256], psv(B[0])[:, kt * 256:(kt + 1) * 256], AF.Exp, scale=ATTN_SCALE)

            def A2_(h=h):
                for kt in range(2):
                    P.mm(psv(B[1])[:, 0:256], Vb.k()[:, 2 * s_ + kt, h * 128:(h + 1) * 128], PT[2 * s_ + kt].k()[:, 0:256],
                         start=(kt == 0), stop=(kt == 1))
                P.tt(pacc[s_].k()[:, 0:256], PT[2 * s_].k()[:, 0:256], PT[2 * s_ + 1].k()[:, 0:256], ALU.add, eng='pool')
                P.mm(psv(B[1])[:, 256:512], onesf.k(), pacc[s_].k()[:, 0:256])
                rd = rden[s_]
                P.act(rd.k()[:, 0:256], psv(B[1])[:, 256:512], AF.Ln)
                P.act(rd.k()[:, 0:256], rd.k()[:, 0:256], AF.Exp, scale=-1.0)
                P.tt(mixT.k(8 + h)[:, 8 + h, qoff:qoff + 256], psv(B[1])[:, 0:256], rd.k()[:, 0:256], ALU.mult)

            st_ += [Q1, Q2, K1, K2, A1_, A2_]
        return st_

    ps0 = prompt_stream(0, 0, 0)
    ps1 = prompt_stream(1, 256, 256)
    for a_, b_ in zip(ps0, ps1):
        a_()
        b_()

    keysets = [
        (512, NKEY_S, 512, [(0, 257), (257, 257)], True),
    ]
    mod_chunks = []

    def mk_mod(gi):
        col0 = 4096 + gi * 256

        def fd():
            P.dma(m3_wblk[gi % 2].k(), V(WMOD, None, wmod_d.rearrange("(kc p) n -> p kc n", p=128)[:, :, col0:col0 + 256]),
                  q='pool', nm='wmod')

        def fc():
            mod_group(col0, 256, m3_wblk[gi % 2], m3_b2[gi % 2], m3_row[gi % 2], 3, 2, skip_wdma=True)
        return fd, fc

    mods = [mk_mod(gi) for gi in range(16, 32)]
    mod_chunks.append(mods[0][0])
    mod_chunks.append(mods[1][0])
    for gi in range(16):
        mod_chunks.append(mods[gi][1])
        if gi + 2 < 16:
            mod_chunks.append(mods[gi + 2][0])
    for (kcol0, nkeys, qoff, splits, rope) in keysets:
        nkt = nkeys // 128
        build_V(kcol0, nkt)
        for f in gen_chunks(0, kcol0, nkeys, qoff, splits, rope):
            f()
        for h in range(8):
            pending = gen_chunks(h + 1, kcol0, nkeys, qoff, splits, rope) if h < 7 else []
            if rope:
                for _ in range(4):
                    if mod_chunks:
                        pending.append(mod_chunks.pop(0))
            attend(h, nkt, splits, qoff, pending)
    while mod_chunks:
        mod_chunks.pop(0)()
    mod_finish(1)
    P.barrier()
    if stop_after == 'ATT':
        P.emit()
        return nc

    def build_gate(GB, kind, Dt, conds=(0, 1)):
        for ci, c in enumerate(conds):
            for kc in range(NKC):
                d = Dt[kc % 2]
                P.ts(d.k(), identf.k(), MODF.k()[:, kind * 16 + kc, c:c + 1], ALU.mult)
                if kc % 4 == 0:
                    pb = nextps(GENB)
                P.mm(psv(pb)[:, (kc % 4) * 128:(kc % 4 + 1) * 128], onesf.k(), d.k())
                if kc % 4 == 3:
                    P.copy(GB.k()[:, ci, (kc // 4) * 512:(kc // 4 + 1) * 512], psv(pb), eng='act')

    AR.reset(mark_cqn)
    h2T = AR.alloc("h2T", [128, 16, NALL], BF16)
    assert AR.cur <= mark_arena
    AR.reset(mark_arena)
    w_out = AR.alloc("w_out", [128, 16, 2048], BF16)
    GB1 = AR.alloc("GB1", [128, 1, 2048], F32)
    wo_xt = [AR.alloc("wo_xt%d" % i, [128, D], F32) for i in range(2)]
    wo_x1 = [AR.alloc("wo_x1%d" % i, [128, D], F32) for i in range(3)]
    wo_xs = [AR.alloc("wo_xs%d" % i, [128, D], BF16) for i in range(3)]
    Dt = [AR.alloc("Dt%d" % i, [128, 128], F32) for i in range(2)]
    mixH = AR.alloc("mixH", [128, 16, 2], BF16)
    hh = AR.alloc("hh", [128, 16, 2], BF16)
    for blk in range(4):
        P.dma(w_out.k(blk)[:, :, blk * 512:(blk + 1) * 512],
              V(WOUT, None, wout_d.rearrange("(kc p) n -> p kc n", p=128)[:, :, blk * 512:(blk + 1) * 512]),
              q='pool', nm='w_out')
    build_gate(GB1, 2, Dt, conds=(0,))
    P.copy(mixH.k()[:, :, 0:1], mixT.k()[:, :, 512:513], eng='dve')
    P.copy(mixH.k()[:, :, 1:2], mixT.k()[:, :, 1025:1026], eng='dve')
    wtiles = []
    for t in range(4):
        wtiles.append(('P', t))
    for t in range(4):
        wtiles.append(('S', t))
    wtiles.append(('H', 0))
    def wo_front(wi):
        kind, t = wtiles[wi]
        cnd = 0 if kind == 'P' else 1
        ntok = 2 if kind == 'H' else 128
        xt = wo_xt[wi % 2]
        x1 = wo_x1[wi % 3]
        mc0 = 0
        if wi == 4:
            build_gate(GB1, 2, Dt, conds=(1,))
        if kind == 'P':
            P.dma(xt.k(), V(XP, None, xp_d[t * 128:(t + 1) * 128, :]), nm='ldx')
            mc0 = t * 128
        elif kind == 'S':
            P.dma(xt.k(), V(XO, None, xo_d[t * 128:(t + 1) * 128, :]), nm='ldx')
            mc0 = 512 + 1 + t * 128
        else:
            P.dma(xt.k()[0:2, :], V(XH, None, xh_d[8:10, :]), nm='ldx')
        for cg in range(4):
            pb = nextps((4, 5, 6, 7))
            for kc in range(NKC):
                lhsT = mixH.k()[:, kc, :] if kind == 'H' else mixT.k()[:, kc, mc0:mc0 + 128]
                P.mm(psv(pb)[0:ntok, :], lhsT, w_out.k(cg)[:, kc, cg * 512:(cg + 1) * 512], start=(kc == 0), stop=(kc == NKC - 1))
            P.tt(x1.k()[0:ntok, cg * 512:(cg + 1) * 512], psv(pb)[0:ntok, :], GB1.k()[0:ntok, 0, cg * 512:(cg + 1) * 512], ALU.mult)
            P.tt(x1.k()[0:ntok, cg * 512:(cg + 1) * 512], x1.k()[0:ntok, cg * 512:(cg + 1) * 512],
                 xt.k()[0:ntok, cg * 512:(cg + 1) * 512], ALU.add)
        if kind == 'P':
            P.dma(V(YP, t, yp_d[t * 128:(t + 1) * 128, :]), x1.k(), nm='st_x1')
        elif kind == 'S':
            P.dma(V(YS, t, ys_d[t * 128:(t + 1) * 128, :]), x1.k(), nm='st_x1')

    wo_idx = {}

    def wo_stats(wi):
        kind, t = wtiles[wi]
        ntok = 2 if kind == 'H' else 128
        x1 = wo_x1[wi % 3]
        wo_idx[wi] = norm_stats(x1.k()[0:ntok, :], ntok, wo_xs)

    def wo_back(wi):
        kind, t = wtiles[wi]
        cnd = 0 if kind == 'P' else 1
        i_ = wo_idx[wi]
        if kind == 'P':
            norm_trans(i_, 128, 1, cnd, lambda kc, c0=t * 128: h2T.k((c0, kc // 8))[:, kc, c0:c0 + 128], wo_xs, [0, 1, 2, 3])
        elif kind == 'S':
            mc0 = 512 + 1 + t * 128
            norm_trans(i_, 128, 1, cnd, lambda kc, c0=mc0: h2T.k((c0, kc // 8))[:, kc, c0:c0 + 128], wo_xs, [0, 1, 2, 3])
        else:
            norm_trans(i_, 2, 1, cnd, lambda kc: hh.k(kc // 8)[:, kc, :], wo_xs, [0, 1, 2, 3])
            P.ts(h2T.k()[:, :, 512:513], hh.k()[:, :, 0:1], vcol(VC_ML), ALU.mult)
            P.ts(h2T.k()[:, :, 1025:1026], hh.k()[:, :, 1:2], vcol(VC_MR), ALU.mult)

    for j in range(2):
        wo_front(j)
        wo_stats(j)
    for wi in range(len(wtiles)):
        if wi + 2 < len(wtiles):
            wo_front(wi + 2)
            wo_stats(wi + 2)
        wo_back(wi)
    P.barrier()
    if stop_after == 'WOUT':
        P.emit()
        return nc

    AR.reset(mark_mix)
    wub = [AR.alloc("wub%d" % i, [128, 16, 512], BF16) for i in range(2)]
    assert AR.cur <= mark_cqn
    AR.reset(mark_arena)
    gT = AR.alloc("gT", [128, NJ, NALL], BF16)
    mark_fdn = AR.cur
    upb = [AR.alloc("upb%d" % i, [128, NALL], F32) for i in range(2)]
    zb = [AR.alloc("zb%d" % i, [128, NALL], F32) for i in range(2)]
    sab = [AR.alloc("sab%d" % i, [128, NALL], F32) for i in range(2)]
    usplits = [(0, 512), (512, 257), (769, 257)]
    cix = 0
    def ld_wup(jj):
        P.dma(wub[jj % 2].k(), V(WUP, None, wup_d.rearrange("(kc p) n -> p kc n", p=128)[:, :, jj * 512:(jj + 1) * 512]),
              q='pool', nm='w_up')

    ld_wup(0)
    for jj in range(22):
        wb = wub[jj % 2]
        if jj + 1 < 22:
            ld_wup(jj + 1)
        for q4 in range(4):
            ci = 4 * jj + q4
            up = upb[cix % 2]
            z = zb[cix % 2]
            cix += 1
            for (c0, cn) in usplits:
                pb = nextps((0, 1, 2, 3, 4, 5, 6, 7))
                for kc in range(NKC):
                    P.mm(psv(pb)[:, 0:cn], wb.k()[:, kc, q4 * 128:(q4 + 1) * 128], h2T.k()[:, kc, c0:c0 + cn],
                         start=(kc == 0), stop=(kc == NKC - 1))
                P.copy(up.k()[:, c0:c0 + cn], psv(pb)[:, 0:cn], eng='act')
            w0 = vcol(VC_CW + ci)
            w1 = vcol(VC_CW + 88 + ci)
            w2 = vcol(VC_CW + 176 + ci)
            bb = vcol(VC_CB + ci)
            P.ts(z.k(), up.k(), w1, ALU.mult, bb, ALU.add, eng='pool')
            zP = z.k()[:, 0:512].re("p (s l) -> p s l", l=256)
            uP = up.k()[:, 0:512].re("p (s l) -> p s l", l=256)
            zS = z.k()[:, 512:NALL]
            uS = up.k()[:, 512:NALL]
            P.stt(zP[:, :, 1:256], uP[:, :, 0:255], w0, zP[:, :, 1:256], ALU.mult, ALU.add)
            P.stt(zS[:, 1:NS], uS[:, 0:NS - 1], w0, zS[:, 1:NS], ALU.mult, ALU.add)
            P.stt(zP[:, :, 0:255], uP[:, :, 1:256], w2, zP[:, :, 0:255], ALU.mult, ALU.add)
            P.stt(zS[:, 0:NS - 1], uS[:, 1:NS], w2, zS[:, 0:NS - 1], ALU.mult, ALU.add)
            if q4 % 2 == 0:
                sa = sab[(ci // 2) % 2]
                P.act(sa.k(), z.k(), AF.Silu)
            else:
                j = 2 * jj + q4 // 2
                sa = sab[(ci // 2) % 2]
                P.tt(gT.k(j)[:, j, :], sa.k(), z.k(), ALU.mult)
    P.barrier()
    if stop_after == 'FUP':
        P.emit()
        return nc

    AR.reset(mark_mix)
    wdb0 = AR.alloc("wdb0", [128, NJ, 256], BF16)
    assert AR.cur <= mark_cqn
    AR.reset(mark_cqn)
    GB2 = AR.alloc("GB2", [128, 2, 2048], F32)
    Dt2 = [AR.alloc("Dt2_%d" % i, [128, 128], F32) for i in range(2)]
    x1s = [AR.alloc("x1s%d" % i, [128, 256], F32) for i in range(4)]
    ot = [AR.alloc("ot%d" % i, [128, 256], F32) for i in range(4)]
    assert AR.cur <= mark_arena
    AR.reset(mark_fdn)
    wdb1 = AR.alloc("wdb1", [128, NJ, 256], BF16)
    wdb = [wdb0, wdb1]
    build_gate(GB2, 5, Dt2)
    oi = 0
    def ld_wdn(cg):
        P.dma(wdb[cg % 2].k(), V(WDN, None, wdn_d.rearrange("(kc p) n -> p kc n", p=128)[:, :, cg * 256:(cg + 1) * 256]),
              q='pool', nm='w_dn')

    ld_wdn(0)
    for cg in range(8):
        wd = wdb[cg % 2]
        if cg + 1 < 8:
            ld_wdn(cg + 1)
        for ti in range(8):
            isP = ti < 4
            t = ti % 4
            cnd = 0 if isP else 1
            gc0 = t * 128 if isP else 512 + 1 + t * 128
            YB, y_d = (YP, yp_d) if isP else (YS, ys_d)
            xs_ = x1s[oi % 4]
            o = ot[oi % 4]
            oi += 1
            P.dma(xs_.k(), V(YB, (t, cg), y_d[t * 128:(t + 1) * 128, cg * 256:(cg + 1) * 256]), nm='ld_x1')
            pb = nextps((0, 1, 2, 3, 4, 5, 6, 7))
            for kc in range(NJ):
                P.mm(psv(pb)[:, 0:256], gT.k()[:, kc, gc0:gc0 + 128], wd.k()[:, kc, :], start=(kc == 0), stop=(kc == NJ - 1))
            P.tt(o.k(), psv(pb)[:, 0:256], GB2.k()[:, cnd, cg * 256:(cg + 1) * 256], ALU.mult)
            P.tt(o.k(), o.k(), xs_.k(), ALU.add, eng='pool')
            P.dma(V(YB, (t, cg), y_d[t * 128:(t + 1) * 128, cg * 256:(cg + 1) * 256]), o.k(), nm='st_y')
    P.emit()
    return nc


def _rope_tables(pos):
    pos = np.asarray(pos)
    row = (pos // 64).astype(np.float32)
    col = (pos % 64).astype(np.float32)
    inv = (10000.0 ** (-np.arange(16, dtype=np.float32) / 16)).astype(np.float32)
    ar = row[:, None] * inv
    ac = col[:, None] * inv
    ang = np.concatenate([ar, ar, ac, ac], axis=-1).astype(np.float32)
    return np.cos(ang).astype(np.float32), (np.sin(ang).astype(np.float32) * SGN[None, :]).astype(np.float32)


def _rc_table(pos, T):
    out = np.zeros((4, len(pos)), np.float32)
    for g, w in enumerate(POOL_W):
        lo = np.clip(pos - w // 2, 0, T - 1)
        hi = np.clip(pos - w // 2 + w - 1, 0, T - 1)
        out[g] = 1.0 / np.maximum(hi - lo + 1, 1).astype(np.float32)
    return out


def prep_inputs(inp):
    f = np.float32
    g = lambda k: np.asarray(inp[k], dtype=f)
    x_prompt, x_sample = g('x_prompt'), g('x_sample')
    cache_ckv, cache_kpe, c, c_ctx = g('cache_ckv'), g('cache_kpe'), g('c'), g('c_ctx')
    w_in = g('w_in')[0]
    w_uq = g('w_uq')[0]
    w_ukv = g('w_ukv')[0]
    w_up = g('w_up')[0]
    conv_w, conv_b = g('conv_w')[0], g('conv_b')[0]
    qg, kg = g('q_norm_g')[0], g('k_norm_g')[0]
    shared = {}
    shared['w_mod'] = np.ascontiguousarray(g('w_mod')[0])
    shared['b_mod2'] = np.ascontiguousarray(np.broadcast_to(g('b_mod')[0][None, :], (2, 6 * D)))
    shared['w_in_uq'] = np.ascontiguousarray(w_in[:, :1536])
    shared['w_in_kv'] = np.ascontiguousarray(w_in[:, 1536:1856])
    shared['w_pool'] = np.ascontiguousarray(g('w_pool')[0].reshape(1024, 256))
    cols = []
    for h in range(8):
        b0 = h * 192
        cols += list(range(b0, b0 + 128)) + list(range(b0 + 128, b0 + 192)) + list(b0 + 128 + PERM)
    shared['w_uq_x'] = np.ascontiguousarray(w_uq[:, cols])
    kc_, vc_ = [], []
    for h in range(8):
        kc_ += list(range(h * 256, h * 256 + 128))
        vc_ += list(range(h * 256 + 128, h * 256 + 256))
    shared['w_ukv_x'] = np.ascontiguousarray(w_ukv[:, kc_ + vc_])
    shared['w_out'] = np.ascontiguousarray(g('w_out')[0])
    chunk_col = []
    for jj in range(22):
        for j in (2 * jj, 2 * jj + 1):
            chunk_col += [j * 128, DFF + j * 128]
    upcols = np.concatenate([np.arange(c0, c0 + 128) for c0 in chunk_col])
    shared['w_up_x'] = np.ascontiguousarray(w_up[:, upcols])
    shared['w_down'] = np.ascontiguousarray(g('w_down')[0])
    shared['ident'] = np.eye(128, dtype=f).astype(ml_dtypes.bfloat16)
    shared['identf'] = np.eye(128, dtype=f)
    rck, rsk = _rope_tables(np.arange(2048))
    shared['ropek'] = np.ascontiguousarray(np.concatenate([rck, rsk], axis=1))

    def fm(v, nch):
        return np.asarray(v, f).reshape(nch, 128).T

    vecs = np.zeros((128, NV), f)
    vecs[:, VC_G1:VC_G1 + 16] = fm(g('norm1_g')[0], 16)
    vecs[:, VC_G2:VC_G2 + 16] = fm(g('norm2_g')[0], 16)
    vecs[:, VC_PSC:VC_PSC + 8] = fm(g('pool_scale')[0], 8)
    vecs[:, VC_QAG:VC_QAG + 4] = fm(g('q_a_g')[0], 4)
    cwp = conv_w[:, upcols]
    cbp = conv_b[upcols]
    for tap in range(3):
        vecs[:, VC_CW + tap * 88:VC_CW + (tap + 1) * 88] = fm(cwp[tap], 88)
    vecs[:, VC_CB:VC_CB + 88] = fm(cbp, 88)
    vecs[:, VC_QGN] = qg[:128]
    vecs[:, VC_KGN] = kg[:128]
    vecs[:64, VC_QGP] = qg[128:192]
    vecs[:64, VC_QGPP] = qg[128:192][PERM]
    vecs[:, VC_EPS] = EPS
    vecs[:, VC_ONE] = 1.0
    rcP = _rc_table(np.arange(256), 256)
    in_maps = []
    for core in range(8):
        b = core // 4
        qd = core % 4
        s0 = qd * 512
        m = dict(shared)
        m['xp'] = np.ascontiguousarray(x_prompt[2 * core:2 * core + 2].reshape(512, D))
        m['xo'] = np.ascontiguousarray(x_sample[b, s0:s0 + 512])
        m['xs'] = np.ascontiguousarray(x_sample[b])
        hidx = np.concatenate([np.arange(s0 - 9, s0), np.arange(s0 + 512, s0 + 520)])
        m['xh'] = np.ascontiguousarray(x_sample[b, np.clip(hidx, 0, 2047)])
        m['cckv'] = np.ascontiguousarray(cache_ckv[b, 0])
        m['ckpe'] = np.ascontiguousarray(cache_kpe[b, 0])
        cc = np.stack([c_ctx, c[b]], axis=0)
        m['cT'] = np.ascontiguousarray(cc.reshape(2, 16, 128).transpose(2, 1, 0).reshape(128, 32))
        v = vecs.copy()
        v[:, VC_ML] = 0.0 if qd == 0 else 1.0
        v[:, VC_MR] = 0.0 if qd == 3 else 1.0
        m['vecs'] = v
        sxpos = np.arange(s0 - 1, s0 + 513)
        rows = np.zeros((1, NR), f)
        rows[0, RW_KVG:RW_KVG + 256] = g('kv_a_g')[0]
        rows[0, RW_KGP:RW_KGP + 64] = kg[128:192]
        rows[0, RW_RCP:RW_RCP + 1024] = rcP.reshape(-1)
        rows[0, RW_RCS:RW_RCS + 4 * NS] = _rc_table(sxpos, 2048).reshape(-1)
        m['rows'] = rows
        qc, qs = _rope_tables(np.clip(sxpos, 0, 2047))
        m['ropeq'] = np.ascontiguousarray(np.concatenate([qc.T, qs.T], axis=1))
        in_maps.append(m)
    return in_maps


_NC_CACHE = {}


def kernel(**inputs):
    in_maps = prep_inputs(inputs)
    if 'nc' not in _NC_CACHE:
        _NC_CACHE['nc'] = build()
    nc = _NC_CACHE['nc']
    res = run_bass_kernel_spmd(nc, in_maps, core_ids=list(range(8)))
    r = res.results
    yp = np.stack([r[c]['yp'] for c in range(8)], 0).reshape(16, 256, D).astype(np.float32)
    ys = np.stack([r[c]['ys'] for c in range(8)], 0).reshape(2, 2048, D).astype(np.float32)
    nckv = np.stack([r[c]['nckv'] for c in range(8)], 0).reshape(16, 1, 256, 256).astype(np.float32)
    nkpe = np.stack([r[c]['nkpe'] for c in range(8)], 0).reshape(16, 1, 256, 64).astype(np.float32)
    return (yp, ys, nckv, nkpe)
```

```python
import numpy as np
import ml_dtypes
import concourse.bass as bass
import concourse.mybir as mybir
from concourse.bass_utils import run_bass_kernel_spmd
from contextlib import ExitStack

F32 = mybir.dt.float32
BF16 = mybir.dt.bfloat16
AF = mybir.ActivationFunctionType
ALU = mybir.AluOpType

STREAMS = ['pe', 'act', 'dve', 'pool', 'sp']

D = 2048
NKC = 16
DFF = 5632
NJ = 44
EPS = 1e-6
ATTN_SCALE = 192 ** -0.5
NP_ = 512
NS = 514
NH = 17
NALL = NP_ + NS
NKEY_S = 2560
NK_TOT = 512 + NKEY_S
PERM = np.concatenate([np.arange(16, 32), np.arange(0, 16), np.arange(48, 64), np.arange(32, 48)])
SGN = np.concatenate([-np.ones(16), np.ones(16), -np.ones(16), np.ones(16)]).astype(np.float32)
POOL_W = (2, 4, 8, 16)

VC_G1 = 0
VC_G2 = 16
VC_PSC = 32
VC_QAG = 40
VC_CW = 44
VC_CB = VC_CW + 264
VC_QGN = VC_CB + 88
VC_KGN = VC_QGN + 1
VC_QGP = VC_KGN + 1
VC_QGPP = VC_QGP + 1
VC_ML = VC_QGPP + 1
VC_MR = VC_ML + 1
VC_EPS = VC_MR + 1
VC_ONE = VC_EPS + 1
NV = VC_ONE + 1
RW_KVG = 0
RW_KGP = 256
RW_RCP = 320
RW_RCS = RW_RCP + 4 * 256
NR = RW_RCS + 4 * NS


class V:
    __slots__ = ('buf', 'key', 'ap')

    def __init__(self, buf, key, ap):
        self.buf = buf
        self.key = key
        self.ap = ap

    def __getitem__(self, idx):
        return V(self.buf, self.key, self.ap[idx])

    def bc(self, shape):
        return V(self.buf, self.key, self.ap.to_broadcast(list(shape)))

    def re(self, s, **kw):
        return V(self.buf, self.key, self.ap.rearrange(s, **kw))


class Buf:
    def __init__(self, ap, name):
        self.ap = ap
        self.name = name
        self.wr = {}
        self.rd = {}

    def k(self, key=None):
        return V(self, key, self.ap)

    def __getitem__(self, idx):
        return V(self, None, self.ap[idx])


class Op:
    __slots__ = ('stream', 'fn', 'deps', 'dma', 'sem', 'val', 'needed', 'idx', 'nm')


class Prog:
    def __init__(self, nc, n_dma_sems=(40, 40)):
        self.nc = nc
        self.ops = []
        self.es = ExitStack()
        self.csem = {}
        for s in ['pe', 'act', 'dve', 'pool']:
            self.csem[s] = self.es.enter_context(nc.semaphore('c_' + s))
        self.dsem = {'sp': [], 'pool': []}
        for i in range(n_dma_sems[0]):
            self.dsem['sp'].append(self.es.enter_context(nc.semaphore('d_sp%d' % i)))
        for i in range(n_dma_sems[1]):
            self.dsem['pool'].append(self.es.enter_context(nc.semaphore('d_pl%d' % i)))
        self.dcnt = {'sp': 0, 'pool': 0}
        self.dlast = {}
        self.duse = {}
        self.last = {}
        self.dma_since = []

    def sbuf(self, name, shape, dtype):
        t = self.es.enter_context(self.nc.sbuf_tensor(name, list(shape), dtype))
        return Buf(t[:], name)

    def psum(self, name, shape, dtype):
        t = self.es.enter_context(self.nc.psum_tensor(name, list(shape), dtype))
        b = Buf(t[:], name)
        b.excl = True
        return b

    def _deps_read(self, v, deps):
        b = v.buf
        if v.key is None:
            for op in b.wr.values():
                deps.add(op)
        else:
            for k in (None, v.key):
                op = b.wr.get(k)
                if op is not None:
                    deps.add(op)

    def _deps_write(self, v, deps):
        b = v.buf
        if v.key is None:
            for op in b.wr.values():
                deps.add(op)
            for d in b.rd.values():
                for op in d.values():
                    deps.add(op)
        else:
            for k in (None, v.key):
                op = b.wr.get(k)
                if op is not None:
                    deps.add(op)
                d = b.rd.get(k)
                if d:
                    for op in d.values():
                        deps.add(op)

    def add(self, stream, fn, reads=(), writes=(), dma=False, nm='', extra_deps=()):
        op = Op()
        op.stream = stream
        op.fn = fn
        op.dma = dma
        op.needed = False
        op.idx = len(self.ops)
        op.sem = None
        op.val = None
        op.nm = nm
        deps = set(extra_deps)
        for v in reads:
            self._deps_read(v, deps)
            if getattr(v.buf, 'excl', False):
                for d in v.buf.rd.values():
                    for st, o in d.items():
                        if st != stream:
                            deps.add(o)
        for v in writes:
            self._deps_write(v, deps)
        if dma:
            q = stream
            i = self.dcnt[q] % len(self.dsem[q])
            self.dcnt[q] += 1
            prev = self.dlast.get((q, i))
            if prev is not None:
                deps.add(prev)
            self.dlast[(q, i)] = op
            n = self.duse.get((q, i), 0) + 1
            self.duse[(q, i)] = n
            op.sem = self.dsem[q][i]
            op.val = 16 * n
            op.needed = True
            self.dma_since.append(op)
        if stream == 'pe' and not dma:
            deps = {d for d in deps if d.dma or d.stream != 'pe'}
        for d in deps:
            d.needed = True
        op.deps = deps
        for v in reads:
            v.buf.rd.setdefault(v.key, {})[stream if not dma else ('dma', op.idx)] = op
        for v in writes:
            b = v.buf
            if v.key is None:
                b.wr = {None: op}
                b.rd = {}
            else:
                b.wr[v.key] = op
                b.rd[v.key] = {}
        self.ops.append(op)
        if not dma:
            self.last[stream] = op
        return op

    def barrier(self):
        lasts = [o for o in self.last.values()]
        dm = list(self.dma_since)
        self.dma_since = []
        for s in STREAMS:
            deps = list(lasts) + dm
            self.add(s, None, extra_deps=deps, nm='barrier')

    def emit(self):
        nc = self.nc
        cnt = {s: 0 for s in self.csem}
        for op in self.ops:
            if not op.dma and op.needed and op.fn is not None:
                cnt[op.stream] += 1
                op.sem = self.csem[op.stream]
                op.val = cnt[op.stream]
        per = {s: [o for o in self.ops if o.stream == s] for s in STREAMS}
        final_dma = []
        for (q, i), n in self.duse.items():
            final_dma.append((self.dsem[q][i], 16 * n))

        def run(stream, eng):
            seen = {}
            for op in per[stream]:
                waits = {}
                for d in op.deps:
                    if d.sem is None:
                        continue
                    key = id(d.sem)
                    if seen.get(key, 0) >= d.val:
                        continue
                    if key not in waits or waits[key][1] < d.val:
                        waits[key] = (d.sem, d.val)
                for key, (sem, val) in waits.items():
                    eng.wait_ge(sem, val)
                    seen[key] = val
                if op.fn is None:
                    continue
                ins = op.fn(eng)
                if op.dma:
                    ins.then_inc(op.sem, 16)
                elif op.needed:
                    ins.then_inc(op.sem, 1)
            if stream == 'sp':
                for sem, val in final_dma:
                    if seen.get(id(sem), 0) < val:
                        eng.wait_ge(sem, val)

        with nc.Block() as block:
            @block.sync
            def _(e):
                run('sp', e)

            @block.gpsimd
            def _(e):
                run('pool', e)

            @block.tensor
            def _(e):
                run('pe', e)

            @block.scalar
            def _(e):
                run('act', e)

            @block.vector
            def _(e):
                run('dve', e)
        self.es.close()

    def dma(self, out, in_, q='sp', nm='dma', **kw):
        return self.add(q, lambda e: e.dma_start(out=out.ap, in_=in_.ap, **kw),
                        reads=[in_], writes=[out], dma=True, nm=nm)

    def mm(self, out, lhsT, rhs, start=True, stop=True, nm='mm'):
        return self.add('pe', lambda e: e.matmul(out.ap, lhsT.ap, rhs.ap, start=start, stop=stop),
                        reads=[lhsT, rhs], writes=[out], nm=nm)

    def transpose(self, out, in_, ident, nm='tr'):
        return self.add('pe', lambda e: e.transpose(out.ap, in_.ap, ident.ap),
                        reads=[in_, ident], writes=[out], nm=nm)

    def act(self, out, in_, func, bias=None, scale=None, accum_out=None, nm='act'):
        reads = [in_]
        kw = {}
        if bias is not None:
            if isinstance(bias, V):
                reads.append(bias)
                kw['bias'] = bias.ap
            else:
                kw['bias'] = bias
        if scale is not None:
            if isinstance(scale, V):
                reads.append(scale)
                kw['scale'] = scale.ap
            else:
                kw['scale'] = scale
        writes = [out]
        if accum_out is not None:
            writes.append(accum_out)
            kw['accum_out'] = accum_out.ap
        return self.add('act', lambda e: e.activation(out.ap, in_.ap, func, **kw),
                        reads=reads, writes=writes, nm=nm)

    def tt(self, out, in0, in1, op, eng='dve', nm='tt'):
        return self.add(eng, lambda e: e.tensor_tensor(out.ap, in0.ap, in1.ap, op),
                        reads=[in0, in1], writes=[out], nm=nm)

    def ts(self, out, in0, s1, op0, s2=None, op1=None, eng='dve', nm='ts'):
        reads = [in0]
        a1 = s1
        if isinstance(s1, V):
            reads.append(s1)
            a1 = s1.ap
        a2 = s2
        if isinstance(s2, V):
            reads.append(s2)
            a2 = s2.ap
        if op1 is None:
            return self.add(eng, lambda e: e.tensor_scalar(out.ap, in0.ap, a1, None, op0),
                            reads=reads, writes=[out], nm=nm)
        return self.add(eng, lambda e: e.tensor_scalar(out.ap, in0.ap, a1, a2, op0, op1),
                        reads=reads, writes=[out], nm=nm)

    def stt(self, out, in0, scalar, in1, op0, op1, nm='stt'):
        reads = [in0, in1]
        sc = scalar
        if isinstance(scalar, V):
            reads.append(scalar)
            sc = scalar.ap
        return self.add('dve', lambda e: e.scalar_tensor_tensor(out.ap, in0.ap, sc, in1.ap, op0, op1),
                        reads=reads, writes=[out], nm=nm)

    def copy(self, out, in_, eng='dve', nm='cp'):
        if eng == 'act':
            return self.add('act', lambda e: e.copy(out.ap, in_.ap), reads=[in_], writes=[out], nm=nm)
        return self.add(eng, lambda e: e.tensor_copy(out.ap, in_.ap), reads=[in_], writes=[out], nm=nm)

    def memset(self, out, val, eng='dve', nm='ms'):
        return self.add(eng, lambda e: e.memset(out.ap, val), reads=[], writes=[out], nm=nm)


class Arena:
    def __init__(self, P, name, nbytes):
        self.P = P
        self.nb = nbytes
        self.t = P.es.enter_context(P.nc.sbuf_tensor(name, [128, nbytes // 2], BF16))
        self.cur = 0
        self.hi = 0

    def reset(self, to=0):
        self.cur = to

    def alloc(self, name, shape, dtype):
        esz = 4 if dtype == F32 else 2
        n = 1
        for s in shape[1:]:
            n *= s
        nbytes = (n * esz + 31) // 32 * 32
        off = self.cur
        assert off + nbytes <= self.nb, "arena overflow %s: %d + %d > %d" % (name, off, nbytes, self.nb)
        self.cur += nbytes
        self.hi = max(self.hi, self.cur)
        ap = self.t[0:shape[0], off // 2:(off + n * esz) // 2]
        if dtype == F32:
            ap = ap.bitcast(F32)
        if len(shape) == 3:
            ap = ap.rearrange("p (a b) -> p a b", b=shape[2])
        elif len(shape) == 4:
            ap = ap.rearrange("p (a b c) -> p a b c", b=shape[2], c=shape[3])
        return Buf(ap, name)


def build(stop_after=None):
    nc = bass.Bass("TRN2", target_bir_lowering=False)

    def din(name, shape, dt=F32):
        return nc.dram_tensor(name, list(shape), dt, kind="ExternalInput")

    def dout(name, shape):
        return nc.dram_tensor(name, list(shape), F32, kind="ExternalOutput")

    xp_d = din("xp", [512, D])
    xo_d = din("xo", [512, D])
    xs_d = din("xs", [2048, D])
    xh_d = din("xh", [NH, D])
    cckv_d = din("cckv", [512, 256])
    ckpe_d = din("ckpe", [512, 64])
    cT_d = din("cT", [128, 32])
    wmod_d = din("w_mod", [D, 6 * D])
    bmod_d = din("b_mod2", [2, 6 * D])
    winuq_d = din("w_in_uq", [D, 1536])
    winkv_d = din("w_in_kv", [D, 320])
    wpool_d = din("w_pool", [1024, 256])
    wuq_d = din("w_uq_x", [512, 2048])
    wukv_d = din("w_ukv_x", [256, 2048])
    wout_d = din("w_out", [D, D])
    wup_d = din("w_up_x", [D, 2 * DFF])
    wdn_d = din("w_down", [DFF, D])
    vecs_d = din("vecs", [128, NV])
    rows_d = din("rows", [1, NR])
    ropek_d = din("ropek", [2048, 128])
    ropeq_d = din("ropeq", [64, 2 * NS])
    ident_d = din("ident", [128, 128], BF16)
    identf_d = din("identf", [128, 128])
    yp_d = dout("yp", [512, D])
    ys_d = dout("ys", [512, D])
    nckv_d = dout("nckv", [512, 256])
    nkpe_d = dout("nkpe", [512, 64])

    P = Prog(nc, n_dma_sems=(44, 44))

    def DB(t, name):
        return Buf(t[:], name)

    XP, XO, XS, XH = DB(xp_d, 'xp'), DB(xo_d, 'xo'), DB(xs_d, 'xs'), DB(xh_d, 'xh')
    CCKV, CKPE = DB(cckv_d, 'cckv'), DB(ckpe_d, 'ckpe')
    WMOD, BMOD = DB(wmod_d, 'wmod'), DB(bmod_d, 'bmod')
    WINUQ, WINKV, WPOOL = DB(winuq_d, 'winuq'), DB(winkv_d, 'winkv'), DB(wpool_d, 'wpool')
    WUQ, WUKV, WOUT, WUP, WDN = DB(wuq_d, 'wuq'), DB(wukv_d, 'wukv'), DB(wout_d, 'wout'), DB(wup_d, 'wup'), DB(wdn_d, 'wdn')
    ROPEK, ROPEQ = DB(ropek_d, 'ropek'), DB(ropeq_d, 'ropeq')
    YP, YS, NCKV, NKPE = DB(yp_d, 'yp'), DB(ys_d, 'ys'), DB(nckv_d, 'nckv'), DB(nkpe_d, 'nkpe')
    ROWS = DB(rows_d, 'rows')

    AR = Arena(P, "arena", 206 * 1024)
    PSB = [P.psum("psb%d" % i, [128, 512], F32) for i in range(8)]

    def psv(i, key=None):
        return V(PSB[i], key, PSB[i].ap)

    def psv16(i, key=None):
        return V(PSB[i], key, PSB[i].ap.bitcast(BF16))

    vecs = AR.alloc("vecs", [128, NV], F32)
    ident = AR.alloc("ident", [128, 128], BF16)
    identf = AR.alloc("identf", [128, 128], F32)
    onesb = AR.alloc("onesb", [128, 128], BF16)
    onesf = AR.alloc("onesf", [128, 128], F32)
    cTf = AR.alloc("cTf", [128, 32], F32)
    cTb = AR.alloc("cTb", [128, 16, 2], BF16)
    MODF = AR.alloc("MODF", [128, 96, 2], F32)
    SS = AR.alloc("SS", [128, 2, 16, 2], F32)
    kvg_b = AR.alloc("kvg_b", [128, 256], F32)
    kgp_b = AR.alloc("kgp_b", [128, 64], F32)
    stat = [AR.alloc("stat%d" % i, [128, 8], F32) for i in range(4)]
    mark_mix = AR.cur
    mixT = AR.alloc("mixT", [128, 16, NALL], BF16)
    mark_cqn = AR.cur
    cqn = AR.alloc("cqn", [128, 4, NALL], BF16)
    mark_a2 = AR.cur
    CKVT = AR.alloc("CKVT", [128, 2, NK_TOT], BF16)
    KPT = AR.alloc("KPT", [64, NK_TOT], F32)
    KPSQ = AR.alloc("KPSQ", [128, NK_TOT], BF16)
    mark_arena = AR.cur

    def vcol(c, n=1, parts=128):
        return vecs.k()[0:parts, c:c + n]

    eps_v = vcol(VC_EPS)

    P.dma(vecs.k(), V(DB(vecs_d, 'vecs_d'), None, vecs_d[:]))
    P.dma(ident.k(), V(DB(ident_d, 'ident_d'), None, ident_d[:]))
    P.dma(identf.k(), V(DB(identf_d, 'identf_d'), None, identf_d[:]))
    P.dma(cTf.k(), V(DB(cT_d, 'cT_d'), None, cT_d[:]))
    P.dma(kvg_b.k(), V(ROWS, None, rows_d[:, RW_KVG:RW_KVG + 256].partition_broadcast(128)))
    P.dma(kgp_b.k(), V(ROWS, None, rows_d[:, RW_KGP:RW_KGP + 64].partition_broadcast(128)))
    P.memset(onesb.k(), 1.0)
    P.memset(onesf.k(), 1.0, eng='pool')
    P.act(cTb.k().re("p a b -> p (a b)"), cTf.k(), AF.Silu)

    def mod_group(col0, width, wblk, b2, mrow, psA, psB_, skip_wdma=False):
        nblk = width // 128
        if not skip_wdma:
            P.dma(wblk.k()[:, :, 0:width], V(WMOD, None, wmod_d.rearrange("(kc p) n -> p kc n", p=128)[:, :, col0:col0 + width]),
                  q='pool', nm='wmod')
        P.dma(b2.k()[:, 0:width], V(BMOD, None, bmod_d[:, col0:col0 + width]))
        for kc in range(NKC):
            P.mm(psv(psA)[0:2, 0:width], cTb.k()[:, kc, :], wblk.k()[:, kc, 0:width], start=(kc == 0), stop=(kc == NKC - 1))
        P.tt(mrow.k()[:, 0:width], psv(psA)[0:2, 0:width], b2.k()[:, 0:width], ALU.add)
        for j in range(nblk):
            P.mm(psv(psB_)[:, 2 * j:2 * j + 2], mrow.k()[0:2, j * 128:(j + 1) * 128], identf.k()[0:2, 0:2])
        b0 = col0 // 128
        P.copy(MODF.k()[:, b0:b0 + nblk, :].re("p a b -> p (a b)"), psv(psB_)[:, 0:2 * nblk], eng='dve')

    def mod_finish(which):
        sc0 = 16 if which == 0 else 64
        gcol = VC_G1 if which == 0 else VC_G2
        for c in range(2):
            P.stt(SS.k()[:, which, :, c], MODF.k()[:, sc0:sc0 + 16, c], 1.0, vcol(gcol, 16), ALU.add, ALU.mult)

    AR.reset(mark_arena)
    m_wblk = [AR.alloc("m_wblk%d" % i, [128, 16, 512], BF16) for i in range(3)]
    m_b2 = [AR.alloc("m_b2_%d" % i, [2, 512], F32) for i in range(2)]
    m_row = [AR.alloc("m_row%d" % i, [2, 512], F32) for i in range(2)]
    mark_a1 = AR.cur
    for cg in range(8):
        mod_group(cg * 512, 512, m_wblk[cg % 3], m_b2[cg % 2], m_row[cg % 2], 6, 7)
    mod_finish(0)

    ctr = {'n': 0}

    def norm_stats(x_v, ntok, xsb_list):
        i = ctr['n']
        ctr['n'] += 1
        xsb = xsb_list[i % len(xsb_list)]
        st = stat[i % 4]
        P.act(xsb.k()[0:ntok, :], x_v, AF.Square, accum_out=st.k()[0:ntok, 0:1])
        P.act(st.k()[0:ntok, 1:2], st.k()[0:ntok, 0:1], AF.Sqrt, bias=eps_v[0:ntok], scale=1.0 / D)
        P.add('dve', lambda e: e.reciprocal(st.ap[0:ntok, 2:3], st.ap[0:ntok, 1:2]),
              reads=[st.k()], writes=[st.k()], nm='rc')
        P.ts(xsb.k()[0:ntok, :], x_v, st.k()[0:ntok, 2:3], ALU.mult)
        return i

    def norm_T(x_v, ntok, which, c, dst_fn, xsb_list, psT, plain_dst=None):
        i = norm_stats(x_v, ntok, xsb_list)
        norm_trans(i, ntok, which, c, dst_fn, xsb_list, psT, plain_dst)

    def norm_trans(i, ntok, which, c, dst_fn, xsb_list, psT, plain_dst=None):
        xsb = xsb_list[i % len(xsb_list)]
        boff = 0 if which == 0 else 48
        for half in range(2):
            bank = psT[(2 * i + half) % len(psT)]
            for cc in range(8):
                kc = half * 8 + cc
                P.transpose(psv16(bank)[:, cc * 128:cc * 128 + ntok], xsb.k()[0:ntok, kc * 128:(kc + 1) * 128],
                            ident.k()[0:ntok, 0:ntok])
            if plain_dst is not None:
                src = psv16(bank)[:, 0:1024].re("p (a b) -> p a b", b=128)
                P.copy(plain_dst(half), src, eng=('act' if half == 0 else 'dve'))
                continue
            for cc in range(8):
                kc = half * 8 + cc
                if half == 0:
                    P.act(dst_fn(kc), psv16(bank)[:, cc * 128:cc * 128 + ntok], AF.Identity,
                          scale=SS.k()[:, which, kc, c:c + 1], bias=MODF.k()[:, boff + kc, c:c + 1])
                else:
                    P.ts(dst_fn(kc), psv16(bank)[:, cc * 128:cc * 128 + ntok], SS.k()[:, which, kc, c:c + 1], ALU.mult,
                         MODF.k()[:, boff + kc, c:c + 1], ALU.add)

    def phase_A1():
        AR.reset(mark_arena)
        w_kv = AR.alloc("w_kv", [128, 16, 320], BF16)
        P.memset(KPSQ.k()[64:128, :], 0.0)
        a1_xt = [AR.alloc("a1_xt%d" % i, [128, D], F32) for i in range(4)]
        a1_xs = [AR.alloc("a1_xs%d" % i, [128, D], BF16) for i in range(3)]
        a1_hT = [AR.alloc("a1_hT%d" % i, [128, 16, 128], BF16) for i in range(3)]
        a1_kv = [AR.alloc("a1_kv%d" % i, [128, 320], F32) for i in range(2)]
        a1_ck = [AR.alloc("a1_ck%d" % i, [128, 256], F32) for i in range(4)]
        a1_ckb = [AR.alloc("a1_ckb%d" % i, [128, 256], BF16) for i in range(2)]
        a1_kp = [AR.alloc("a1_kp%d" % i, [128, 128], F32) for i in range(4)]
        a1_st = [AR.alloc("a1_st%d" % i, [128, 8], F32) for i in range(2)]
        a1_kg = [AR.alloc("a1_kg%d" % i, [128, 128], F32) for i in range(2)]
        a1_kr = [AR.alloc("a1_kr%d" % i, [128, 128], F32) for i in range(2)]
        for i in range(4):
            P.memset(a1_kp[i].k(), 0.0)
        for i in range(2):
            P.memset(a1_kg[i].k(), 0.0)
            P.memset(a1_kr[i].k(), 0.0)
        a1_tm = [AR.alloc("a1_tm%d" % i, [128, 64], F32) for i in range(2)]
        a1_rp = [AR.alloc("a1_rp%d" % i, [128, 128], F32) for i in range(2)]
        a1_jk = AR.alloc("a1_jk", [128, 256], BF16)
        P.dma(w_kv.k(), V(WINKV, None, winkv_d.rearrange("(kc p) n -> p kc n", p=128)), q='pool', nm='wkv')
        w_kvs = [AR.alloc("w_kvs%d" % i, [128, 16, 320], BF16) for i in range(2)]
        B1b = AR.alloc("B1b", [128, 16, 2], BF16)
        brow = AR.alloc("brow", [128, 2, 320], BF16)
        onerow = AR.alloc("onerow", [128, 128], BF16)
        P.memset(brow.k(), 0.0)
        P.memset(onerow.k(), 0.0)
        P.memset(onerow.k()[0:1, :], 1.0)
        P.copy(B1b.k(), MODF.k()[:, 0:16, :], eng='dve')
        for c_ in range(2):
            for kc in range(NKC):
                P.ts(w_kvs[c_].k()[:, kc, :], w_kv.k()[:, kc, :], SS.k()[:, 0, kc, c_:c_ + 1], ALU.mult)
            for kc in range(NKC):
                P.mm(psv(5)[0:1, 0:320], B1b.k()[:, kc, c_:c_ + 1], w_kv.k()[:, kc, :], start=(kc == 0), stop=(kc == NKC - 1))
            P.copy(brow.k()[0:1, c_, :], psv(5)[0:1, 0:320], eng='act')

        class _V64:
            def __init__(self, b):
                self.b = b

            def k(self):
                return self.b.k()[:, 0:64]

        a1_idx = {}

        def kv_stats(it, kind, src_v):
            if kind == 'x':
                xt = a1_xt[it % len(a1_xt)]
                P.dma(xt.k(), src_v, nm='ldx')
                a1_idx[it] = norm_stats(xt.k(), 128, a1_xs)
            else:
                P.dma(a1_ck[it % 4].k(), src_v[0], nm='ldc')
                P.dma(_V64(a1_kp[it % 4]).k(), src_v[1], nm='ldc')

        def kv_trans(it, kind, c):
            if kind == 'x':
                hT = a1_hT[it % 3]
                norm_trans(a1_idx[it], 128, 0, c, None, a1_xs, [0, 1, 2, 3],
                           plain_dst=lambda half: hT.k(half)[:, half * 8:(half + 1) * 8, :])

        def kv_back(it, kind, col0, rope_row0=None, out_row0=None, c=0):
            ck = a1_ck[it % 4]
            ckb = a1_ckb[it % 2]
            kp_full = a1_kp[it % 4]
            kg_full = a1_kg[it % 2]
            kp = _V64(kp_full)
            kg = _V64(kg_full)
            st = a1_st[it % 2]
            if kind == 'x':
                hT = a1_hT[it % 3]
                pk = 4 + (it % 2)
                for kc in range(NKC):
                    P.mm(psv(pk)[:, 0:320], hT.k()[:, kc, :], w_kvs[c].k()[:, kc, :], start=(kc == 0), stop=False)
                P.mm(psv(pk)[:, 0:320], onerow.k(), brow.k()[:, c, :], start=False, stop=True)
                kvb = a1_kv[it % 2]
                P.copy(kvb.k(), psv(pk)[:, 0:320], eng='dve')
                P.act(a1_jk.k(), kvb.k()[:, 0:256], AF.Square, accum_out=st.k()[:, 4:5])
                P.act(st.k()[:, 5:6], st.k()[:, 4:5], AF.Sqrt, bias=eps_v, scale=1.0 / 256)
                P.add('dve', lambda e: e.reciprocal(st.ap[:, 6:7], st.ap[:, 5:6]), reads=[st.k()], writes=[st.k()], nm='rc')
                P.stt(ck.k(), kvb.k()[:, 0:256], st.k()[:, 6:7], kvg_b.k(), ALU.mult, ALU.mult)
                P.copy(kp.k(), kvb.k()[:, 256:320], eng='pool')
                if out_row0 is not None:
                    P.dma(V(NCKV, out_row0, nckv_d[out_row0:out_row0 + 128, :]), ck.k(), nm='st_ckv')
                    P.dma(V(NKPE, out_row0, nkpe_d[out_row0:out_row0 + 128, :]), kp.k(), nm='st_kpe')
            P.copy(ckb.k(), ck.k(), eng='pool')
            P.tt(kg.k(), kp.k(), kgp_b.k(), ALU.mult, eng='pool')
            if rope_row0 is not None:
                rp = a1_rp[it % 2]
                P.dma(rp.k(), V(ROPEK, None, ropek_d[rope_row0:rope_row0 + 128, :]), nm='ldrope')
                kr_full = a1_kr[it % 2]
                kr = _V64(kr_full)
                tm = a1_tm[it % 2]
                kg4 = kg.k().re("p (a b c) -> p a b c", a=2, b=2, c=16)
                tm4 = tm.k().re("p (a b c) -> p a b c", a=2, b=2, c=16)
                sn4 = rp.k()[:, 64:128].re("p (a b c) -> p a b c", a=2, b=2, c=16)
                for b_ in range(2):
                    P.tt(tm4[:, :, b_, :], kg4[:, :, 1 - b_, :], sn4[:, :, b_, :], ALU.mult, eng='pool')
                P.tt(kr.k(), kg.k(), rp.k()[:, 0:64], ALU.mult, eng='pool')
                P.tt(kr.k(), kr.k(), tm.k(), ALU.add, eng='pool')

        def kv_back2(it, col0, rope_row0):
            ckb = a1_ckb[it % 2]
            kp_full = a1_kp[it % 4]
            kfin = a1_kr[it % 2] if rope_row0 is not None else a1_kg[it % 2]
            pt = 6
            for kc in range(2):
                P.transpose(psv16(pt)[:, kc * 128:(kc + 1) * 128], ckb.k()[:, kc * 128:(kc + 1) * 128], ident.k())
            P.copy(V(CKVT, col0, CKVT.ap[:, :, col0:col0 + 128]), psv16(pt)[:, 0:256].re("p (a b) -> p a b", b=128), eng='dve')
            pt2 = 7
            P.transpose(psv(pt2)[:, 0:128], kp_full.k(), identf.k())
            P.transpose(psv(pt2)[:, 128:256], kfin.k(), identf.k())
            P.act(V(KPSQ, col0, KPSQ.ap[0:64, col0:col0 + 128]), psv(pt2)[0:64, 0:128], AF.Square)
            P.copy(V(KPT, col0, KPT.ap[:, col0:col0 + 128]), psv(pt2)[0:64, 128:256], eng='dve')

        tiles = []
        for t in range(4):
            tiles.append(('x', V(XP, None, xp_d[t * 128:(t + 1) * 128, :]), 0, t * 128, None, t * 128))
        for t in range(4):
            tiles.append(('ctx', (V(CCKV, None, cckv_d[t * 128:(t + 1) * 128, :]), V(CKPE, None, ckpe_d[t * 128:(t + 1) * 128, :])),
                          1, 512 + t * 128, None, None))
        for t in range(16):
            tiles.append(('x', V(XS, None, xs_d[t * 128:(t + 1) * 128, :]), 1, 1024 + t * 128, t * 128, None))
        m1_wblk = [AR.alloc("m1_wblk%d" % i, [128, 16, 256], BF16) for i in range(2)]
        m1_b2 = [AR.alloc("m1_b2", [2, 256], F32)] * 2
        m1_row = [AR.alloc("m1_row", [2, 256], F32)] * 2

        def mk_mod1(gi):
            col0 = 4096 + gi * 256

            def fd():
                P.dma(m1_wblk[gi % 2].k(), V(WMOD, None, wmod_d.rearrange("(kc p) n -> p kc n", p=128)[:, :, col0:col0 + 256]),
                      q='pool', nm='wmod')

            def fc():
                mod_group(col0, 256, m1_wblk[gi % 2], m1_b2[gi % 2], m1_row[gi % 2], 6, 7, skip_wdma=True)
            return fd, fc

        mods1 = [mk_mod1(gi) for gi in range(16)]
        mq = [mods1[0][0], mods1[1][0]]
        for gi in range(16):
            mq.append(mods1[gi][1])
            if gi + 2 < 16:
                mq.append(mods1[gi + 2][0])

        nt = len(tiles)

        def stage(i):
            if 0 <= i + 3 < nt:
                kv_stats(i + 3, tiles[i + 3][0], tiles[i + 3][1])
            if 0 <= i + 2 < nt:
                kv_trans(i + 2, tiles[i + 2][0], tiles[i + 2][2])
            if 0 <= i + 1 < nt:
                kind, src, c, col0, rrow, orow = tiles[i + 1]
                kv_back(i + 1, kind, col0, rrow, orow, c)
            if 0 <= i < nt:
                kind, src, c, col0, rrow, orow = tiles[i]
                kv_back2(i, col0, rrow)

        for i in range(-3, nt):
            stage(i)
            for _ in range(2):
                if mq:
                    mq.pop(0)()
        while mq:
            mq.pop(0)()

    P.barrier()
    if stop_after == 'P0':
        P.emit()
        return nc

    AR.reset(mark_a2)
    w_in = AR.alloc("w_in", [128, 16, 1536], BF16)
    w_pool = AR.alloc("w_pool", [128, 8, 256], BF16)
    a2_xt = [AR.alloc("a2_xt%d" % i, [128, D], F32) for i in range(2)]
    a2_xs = [AR.alloc("a2_xs%d" % i, [128, D], BF16) for i in range(2)]
    hTP = AR.alloc("hTP", [128, 16, 512], BF16)
    hTS = AR.alloc("hTS", [128, 16, NS], BF16)
    hTH = AR.alloc("hTH", [128, 16, NH], BF16)
    Ub = [AR.alloc("Ub%d" % i, [128, 2 * 544], F32) for i in range(2)]
    cq = AR.alloc("cq", [128, 4, NS], F32)
    sqb = [AR.alloc("sqb%d" % i, [128, NS], BF16) for i in range(2)]
    rq = AR.alloc("rq", [128, NS], F32)
    Ta = AR.alloc("Ta", [128, 2 * 544], F32)
    Tb = AR.alloc("Tb", [128, 2 * 544], F32)
    rcb = [AR.alloc("rcb%d" % i, [128, NS], F32) for i in range(2)]
    dT = [AR.alloc("dT%d" % i, [128, 2 * NS], BF16) for i in range(2)]
    for blk in range(3):
        P.dma(w_in.k(blk)[:, :, blk * 512:(blk + 1) * 512],
              V(WINUQ, None, winuq_d.rearrange("(kc p) n -> p kc n", p=128)[:, :, blk * 512:(blk + 1) * 512]),
              q='pool', nm='w_in')
    P.dma(w_pool.k(), V(WPOOL, None, wpool_d.rearrange("(a p) n -> p a n", p=128)), q='pool', nm='w_pool')
    psrot = {'i': 0}

    def nextps(banks=(4, 5, 6, 7)):
        b = banks[psrot['i'] % len(banks)]
        psrot['i'] += 1
        return b

    xctr = {'i': 0}

    def a2_group(isP):
        cond = 0 if isP else 1
        n = 512 if isP else NS
        splits = [(0, 512)] if isP else [(0, 257), (257, 257)]
        coff = 0 if isP else NP_
        L = 272 if isP else 530
        R = 4 if isP else 2
        hT = hTP if isP else hTS
        def proj(g):
            ub = Ub[g % 2]
            P.memset(ub.k(), 0.0, eng='pool')
            for ocl in range(2):
                oc = 2 * g + ocl
                for (c0, cn) in splits:
                    pb = nextps()
                    for kc in range(NKC):
                        P.mm(psv(pb)[:, 0:cn], w_in.k(oc // 4)[:, kc, oc * 128:(oc + 1) * 128], hT.k()[:, kc, c0:c0 + cn],
                             start=(kc == 0), stop=(kc == NKC - 1))
                    if isP:
                        dst = ub.k().re("p (a b c) -> p a b c", a=2, b=2, c=272)[:, ocl, :, 8:264]
                        P.copy(dst, psv(pb)[:, 0:512].re("p (b c) -> p b c", c=256), eng='act')
                    else:
                        dst = ub.k().re("p (a c) -> p a c", a=2)[:, ocl, 8 + c0:8 + c0 + cn]
                        P.copy(dst, psv(pb)[:, 0:cn], eng='act')
                if not isP:
                    pb = nextps()
                    for kc in range(NKC):
                        P.mm(psv(pb)[:, 0:NH], w_in.k(oc // 4)[:, kc, oc * 128:(oc + 1) * 128], hTH.k()[:, kc, :],
                             start=(kc == 0), stop=(kc == NKC - 1))
                    u2 = ub.k().re("p (a c) -> p a c", a=2)
                    P.copy(u2[:, ocl, 0:8], psv(pb)[:, 0:8], eng='dve')
                    P.copy(u2[:, ocl, 522:529], psv(pb)[:, 10:17], eng='dve')
        def pool_(g):
            ub = Ub[g % 2]
            rc = rcb[g % 2]
            if isP:
                P.dma(rc.k()[:, 0:256], V(ROWS, None, rows_d[:, RW_RCP + g * 256:RW_RCP + (g + 1) * 256].partition_broadcast(128)), nm='ldrc')
            else:
                P.dma(rc.k(), V(ROWS, None, rows_d[:, RW_RCS + g * NS:RW_RCS + (g + 1) * NS].partition_broadcast(128)), nm='ldrc')
            X = ub.k().re("p (r l) -> p r l", l=272)[:, 0:R, 0:L] if isP else ub.k().re("p (r l) -> p r l", l=544)[:, 0:R, 0:L]
            TA = Ta.k().re("p (r l) -> p r l", l=272)[:, 0:R, 0:L] if isP else Ta.k().re("p (r l) -> p r l", l=544)[:, 0:R, 0:L]
            TB = Tb.k().re("p (r l) -> p r l", l=272)[:, 0:R, 0:L] if isP else Tb.k().re("p (r l) -> p r l", l=544)[:, 0:R, 0:L]
            P.tt(TA[:, :, 1:L], X[:, :, 0:L - 1], X[:, :, 1:L], ALU.add, eng='dve')
            sfin = TA
            if g >= 1:
                P.tt(TB[:, :, 2:L - 1], TA[:, :, 1:L - 2], TA[:, :, 3:L], ALU.add, eng='dve')
                sfin = TB
            if g >= 2:
                P.tt(TA[:, :, 4:L - 3], TB[:, :, 2:L - 5], TB[:, :, 6:L - 1], ALU.add, eng='dve')
                sfin = TA
            if g >= 3:
                P.tt(TB[:, :, 8:L - 7], TA[:, :, 4:L - 11], TA[:, :, 12:L - 3], ALU.add, eng='dve')
                sfin = TB
            nn = 256 if isP else NS
            rcv = V(rc, None, rc.ap[:, 0:nn].unsqueeze(1).to_broadcast([128, R, nn]))
            P.tt(sfin[:, :, 8:8 + nn], sfin[:, :, 8:8 + nn], rcv, ALU.mult)
            dt_ = dT[g % 2]
            dv = dt_.k()[:, 0:1024].re("p (r l) -> p r l", l=256) if isP else dt_.k().re("p (r l) -> p r l", l=NS)
            P.tt(dv, sfin[:, :, 8:8 + nn], X[:, :, 8:8 + nn], ALU.subtract)
        def pmm(g):
            dt_ = dT[g % 2]
            dflat = dt_.k()[:, 0:1024].re("p (a l) -> p a l", a=2) if isP else dt_.k().re("p (a l) -> p a l", a=2)
            for oc2 in range(2):
                for (c0, cn) in splits:
                    pb = nextps()
                    for kc2 in range(2):
                        P.mm(psv(pb)[:, 0:cn], w_pool.k()[:, g * 2 + kc2, oc2 * 128:(oc2 + 1) * 128], dflat[:, kc2, c0:c0 + cn],
                             start=(kc2 == 0), stop=(kc2 == 1))
                    ch = 2 * g + oc2
                    P.act(mixT.k(ch)[:, ch, coff + c0:coff + c0 + cn], psv(pb)[:, 0:cn], AF.Identity,
                          scale=vcol(VC_PSC + ch))

        def cqproj():
            for c4 in range(4):
                oc = 8 + c4
                for (c0, cn) in splits:
                    pb = nextps()
                    for kc in range(NKC):
                        P.mm(psv(pb)[:, 0:cn], w_in.k(oc // 4)[:, kc, oc * 128:(oc + 1) * 128], hT.k()[:, kc, c0:c0 + cn],
                             start=(kc == 0), stop=(kc == NKC - 1))
                    P.copy(cq.k()[:, c4, c0:c0 + cn], psv(pb)[:, 0:cn], eng='act')

        proj(0)
        proj(1)
        cqproj()
        for g in range(4):
            pool_(g)
            pmm(g)
            if g + 2 < 4:
                proj(g + 2)
        for (c0, cn) in splits:
            pb = nextps()
            for c4 in range(4):
                sq = sqb[c4 % 2]
                P.act(sq.k()[:, 0:cn], cq.k()[:, c4, c0:c0 + cn], AF.Square)
                P.mm(psv(pb)[:, 0:cn], onesb.k(), sq.k()[:, 0:cn], start=(c4 == 0), stop=(c4 == 3))
            P.act(rq.k()[:, c0:c0 + cn], psv(pb)[:, 0:cn], AF.Ln, bias=eps_v, scale=1.0 / 512)
            P.act(rq.k()[:, c0:c0 + cn], rq.k()[:, c0:c0 + cn], AF.Exp, scale=-0.5)
            for c4 in range(4):
                P.stt(cqn.k()[:, c4, coff + c0:coff + c0 + cn], cq.k()[:, c4, c0:c0 + cn], vcol(VC_QAG + c4),
                      rq.k()[:, c0:c0 + cn], ALU.mult, ALU.mult)

    ntiles = []
    for t in range(4):
        ntiles.append((V(XP, None, xp_d[t * 128:(t + 1) * 128, :]), 128, 0,
                       (lambda kc, c0=t * 128: hTP.k((c0, kc // 8))[:, kc, c0:c0 + 128]), None))

    def post_H():
        P.ts(hTH.k()[:, :, 0:9], hTH.k()[:, :, 0:9], vcol(VC_ML), ALU.mult)
        P.ts(hTH.k()[:, :, 9:17], hTH.k()[:, :, 9:17], vcol(VC_MR), ALU.mult)
        P.copy(hTS.k('h0')[:, :, 0:1], hTH.k()[:, :, 8:9], eng='dve')
        P.copy(hTS.k('h1')[:, :, 513:514], hTH.k()[:, :, 9:10], eng='dve')

    ntiles.append((V(XH, None, xh_d[:, :]), NH, 1, (lambda kc: hTH.k(kc // 8)[:, kc, :]), post_H))
    for t in range(4):
        ntiles.append((V(XO, None, xo_d[t * 128:(t + 1) * 128, :]), 128, 1,
                       (lambda kc, c0=1 + t * 128: hTS.k((c0, kc // 8))[:, kc, c0:c0 + 128]), None))

    def n_stats(j):
        src, ntok, cnd, dfn, post = ntiles[j]
        xt = a2_xt[j % 2]
        P.dma(xt.k()[0:ntok, :], src, nm='ldx')
        return norm_stats(xt.k()[0:ntok, :], ntok, a2_xs)

    idx = {0: n_stats(0)}
    for j in range(len(ntiles)):
        if j + 1 < len(ntiles):
            idx[j + 1] = n_stats(j + 1)
        src, ntok, cnd, dfn, post = ntiles[j]
        norm_trans(idx[j], ntok, 0, cnd, dfn, a2_xs, [0, 1, 2, 3])
        if post is not None:
            post()
    a2_group(True)
    a2_group(False)
    P.barrier()
    if stop_after == 'A2':
        P.emit()
        return nc
    phase_A1()
    P.barrier()
    if stop_after == 'A':
        P.emit()
        return nc

    AR.reset(mark_arena)
    w_uq = AR.alloc("w_uq", [128, 4, 2048], BF16)
    w_ukv = AR.alloc("w_ukv", [128, 2, 2048], BF16)
    Vb = AR.alloc("Vb", [128, 20, 1024], BF16)
    Kn = [AR.alloc("Kn%d" % i, [128, NKEY_S], BF16) for i in range(2)]
    Kp = [AR.alloc("Kp%d" % i, [128, NKEY_S], BF16) for i in range(2)]
    Qn = [AR.alloc("Qn%d" % i, [128, NS], BF16) for i in range(2)]
    Qp = [AR.alloc("Qp%d" % i, [128, NS], BF16) for i in range(2)]
    PT = [AR.alloc("PT%d" % i, [128, 260], BF16) for i in range(4)]
    pacc = [AR.alloc("pacc%d" % i, [128, 260], F32) for i in range(2)]
    paccB = [AR.alloc("paccB%d" % i, [128, 260], F32) for i in range(2)]
    sqn = [AR.alloc("sqn%d" % i, [128, 512], BF16) for i in range(2)]
    sqp = [AR.alloc("sqp%d" % i, [128, 260], BF16) for i in range(2)]
    Rk = [AR.alloc("Rk%d" % i, [128, 512], F32) for i in range(2)]
    Rqb = [AR.alloc("Rq%d" % i, [128, 260], F32) for i in range(2)]
    t1b = [AR.alloc("t1b%d" % i, [64, 260], F32) for i in range(1)] * 2
    t2b = [AR.alloc("t2b%d" % i, [64, 260], F32) for i in range(1)] * 2
    rden = [AR.alloc("rden%d" % i, [128, 260], F32) for i in range(2)]
    ropq = AR.alloc("ropq", [64, 2 * NS], F32)
    m3_wblk = [AR.alloc("m3_wblk%d" % i, [128, 16, 256], BF16) for i in range(2)]
    m3_b2 = [AR.alloc("m3_b2_%d" % i, [2, 256], F32) for i in range(1)] * 2
    m3_row = [AR.alloc("m3_row%d" % i, [2, 256], F32) for i in range(1)] * 2

    P.dma(w_uq.k(), V(WUQ, None, wuq_d.rearrange("(kc p) n -> p kc n", p=128)), q='pool', nm='w_uq')
    P.dma(w_ukv.k(), V(WUKV, None, wukv_d.rearrange("(kc p) n -> p kc n", p=128)), q='pool', nm='w_ukv')
    P.dma(ropq.k(), V(ROPEQ, None, ropeq_d[:, :]), nm='ropeq')
    for i_ in range(2):
        P.memset(Kp[i_].k()[64:128, :], 0.0)
        P.memset(Qp[i_].k()[64:128, :], 0.0, eng='pool')
        P.memset(sqp[i_].k()[64:128, :], 0.0)
    GENB = (0, 1, 2, 3)
    gctr = {'k': 0, 'q': 0, 'a': 0, 'pt': 0, 'v': 0}

    def build_V(kcol0, nkt):
        for kt in range(nkt):
            for half in range(2):
                pb = nextps(GENB)
                for kc in range(2):
                    P.mm(psv(pb), CKVT.k()[:, kc, kcol0 + kt * 128:kcol0 + (kt + 1) * 128],
                         w_ukv.k()[:, kc, 1024 + half * 512:1024 + (half + 1) * 512], start=(kc == 0), stop=(kc == 1))
                gctr['v'] += 1
                P.copy(Vb.k()[:, kt, half * 512:(half + 1) * 512], psv(pb), eng=('act' if gctr['v'] % 2 else 'dve'))

    def gen_chunks(h, kcol0, nkeys, qoff, splits, rope):
        hb = h % 2
        chunks = []

        def kchunk(k0, kn, j):
            pb = j % 2
            sq = sqn[j % 2]
            R = Rk[j % 2]

            def fa():
                for kc in range(2):
                    P.mm(psv(pb)[:, 0:kn], w_ukv.k()[:, kc, h * 128:(h + 1) * 128], CKVT.k()[:, kc, kcol0 + k0:kcol0 + k0 + kn],
                         start=(kc == 0), stop=(kc == 1))
                P.act(sq.k()[:, 0:kn], psv(pb)[:, 0:kn], AF.Square)

            def fb():
                pb2 = 2
                P.mm(psv(pb2)[:, 0:kn], onesb.k(), sq.k()[:, 0:kn], start=True, stop=False)
                P.mm(psv(pb2)[:, 0:kn], onesb.k(), KPSQ.k()[:, kcol0 + k0:kcol0 + k0 + kn], start=False, stop=True)
                P.act(R.k()[:, 0:kn], psv(pb2)[:, 0:kn], AF.Ln, bias=eps_v, scale=1.0 / 192)
                P.act(R.k()[:, 0:kn], R.k()[:, 0:kn], AF.Exp, scale=-0.5)
                P.stt(Kn[hb].k()[:, k0:k0 + kn], psv(pb)[:, 0:kn], vcol(VC_KGN), R.k()[:, 0:kn], ALU.mult, ALU.mult)
                P.tt(Kp[hb].k()[0:64, k0:k0 + kn], KPT.k()[0:64, kcol0 + k0:kcol0 + k0 + kn], R.k()[0:64, 0:kn], ALU.mult)
            return fa, fb

        def qchunk(c0, cn):
            def f():
                i = gctr['q']
                gctr['q'] += 1
                qa = qoff + c0
                pbn = nextps(GENB)
                for kc in range(4):
                    P.mm(psv(pbn)[:, 0:cn], w_uq.k()[:, kc, h * 256:h * 256 + 128], cqn.k()[:, kc, qa:qa + cn],
                         start=(kc == 0), stop=(kc == 3))
                pbp = nextps(GENB)
                for kc in range(4):
                    P.mm(psv(pbp)[:, 0:cn], w_uq.k()[:, kc, h * 256 + 128:h * 256 + 256], cqn.k()[:, kc, qa:qa + cn],
                         start=(kc == 0), stop=(kc == 3))
                if rope:
                    pbr = nextps(GENB)
                    for kc in range(4):
                        P.mm(psv(pbr)[0:64, 0:cn], w_uq.k()[:, kc, h * 256 + 192:h * 256 + 256], cqn.k()[:, kc, qa:qa + cn],
                             start=(kc == 0), stop=(kc == 3))
                s1 = sqn[i % 2]
                s2 = sqp[i % 2]
                P.act(s1.k()[:, 0:cn], psv(pbn)[:, 0:cn], AF.Square)
                P.act(s2.k()[0:64, 0:cn], psv(pbp)[0:64, 0:cn], AF.Square)
                pss = nextps(GENB)
                P.mm(psv(pss)[:, 0:cn], onesb.k(), s1.k()[:, 0:cn], start=True, stop=False)
                P.mm(psv(pss)[:, 0:cn], onesb.k(), s2.k()[:, 0:cn], start=False, stop=True)
                R = Rqb[i % 2]
                P.act(R.k()[:, 0:cn], psv(pss)[:, 0:cn], AF.Ln, bias=eps_v, scale=1.0 / 192)
                P.act(R.k()[:, 0:cn], R.k()[:, 0:cn], AF.Exp, scale=-0.5)
                P.stt(Qn[hb].k()[:, c0:c0 + cn], psv(pbn)[:, 0:cn], vcol(VC_QGN), R.k()[:, 0:cn], ALU.mult, ALU.mult)
                if not rope:
                    P.stt(Qp[hb].k()[0:64, c0:c0 + cn], psv(pbp)[0:64, 0:cn], vcol(VC_QGP, 1, 64), R.k()[0:64, 0:cn], ALU.mult, ALU.mult)
                else:
                    t1 = t1b[i % 2]
                    t2 = t2b[i % 2]
                    P.stt(t1.k()[:, 0:cn], psv(pbp)[0:64, 0:cn], vcol(VC_QGP, 1, 64), ropq.k()[:, c0:c0 + cn], ALU.mult, ALU.mult)
                    P.stt(t2.k()[:, 0:cn], psv(pbr)[0:64, 0:cn], vcol(VC_QGPP, 1, 64), ropq.k()[:, NS + c0:NS + c0 + cn], ALU.mult, ALU.mult)
                    P.tt(t1.k()[:, 0:cn], t1.k()[:, 0:cn], t2.k()[:, 0:cn], ALU.add, eng='pool')
                    P.tt(Qp[hb].k()[0:64, c0:c0 + cn], t1.k()[:, 0:cn], R.k()[0:64, 0:cn], ALU.mult)
            return f

        for (c0, cn) in splits:
            chunks.append(qchunk(c0, cn))
        k0 = 0
        j = 0
        while k0 < nkeys:
            kn = min(512, nkeys - k0)
            fa, fb = kchunk(k0, kn, j)
            chunks.append(fa)
            chunks.append(fb)
            k0 += kn
            j += 1
        return chunks

    def attend(h, nkt, splits, mcol0, pending):
        hb = h % 2
        it_ = 0
        for (c0, cn) in splits:
            ai = gctr['a']
            gctr['a'] += 1
            pob = 6 + (ai % 2)
            pa = pacc[ai % 2]
            pb_ = paccB[ai % 2]
            def score(kt):
                sb = nextps((4, 5))
                P.mm(psv(sb)[:, 0:cn], Kn[hb].k()[:, kt * 128:(kt + 1) * 128], Qn[hb].k()[:, c0:c0 + cn], start=True, stop=False)
                P.mm(psv(sb)[:, 0:cn], Kp[hb].k()[:, kt * 128:(kt + 1) * 128], Qp[hb].k()[:, c0:c0 + cn], start=False, stop=True)
                pt = PT[gctr['pt'] % 4]
                gctr['pt'] += 1
                P.act(pt.k()[:, 0:cn], psv(sb)[:, 0:cn], AF.Exp, scale=ATTN_SCALE)
                return pt

            pts = {0: score(0)}
            for kt in range(nkt):
                if kt + 1 < nkt:
                    pts[kt + 1] = score(kt + 1)
                pt = pts.pop(kt)
                P.mm(psv(pob)[:, 0:cn], Vb.k()[:, kt, h * 128:(h + 1) * 128], pt.k()[:, 0:cn], start=(kt == 0), stop=(kt == nkt - 1))
                acc, eng_ = (pa, 'pool') if kt % 2 == 0 else (pb_, 'dve')
                if kt < 2:
                    P.copy(acc.k()[:, 0:cn], pt.k()[:, 0:cn], eng=eng_)
                else:
                    P.tt(acc.k()[:, 0:cn], acc.k()[:, 0:cn], pt.k()[:, 0:cn], ALU.add, eng=eng_)
                it_ += 1
                if pending and it_ % 2 == 0:
                    pending.pop(0)()
            pdb = nextps((2, 3))
            P.mm(psv(pdb)[:, 0:cn], onesf.k(), pa.k()[:, 0:cn], start=True, stop=False)
            P.mm(psv(pdb)[:, 0:cn], onesf.k(), pb_.k()[:, 0:cn], start=False, stop=True)
            rd = rden[ai % 2]
            P.act(rd.k()[:, 0:cn], psv(pdb)[:, 0:cn], AF.Ln)
            P.act(rd.k()[:, 0:cn], rd.k()[:, 0:cn], AF.Exp, scale=-1.0)
            P.tt(mixT.k(8 + h)[:, 8 + h, mcol0 + c0:mcol0 + c0 + cn], psv(pob)[:, 0:cn], rd.k()[:, 0:cn], ALU.mult)
        while pending:
            pending.pop(0)()

    build_V(0, 2)

    def build_V_tiles(kcol0, t0):
        for kt in range(2):
            for half in range(2):
                pb = nextps(GENB)
                for kc in range(2):
                    P.mm(psv(pb), CKVT.k()[:, kc, kcol0 + kt * 128:kcol0 + (kt + 1) * 128],
                         w_ukv.k()[:, kc, 1024 + half * 512:1024 + (half + 1) * 512], start=(kc == 0), stop=(kc == 1))
                P.copy(Vb.k()[:, t0 + kt, half * 512:(half + 1) * 512], psv(pb), eng=('act' if half else 'dve'))

    build_V_tiles(256, 2)

    def prompt_stream(s_, kcol0, qoff):
        B = (4 * s_, 4 * s_ + 1, 4 * s_ + 2, 4 * s_ + 3)
        st_ = []
        for h in range(8):
            def Q1(h=h):
                for kc in range(4):
                    P.mm(psv(B[0])[:, 0:256], w_uq.k()[:, kc, h * 256:h * 256 + 128], cqn.k()[:, kc, qoff:qoff + 256],
                         start=(kc == 0), stop=(kc == 3))
                for kc in range(4):
                    P.mm(psv(B[1])[:, 0:256], w_uq.k()[:, kc, h * 256 + 128:h * 256 + 256], cqn.k()[:, kc, qoff:qoff + 256],
                         start=(kc == 0), stop=(kc == 3))
                P.act(sqn[s_].k()[:, 0:256], psv(B[0])[:, 0:256], AF.Square)
                P.act(sqp[s_].k()[0:64, 0:256], psv(B[1])[0:64, 0:256], AF.Square)

            def Q2(h=h):
                P.mm(psv(B[2])[:, 0:256], onesb.k(), sqn[s_].k()[:, 0:256], start=True, stop=False)
                P.mm(psv(B[2])[:, 0:256], onesb.k(), sqp[s_].k()[:, 0:256], start=False, stop=True)
                R = Rqb[s_]
                P.act(R.k()[:, 0:256], psv(B[2])[:, 0:256], AF.Ln, bias=eps_v, scale=1.0 / 192)
                P.act(R.k()[:, 0:256], R.k()[:, 0:256], AF.Exp, scale=-0.5)
                P.stt(Qn[s_].k()[:, 0:256], psv(B[0])[:, 0:256], vcol(VC_QGN), R.k()[:, 0:256], ALU.mult, ALU.mult)
                P.stt(Qp[s_].k()[0:64, 0:256], psv(B[1])[0:64, 0:256], vcol(VC_QGP, 1, 64), R.k()[0:64, 0:256], ALU.mult, ALU.mult)

            def K1(h=h):
                for kc in range(2):
                    P.mm(psv(B[3])[:, 0:256], w_ukv.k()[:, kc, h * 128:(h + 1) * 128], CKVT.k()[:, kc, kcol0:kcol0 + 256],
                         start=(kc == 0), stop=(kc == 1))
                P.act(sqn[s_].k()[:, 256:512], psv(B[3])[:, 0:256], AF.Square)

            def K2(h=h):
                P.mm(psv(B[2])[:, 256:512], onesb.k(), sqn[s_].k()[:, 256:512], start=True, stop=False)
                P.mm(psv(B[2])[:, 256:512], onesb.k(), KPSQ.k()[:, kcol0:kcol0 + 256], start=False, stop=True)
                R = Rk[s_]
                P.act(R.k()[:, 0:256], psv(B[2])[:, 256:512], AF.Ln, bias=eps_v, scale=1.0 / 192)
                P.act(R.k()[:, 0:256], R.k()[:, 0:256], AF.Exp, scale=-0.5)
                P.stt(Kn[s_].k()[:, 0:256], psv(B[3])[:, 0:256], vcol(VC_KGN), R.k()[:, 0:256], ALU.mult, ALU.mult)
                P.tt(Kp[s_].k()[0:64, 0:256], KPT.k()[0:64, kcol0:kcol0 + 256], R.k()[0:64, 0:256], ALU.mult)

            def A1_(h=h):
                for kt in range(2):
                    P.mm(psv(B[0])[:, kt * 256:(kt + 1) * 256], Kn[s_].k()[:, kt * 128:(kt + 1) * 128], Qn[s_].k()[:, 0:256],
                         start=True, stop=False)
                    P.mm(psv(B[0])[:, kt * 256:(kt + 1) * 256], Kp[s_].k()[:, kt * 128:(kt + 1) * 128], Qp[s_].k()[:, 0:256],
                         start=False, stop=True)
                for kt in range(2):
                    P.act(PT[2 * s_ + kt].k()[:, 0:256], psv(B[0])[:, kt * 256:(kt + 1) * 256], AF.Exp, scale=ATTN_SCALE)

            def A2_(h=h):
                for kt in range(2):
                    P.mm(psv(B[1])[:, 0:256], Vb.k()[:, 2 * s_ + kt, h * 128:(h + 1) * 128], PT[2 * s_ + kt].k()[:, 0:256],
                         start=(kt == 0), stop=(kt == 1))
                P.tt(pacc[s_].k()[:, 0:256], PT[2 * s_].k()[:, 0:256], PT[2 * s_ + 1].k()[:, 0:256], ALU.add, eng='pool')
                P.mm(psv(B[1])[:, 256:512], onesf.k(), pacc[s_].k()[:, 0:256])
                rd = rden[s_]
                P.act(rd.k()[:, 0:256], psv(B[1])[:, 256:512], AF.Ln)
                P.act(rd.k()[:, 0:256], rd.k()[:, 0:256], AF.Exp, scale=-1.0)
                P.tt(mixT.k(8 + h)[:, 8 + h, qoff:qoff + 256], psv(B[1])[:, 0:256], rd.k()[:, 0:256], ALU.mult)

            st_ += [Q1, Q2, K1, K2, A1_, A2_]
        return st_

    ps0 = prompt_stream(0, 0, 0)
    ps1 = prompt_stream(1, 256, 256)
    for a_, b_ in zip(ps0, ps1):
        a_()
        b_()

    keysets = [
        (512, NKEY_S, 512, [(0, 257), (257, 257)], True),
    ]
    mod_chunks = []

    def mk_mod(gi):
        col0 = 4096 + gi * 256

        def fd():
            P.dma(m3_wblk[gi % 2].k(), V(WMOD, None, wmod_d.rearrange("(kc p) n -> p kc n", p=128)[:, :, col0:col0 + 256]),
                  q='pool', nm='wmod')

        def fc():
            mod_group(col0, 256, m3_wblk[gi % 2], m3_b2[gi % 2], m3_row[gi % 2], 3, 2, skip_wdma=True)
        return fd, fc

    mods = [mk_mod(gi) for gi in range(16, 32)]
    mod_chunks.append(mods[0][0])
    mod_chunks.append(mods[1][0])
    for gi in range(16):
        mod_chunks.append(mods[gi][1])
        if gi + 2 < 16:
            mod_chunks.append(mods[gi + 2][0])
    for (kcol0, nkeys, qoff, splits, rope) in keysets:
        nkt = nkeys // 128
        build_V(kcol0, nkt)
        for f in gen_chunks(0, kcol0, nkeys, qoff, splits, rope):
            f()
        for h in range(8):
            pending = gen_chunks(h + 1, kcol0, nkeys, qoff, splits, rope) if h < 7 else []
            if rope:
                for _ in range(4):
                    if mod_chunks:
                        pending.append(mod_chunks.pop(0))
            attend(h, nkt, splits, qoff, pending)
    while mod_chunks:
        mod_chunks.pop(0)()
    mod_finish(1)
    P.barrier()
    if stop_after == 'ATT':
        P.emit()
        return nc

    def build_gate(GB, kind, Dt, conds=(0, 1)):
        for ci, c in enumerate(conds):
            for kc in range(NKC):
                d = Dt[kc % 2]
                P.ts(d.k(), identf.k(), MODF.k()[:, kind * 16 + kc, c:c + 1], ALU.mult)
                if kc % 4 == 0:
                    pb = nextps(GENB)
                P.mm(psv(pb)[:, (kc % 4) * 128:(kc % 4 + 1) * 128], onesf.k(), d.k())
                if kc % 4 == 3:
                    P.copy(GB.k()[:, ci, (kc // 4) * 512:(kc // 4 + 1) * 512], psv(pb), eng='act')

    AR.reset(mark_cqn)
    h2T = AR.alloc("h2T", [128, 16, NALL], BF16)
    assert AR.cur <= mark_arena
    AR.reset(mark_arena)
    w_out = AR.alloc("w_out", [128, 16, 2048], BF16)
    GB1 = AR.alloc("GB1", [128, 1, 2048], F32)
    wo_xt = [AR.alloc("wo_xt%d" % i, [128, D], F32) for i in range(2)]
    wo_x1 = [AR.alloc("wo_x1%d" % i, [128, D], F32) for i in range(3)]
    wo_xs = [AR.alloc("wo_xs%d" % i, [128, D], BF16) for i in range(3)]
    Dt = [AR.alloc("Dt%d" % i, [128, 128], F32) for i in range(2)]
    mixH = AR.alloc("mixH", [128, 16, 2], BF16)
    hh = AR.alloc("hh", [128, 16, 2], BF16)
    for blk in range(4):
        P.dma(w_out.k(blk)[:, :, blk * 512:(blk + 1) * 512],
              V(WOUT, None, wout_d.rearrange("(kc p) n -> p kc n", p=128)[:, :, blk * 512:(blk + 1) * 512]),
              q='pool', nm='w_out')
    build_gate(GB1, 2, Dt, conds=(0,))
    P.copy(mixH.k()[:, :, 0:1], mixT.k()[:, :, 512:513], eng='dve')
    P.copy(mixH.k()[:, :, 1:2], mixT.k()[:, :, 1025:1026], eng='dve')
    wtiles = []
    for t in range(4):
        wtiles.append(('P', t))
    wtiles.append(('H', 0))
    for t in range(4):
        wtiles.append(('S', t))
    def wo_front01():
        for wi in range(2):
            kind, t = wtiles[wi]
            P.dma(wo_xt[wi % 2].k(), V(XP, None, xp_d[t * 128:(t + 1) * 128, :]), nm='ldx')
        for cg in range(4):
            for wi in range(2):
                kind, t = wtiles[wi]
                xt = wo_xt[wi % 2]
                x1 = wo_x1[wi % 3]
                mc0 = t * 128
                pb = (4 + cg) if wi == 0 else cg
                for kc in range(NKC):
                    P.mm(psv(pb), mixT.k()[:, kc, mc0:mc0 + 128], w_out.k(cg)[:, kc, cg * 512:(cg + 1) * 512],
                         start=(kc == 0), stop=(kc == NKC - 1))
                P.tt(x1.k()[:, cg * 512:(cg + 1) * 512], psv(pb), GB1.k()[:, 0, cg * 512:(cg + 1) * 512], ALU.mult)
                P.tt(x1.k()[:, cg * 512:(cg + 1) * 512], x1.k()[:, cg * 512:(cg + 1) * 512],
                     xt.k()[:, cg * 512:(cg + 1) * 512], ALU.add)
        for wi in range(2):
            kind, t = wtiles[wi]
            P.dma(V(YP, t, yp_d[t * 128:(t + 1) * 128, :]), wo_x1[wi % 3].k(), nm='st_x1')

    def wo_front(wi):
        kind, t = wtiles[wi]
        cnd = 0 if kind == 'P' else 1
        ntok = 2 if kind == 'H' else 128
        xt = wo_xt[wi % 2]
        x1 = wo_x1[wi % 3]
        mc0 = 0
        if wi == 4:
            build_gate(GB1, 2, Dt, conds=(1,))
        if kind == 'P':
            P.dma(xt.k(), V(XP, None, xp_d[t * 128:(t + 1) * 128, :]), nm='ldx')
            mc0 = t * 128
        elif kind == 'S':
            P.dma(xt.k(), V(XO, None, xo_d[t * 128:(t + 1) * 128, :]), nm='ldx')
            mc0 = 512 + 1 + t * 128
        else:
            P.dma(xt.k()[0:2, :], V(XH, None, xh_d[8:10, :]), nm='ldx')
        for cg in range(4):
            pb = nextps((4, 5, 6, 7))
            for kc in range(NKC):
                lhsT = mixH.k()[:, kc, :] if kind == 'H' else mixT.k()[:, kc, mc0:mc0 + 128]
                P.mm(psv(pb)[0:ntok, :], lhsT, w_out.k(cg)[:, kc, cg * 512:(cg + 1) * 512], start=(kc == 0), stop=(kc == NKC - 1))
            P.tt(x1.k()[0:ntok, cg * 512:(cg + 1) * 512], psv(pb)[0:ntok, :], GB1.k()[0:ntok, 0, cg * 512:(cg + 1) * 512], ALU.mult)
            P.tt(x1.k()[0:ntok, cg * 512:(cg + 1) * 512], x1.k()[0:ntok, cg * 512:(cg + 1) * 512],
                 xt.k()[0:ntok, cg * 512:(cg + 1) * 512], ALU.add)
        if kind == 'P':
            P.dma(V(YP, t, yp_d[t * 128:(t + 1) * 128, :]), x1.k(), nm='st_x1')
        elif kind == 'S':
            P.dma(V(YS, t, ys_d[t * 128:(t + 1) * 128, :]), x1.k(), nm='st_x1')

    wo_idx = {}

    def wo_stats(wi):
        kind, t = wtiles[wi]
        ntok = 2 if kind == 'H' else 128
        x1 = wo_x1[wi % 3]
        wo_idx[wi] = norm_stats(x1.k()[0:ntok, :], ntok, wo_xs)

    def wo_back(wi):
        kind, t = wtiles[wi]
        cnd = 0 if kind == 'P' else 1
        i_ = wo_idx[wi]
        if kind == 'P':
            norm_trans(i_, 128, 1, cnd, lambda kc, c0=t * 128: h2T.k((c0, kc // 8))[:, kc, c0:c0 + 128], wo_xs, [0, 1, 2, 3])
        elif kind == 'S':
            mc0 = 512 + 1 + t * 128
            norm_trans(i_, 128, 1, cnd, lambda kc, c0=mc0: h2T.k((c0, kc // 8))[:, kc, c0:c0 + 128], wo_xs, [0, 1, 2, 3])
        else:
            norm_trans(i_, 2, 1, cnd, lambda kc: hh.k(kc // 8)[:, kc, :], wo_xs, [0, 1, 2, 3])
            P.ts(h2T.k()[:, :, 512:513], hh.k()[:, :, 0:1], vcol(VC_ML), ALU.mult)
            P.ts(h2T.k()[:, :, 1025:1026], hh.k()[:, :, 1:2], vcol(VC_MR), ALU.mult)

    wo_front01()
    for j in range(2):
        wo_stats(j)
    for wi in range(len(wtiles)):
        if wi + 2 < len(wtiles):
            wo_front(wi + 2)
            wo_stats(wi + 2)
        wo_back(wi)
    P.barrier()
    if stop_after == 'WOUT':
        P.emit()
        return nc

    AR.reset(mark_mix)
    wub = [AR.alloc("wub%d" % i, [128, 16, 512], BF16) for i in range(2)]
    assert AR.cur <= mark_cqn
    AR.reset(mark_arena)
    gT = AR.alloc("gT", [128, NJ, NALL], BF16)
    mark_fdn = AR.cur
    upb = [AR.alloc("upb%d" % i, [128, NALL], F32) for i in range(2)]
    zb = [AR.alloc("zb%d" % i, [128, NALL], F32) for i in range(2)]
    sab = [AR.alloc("sab%d" % i, [128, NALL], F32) for i in range(2)]
    usplits = [(0, 512), (512, 257), (769, 257)]
    cix = 0
    def ld_wup(jj):
        P.dma(wub[jj % 2].k(), V(WUP, None, wup_d.rearrange("(kc p) n -> p kc n", p=128)[:, :, jj * 512:(jj + 1) * 512]),
              q='pool', nm='w_up')

    ld_wup(0)
    for jj in range(22):
        wb = wub[jj % 2]
        if jj + 1 < 22:
            ld_wup(jj + 1)
        for q4 in range(4):
            ci = 4 * jj + q4
            up = upb[cix % 2]
            z = zb[cix % 2]
            cix += 1
            for (c0, cn) in usplits:
                pb = nextps((0, 1, 2, 3, 4, 5, 6, 7))
                for kc in range(NKC):
                    P.mm(psv(pb)[:, 0:cn], wb.k()[:, kc, q4 * 128:(q4 + 1) * 128], h2T.k()[:, kc, c0:c0 + cn],
                         start=(kc == 0), stop=(kc == NKC - 1))
                P.copy(up.k()[:, c0:c0 + cn], psv(pb)[:, 0:cn], eng='act')
            w0 = vcol(VC_CW + ci)
            w1 = vcol(VC_CW + 88 + ci)
            w2 = vcol(VC_CW + 176 + ci)
            bb = vcol(VC_CB + ci)
            P.ts(z.k(), up.k(), w1, ALU.mult, bb, ALU.add, eng='pool')
            zP = z.k()[:, 0:512].re("p (s l) -> p s l", l=256)
            uP = up.k()[:, 0:512].re("p (s l) -> p s l", l=256)
            zS = z.k()[:, 512:NALL]
            uS = up.k()[:, 512:NALL]
            P.stt(zP[:, :, 1:256], uP[:, :, 0:255], w0, zP[:, :, 1:256], ALU.mult, ALU.add)
            P.stt(zS[:, 1:NS], uS[:, 0:NS - 1], w0, zS[:, 1:NS], ALU.mult, ALU.add)
            P.stt(zP[:, :, 0:255], uP[:, :, 1:256], w2, zP[:, :, 0:255], ALU.mult, ALU.add)
            P.stt(zS[:, 0:NS - 1], uS[:, 1:NS], w2, zS[:, 0:NS - 1], ALU.mult, ALU.add)
            if q4 % 2 == 0:
                sa = sab[(ci // 2) % 2]
                P.act(sa.k(), z.k(), AF.Silu)
            else:
                j = 2 * jj + q4 // 2
                sa = sab[(ci // 2) % 2]
                P.tt(gT.k(j)[:, j, :], sa.k(), z.k(), ALU.mult)
    P.barrier()
    if stop_after == 'FUP':
        P.emit()
        return nc

    AR.reset(mark_mix)
    wdb0 = AR.alloc("wdb0", [128, NJ, 256], BF16)
    assert AR.cur <= mark_cqn
    AR.reset(mark_cqn)
    GB2 = AR.alloc("GB2", [128, 2, 2048], F32)
    Dt2 = [AR.alloc("Dt2_%d" % i, [128, 128], F32) for i in range(2)]
    x1s = [AR.alloc("x1s%d" % i, [128, 256], F32) for i in range(4)]
    ot = [AR.alloc("ot%d" % i, [128, 256], F32) for i in range(4)]
    assert AR.cur <= mark_arena
    AR.reset(mark_fdn)
    wdb1 = AR.alloc("wdb1", [128, NJ, 256], BF16)
    wdb = [wdb0, wdb1]
    build_gate(GB2, 5, Dt2)
    oi = 0
    def ld_wdn(cg):
        P.dma(wdb[cg % 2].k(), V(WDN, None, wdn_d.rearrange("(kc p) n -> p kc n", p=128)[:, :, cg * 256:(cg + 1) * 256]),
              q='pool', nm='w_dn')

    ld_wdn(0)
    for cg in range(8):
        wd = wdb[cg % 2]
        if cg + 1 < 8:
            ld_wdn(cg + 1)
        for ti in range(8):
            isP = ti < 4
            t = ti % 4
            cnd = 0 if isP else 1
            gc0 = t * 128 if isP else 512 + 1 + t * 128
            YB, y_d = (YP, yp_d) if isP else (YS, ys_d)
            xs_ = x1s[oi % 4]
            o = ot[oi % 4]
            oi += 1
            P.dma(xs_.k(), V(YB, (t, cg), y_d[t * 128:(t + 1) * 128, cg * 256:(cg + 1) * 256]), nm='ld_x1')
            pb = nextps((0, 1, 2, 3, 4, 5, 6, 7))
            for kc in range(NJ):
                P.mm(psv(pb)[:, 0:256], gT.k()[:, kc, gc0:gc0 + 128], wd.k()[:, kc, :], start=(kc == 0), stop=(kc == NJ - 1))
            P.tt(o.k(), psv(pb)[:, 0:256], GB2.k()[:, cnd, cg * 256:(cg + 1) * 256], ALU.mult)
            P.tt(o.k(), o.k(), xs_.k(), ALU.add, eng='pool')
            P.dma(V(YB, (t, cg), y_d[t * 128:(t + 1) * 128, cg * 256:(cg + 1) * 256]), o.k(), nm='st_y')
    P.emit()
    return nc


def _rope_tables(pos):
    pos = np.asarray(pos)
    row = (pos // 64).astype(np.float32)
    col = (pos % 64).astype(np.float32)
    inv = (10000.0 ** (-np.arange(16, dtype=np.float32) / 16)).astype(np.float32)
    ar = row[:, None] * inv
    ac = col[:, None] * inv
    ang = np.concatenate([ar, ar, ac, ac], axis=-1).astype(np.float32)
    return np.cos(ang).astype(np.float32), (np.sin(ang).astype(np.float32) * SGN[None, :]).astype(np.float32)


def _rc_table(pos, T):
    out = np.zeros((4, len(pos)), np.float32)
    for g, w in enumerate(POOL_W):
        lo = np.clip(pos - w // 2, 0, T - 1)
        hi = np.clip(pos - w // 2 + w - 1, 0, T - 1)
        out[g] = 1.0 / np.maximum(hi - lo + 1, 1).astype(np.float32)
    return out


def prep_inputs(inp):
    f = np.float32
    g = lambda k: np.asarray(inp[k], dtype=f)
    x_prompt, x_sample = g('x_prompt'), g('x_sample')
    cache_ckv, cache_kpe, c, c_ctx = g('cache_ckv'), g('cache_kpe'), g('c'), g('c_ctx')
    w_in = g('w_in')[0]
    w_uq = g('w_uq')[0]
    w_ukv = g('w_ukv')[0]
    w_up = g('w_up')[0]
    conv_w, conv_b = g('conv_w')[0], g('conv_b')[0]
    qg, kg = g('q_norm_g')[0], g('k_norm_g')[0]
    shared = {}
    shared['w_mod'] = np.ascontiguousarray(g('w_mod')[0])
    shared['b_mod2'] = np.ascontiguousarray(np.broadcast_to(g('b_mod')[0][None, :], (2, 6 * D)))
    shared['w_in_uq'] = np.ascontiguousarray(w_in[:, :1536])
    shared['w_in_kv'] = np.ascontiguousarray(w_in[:, 1536:1856])
    shared['w_pool'] = np.ascontiguousarray(g('w_pool')[0].reshape(1024, 256))
    cols = []
    for h in range(8):
        b0 = h * 192
        cols += list(range(b0, b0 + 128)) + list(range(b0 + 128, b0 + 192)) + list(b0 + 128 + PERM)
    shared['w_uq_x'] = np.ascontiguousarray(w_uq[:, cols])
    kc_, vc_ = [], []
    for h in range(8):
        kc_ += list(range(h * 256, h * 256 + 128))
        vc_ += list(range(h * 256 + 128, h * 256 + 256))
    shared['w_ukv_x'] = np.ascontiguousarray(w_ukv[:, kc_ + vc_])
    shared['w_out'] = np.ascontiguousarray(g('w_out')[0])
    chunk_col = []
    for jj in range(22):
        for j in (2 * jj, 2 * jj + 1):
            chunk_col += [j * 128, DFF + j * 128]
    upcols = np.concatenate([np.arange(c0, c0 + 128) for c0 in chunk_col])
    shared['w_up_x'] = np.ascontiguousarray(w_up[:, upcols])
    shared['w_down'] = np.ascontiguousarray(g('w_down')[0])
    shared['ident'] = np.eye(128, dtype=f).astype(ml_dtypes.bfloat16)
    shared['identf'] = np.eye(128, dtype=f)
    rck, rsk = _rope_tables(np.arange(2048))
    shared['ropek'] = np.ascontiguousarray(np.concatenate([rck, rsk], axis=1))

    def fm(v, nch):
        return np.asarray(v, f).reshape(nch, 128).T

    vecs = np.zeros((128, NV), f)
    vecs[:, VC_G1:VC_G1 + 16] = fm(g('norm1_g')[0], 16)
    vecs[:, VC_G2:VC_G2 + 16] = fm(g('norm2_g')[0], 16)
    vecs[:, VC_PSC:VC_PSC + 8] = fm(g('pool_scale')[0], 8)
    vecs[:, VC_QAG:VC_QAG + 4] = fm(g('q_a_g')[0], 4)
    cwp = conv_w[:, upcols]
    cbp = conv_b[upcols]
    for tap in range(3):
        vecs[:, VC_CW + tap * 88:VC_CW + (tap + 1) * 88] = fm(cwp[tap], 88)
    vecs[:, VC_CB:VC_CB + 88] = fm(cbp, 88)
    vecs[:, VC_QGN] = qg[:128]
    vecs[:, VC_KGN] = kg[:128]
    vecs[:64, VC_QGP] = qg[128:192]
    vecs[:64, VC_QGPP] = qg[128:192][PERM]
    vecs[:, VC_EPS] = EPS
    vecs[:, VC_ONE] = 1.0
    rcP = _rc_table(np.arange(256), 256)
    in_maps = []
    for core in range(8):
        b = core // 4
        qd = core % 4
        s0 = qd * 512
        m = dict(shared)
        m['xp'] = np.ascontiguousarray(x_prompt[2 * core:2 * core + 2].reshape(512, D))
        m['xo'] = np.ascontiguousarray(x_sample[b, s0:s0 + 512])
        m['xs'] = np.ascontiguousarray(x_sample[b])
        hidx = np.concatenate([np.arange(s0 - 9, s0), np.arange(s0 + 512, s0 + 520)])
        m['xh'] = np.ascontiguousarray(x_sample[b, np.clip(hidx, 0, 2047)])
        m['cckv'] = np.ascontiguousarray(cache_ckv[b, 0])
        m['ckpe'] = np.ascontiguousarray(cache_kpe[b, 0])
        cc = np.stack([c_ctx, c[b]], axis=0)
        m['cT'] = np.ascontiguousarray(cc.reshape(2, 16, 128).transpose(2, 1, 0).reshape(128, 32))
        v = vecs.copy()
        v[:, VC_ML] = 0.0 if qd == 0 else 1.0
        v[:, VC_MR] = 0.0 if qd == 3 else 1.0
        m['vecs'] = v
        sxpos = np.arange(s0 - 1, s0 + 513)
        rows = np.zeros((1, NR), f)
        rows[0, RW_KVG:RW_KVG + 256] = g('kv_a_g')[0]
        rows[0, RW_KGP:RW_KGP + 64] = kg[128:192]
        rows[0, RW_RCP:RW_RCP + 1024] = rcP.reshape(-1)
        rows[0, RW_RCS:RW_RCS + 4 * NS] = _rc_table(sxpos, 2048).reshape(-1)
        m['rows'] = rows
        qc, qs = _rope_tables(np.clip(sxpos, 0, 2047))
        m['ropeq'] = np.ascontiguousarray(np.concatenate([qc.T, qs.T], axis=1))
        in_maps.append(m)
    return in_maps


_NC_CACHE = {}


def kernel(**inputs):
    in_maps = prep_inputs(inputs)
    if 'nc' not in _NC_CACHE:
        _NC_CACHE['nc'] = build()
    nc = _NC_CACHE['nc']
    res = run_bass_kernel_spmd(nc, in_maps, core_ids=list(range(8)))
    r = res.results
    yp = np.stack([r[c]['yp'] for c in range(8)], 0).reshape(16, 256, D).astype(np.float32)
    ys = np.stack([r[c]['ys'] for c in range(8)], 0).reshape(2, 2048, D).astype(np.float32)
    nckv = np.stack([r[c]['nckv'] for c in range(8)], 0).reshape(16, 1, 256, 256).astype(np.float32)
    nkpe = np.stack([r[c]['nkpe'] for c in range(8)], 0).reshape(16, 1, 256, 64).astype(np.float32)
    return (yp, ys, nckv, nkpe)
```
